# Optimizing a Trainium2 kernel written in Bass

```python
import math
import jax, jax.numpy as jnp
from jax import lax
import numpy as np

D_MODEL = 1024
BATCH = 32
SEQ = 2048
DEPTH = 4

GRID_W = 64
N_MIXERS = 4
GROUP_WIDTH = D_MODEL // N_MIXERS
HEAD_DIM = 64
Q_BLOCK = 128
EPS = 1e-6

NA_HEADS = GROUP_WIDTH // HEAD_DIM
NA_WIN_ROWS = 8
NA_WIN_COLS = 16

DIFF_HEADS = GROUP_WIDTH // HEAD_DIM
DIFF_QK_DIM = HEAD_DIM // 2
DIFF_V_DIM = HEAD_DIM

GQA_HEADS = GROUP_WIDTH // HEAD_DIM
GQA_KV_HEADS = GQA_HEADS // 2
ROPE_THETA = 10000.0

MLA_HEADS = GROUP_WIDTH // HEAD_DIM
MLA_Q_LORA = GROUP_WIDTH
MLA_KV_LORA = GROUP_WIDTH // 2
MLA_NOPE_DIM = HEAD_DIM
MLA_ROPE_DIM = HEAD_DIM // 2
MLA_V_DIM = HEAD_DIM

D_FF = -(-8 * D_MODEL // (3 * 256)) * 256

IN_SIZES = (
    NA_HEADS * HEAD_DIM, NA_HEADS * HEAD_DIM, NA_HEADS * HEAD_DIM,
    DIFF_HEADS * 2 * DIFF_QK_DIM, DIFF_HEADS * 2 * DIFF_QK_DIM, DIFF_HEADS * DIFF_V_DIM,
    GQA_HEADS * HEAD_DIM, GQA_KV_HEADS * HEAD_DIM, GQA_KV_HEADS * HEAD_DIM,
    MLA_Q_LORA, MLA_KV_LORA, MLA_ROPE_DIM,
)
IN_WIDTH = sum(IN_SIZES)
MIX_WIDTH = NA_HEADS * HEAD_DIM + DIFF_HEADS * DIFF_V_DIM + GQA_HEADS * HEAD_DIM + MLA_HEADS * MLA_V_DIM

kernel_name = "hybrid_parallel_group_encoder"


def rms_norm(x, w):
    xf = x.astype(jnp.float32)
    y = xf * lax.rsqrt(jnp.mean(xf * xf, axis=-1, keepdims=True) + EPS)
    return (y * w.astype(jnp.float32)).astype(x.dtype)


def rope(x, pos):
    d = x.shape[-1]
    inv = ROPE_THETA ** (-jnp.arange(0, d, 2, dtype=jnp.float32) / d)
    ang = pos.astype(jnp.float32)[:, None] * inv[None, :]
    ang = jnp.concatenate([ang, ang], axis=-1)
    xf = x.astype(jnp.float32)
    x1, x2 = jnp.split(xf, 2, axis=-1)
    rot = jnp.concatenate([-x2, x1], axis=-1)
    return (xf * jnp.cos(ang) + rot * jnp.sin(ang)).astype(x.dtype)


def axial_rope(x, row, col):
    half = x.shape[-1] // 2
    return jnp.concatenate([rope(x[..., :half], row), rope(x[..., half:], col)], axis=-1)


def split_heads(t, n):
    b, s, _ = t.shape
    return t.reshape(b, s, n, -1).transpose(0, 2, 1, 3)


def merge_heads(t):
    b, h, s, d = t.shape
    return t.transpose(0, 2, 1, 3).reshape(b, s, h * d)


def to_blocks(t):
    b, h, s, d = t.shape
    return t.reshape(b, h, s // Q_BLOCK, Q_BLOCK, d).transpose(2, 0, 1, 3, 4)


def from_blocks(t):
    nb, b, h, qb, d = t.shape
    return t.transpose(1, 2, 0, 3, 4).reshape(b, h, nb * qb, d)


def neighbourhood_attention(q, k, v, rel_bias):
    b, h, s, d = q.shape
    rows = s // GRID_W
    wr = min(NA_WIN_ROWS, rows)
    wc = NA_WIN_COLS
    q5 = q.reshape(b, h, rows, GRID_W, d)
    k5 = k.reshape(b, h, rows, GRID_W, d)
    v5 = v.reshape(b, h, rows, GRID_W, d)
    r = jnp.arange(rows)
    row_start = jnp.clip(r - wr // 2, 0, rows - wr)
    c = jnp.arange(GRID_W)
    col_idx = jnp.clip(c - wc // 2, 0, GRID_W - wc)[:, None] + jnp.arange(wc)[None, :]
    dc_idx = col_idx - c[:, None] + (NA_WIN_COLS - 1)
    scale = d ** -0.5

    def row_block(args):
        q_r, rs, r_i = args
        k_band = lax.dynamic_slice_in_dim(k5, rs, wr, axis=2)
        v_band = lax.dynamic_slice_in_dim(v5, rs, wr, axis=2)
        k_win = k_band[:, :, :, col_idx, :]
        v_win = v_band[:, :, :, col_idx, :]
        dr_idx = rs + jnp.arange(wr) - r_i + (NA_WIN_ROWS - 1)
        bias = rel_bias[:, dr_idx[None, :, None], dc_idx[:, None, :]]
        sc = jnp.einsum('bhcd,bhrcwd->bhcrw', q_r, k_win).astype(jnp.float32) * scale + bias.astype(jnp.float32)
        p = jax.nn.softmax(sc.reshape(b, h, GRID_W, wr * wc), axis=-1).reshape(b, h, GRID_W, wr, wc)
        return jnp.einsum('bhcrw,bhrcwd->bhcd', p.astype(v.dtype), v_win)

    out = lax.map(row_block, (q5.transpose(2, 0, 1, 3, 4), row_start, r))
    return out.transpose(1, 2, 0, 3, 4).reshape(b, h, s, d)


def diff_attention(q1, q2, k1, k2, v, lam, slopes):
    s = k1.shape[2]
    pos = jnp.arange(s)
    scale = DIFF_QK_DIM ** -0.5

    def block(args):
        qb1, qb2, qp = args
        dist = jnp.abs(qp[:, None] - pos[None, :]).astype(jnp.float32)
        alibi = -slopes[:, None, None] * dist
        s1 = jnp.einsum('bhqd,bhkd->bhqk', qb1, k1).astype(jnp.float32) * scale + alibi
        s2 = jnp.einsum('bhqd,bhkd->bhqk', qb2, k2).astype(jnp.float32) * scale + alibi
        a = jax.nn.softmax(s1, axis=-1) - lam * jax.nn.softmax(s2, axis=-1)
        return jnp.einsum('bhqk,bhkd->bhqd', a.astype(v.dtype), v)

    out = lax.map(block, (to_blocks(q1), to_blocks(q2), pos.reshape(-1, Q_BLOCK)))
    return from_blocks(out)


def gqa_attention(q, k, v):
    b, h, s, d = q.shape
    kvh = k.shape[1]
    g = h // kvh
    scale = d ** -0.5

    def block(qb):
        qg = qb.reshape(b, kvh, g, Q_BLOCK, d)
        sc = jnp.einsum('bkgqd,bksd->bkgqs', qg, k).astype(jnp.float32) * scale
        o = jnp.einsum('bkgqs,bksd->bkgqd', jax.nn.softmax(sc, axis=-1).astype(v.dtype), v)
        return o.reshape(b, h, Q_BLOCK, d)

    return from_blocks(lax.map(block, to_blocks(q)))


def mla_attention(q_nope, q_rope, k_nope, k_rope, v):
    scale = (MLA_NOPE_DIM + MLA_ROPE_DIM) ** -0.5

    def block(args):
        qn, qr = args
        sc = (jnp.einsum('bhqd,bhkd->bhqk', qn, k_nope)
              + jnp.einsum('bhqd,bkd->bhqk', qr, k_rope)).astype(jnp.float32) * scale
        return jnp.einsum('bhqk,bhkd->bhqd', jax.nn.softmax(sc, axis=-1).astype(v.dtype), v)

    return from_blocks(lax.map(block, (to_blocks(q_nope), to_blocks(q_rope))))


def setup_inputs(seed: int = 0) -> dict:
    key = jax.random.key(seed)
    ks = jax.random.split(key, 21)

    def normal(k, shape, scale):
        return scale * jax.random.normal(k, shape, jnp.float32)

    def gain(k, shape):
        return 1.0 + normal(k, shape, 0.05)

    return {
        "x": normal(ks[0], (BATCH, SEQ, D_MODEL), 1.0),
        "pre_mix_norm": gain(ks[1], (DEPTH, D_MODEL)),
        "w_in": normal(ks[2], (DEPTH, D_MODEL, IN_WIDTH), D_MODEL ** -0.5),
        "na_rel_bias": normal(ks[3], (DEPTH, NA_HEADS, 2 * NA_WIN_ROWS - 1, 2 * NA_WIN_COLS - 1), 0.1),
        "diff_lambda_q1": normal(ks[4], (DEPTH, DIFF_QK_DIM), 0.1),
        "diff_lambda_k1": normal(ks[5], (DEPTH, DIFF_QK_DIM), 0.1),
        "diff_lambda_q2": normal(ks[6], (DEPTH, DIFF_QK_DIM), 0.1),
        "diff_lambda_k2": normal(ks[7], (DEPTH, DIFF_QK_DIM), 0.1),
        "diff_subln": gain(ks[8], (DEPTH, DIFF_V_DIM)),
        "gqa_q_norm": gain(ks[9], (DEPTH, HEAD_DIM)),
        "gqa_k_norm": gain(ks[10], (DEPTH, HEAD_DIM)),
        "mla_q_norm": gain(ks[11], (DEPTH, MLA_Q_LORA)),
        "mla_kv_norm": gain(ks[12], (DEPTH, MLA_KV_LORA)),
        "mla_w_uq": normal(ks[13], (DEPTH, MLA_Q_LORA, MLA_HEADS * (MLA_NOPE_DIM + MLA_ROPE_DIM)), MLA_Q_LORA ** -0.5),
        "mla_w_ukv": normal(ks[14], (DEPTH, MLA_KV_LORA, MLA_HEADS * (MLA_NOPE_DIM + MLA_V_DIM)), MLA_KV_LORA ** -0.5),
        "w_o": normal(ks[15], (DEPTH, MIX_WIDTH, D_MODEL), MIX_WIDTH ** -0.5),
        "post_mix_norm": gain(ks[16], (DEPTH, D_MODEL)),
        "pre_ffn_norm": gain(ks[17], (DEPTH, D_MODEL)),
        "ffn_w_gate_up": normal(ks[18], (DEPTH, D_MODEL, 2 * D_FF), D_MODEL ** -0.5),
        "ffn_w_down": normal(ks[19], (DEPTH, D_FF, D_MODEL), D_FF ** -0.5),
        "post_ffn_norm": gain(ks[20], (DEPTH, D_MODEL)),
    }


def reference(x, pre_mix_norm, w_in, na_rel_bias, diff_lambda_q1, diff_lambda_k1, diff_lambda_q2,
              diff_lambda_k2, diff_subln, gqa_q_norm, gqa_k_norm, mla_q_norm, mla_kv_norm, mla_w_uq,
              mla_w_ukv, w_o, post_mix_norm, pre_ffn_norm, ffn_w_gate_up, ffn_w_down, post_ffn_norm):
    b, s, _ = x.shape
    pos = jnp.arange(s)
    grid_row = pos // GRID_W
    grid_col = pos % GRID_W
    offsets = [int(o) for o in np.cumsum(IN_SIZES)[:-1]]
    alibi_slopes = jnp.asarray([2.0 ** (-8.0 * (i + 1) / DIFF_HEADS) for i in range(DIFF_HEADS)], dtype=jnp.float32)

    for l in range(DEPTH):
        h = rms_norm(x, pre_mix_norm[l])
        proj = jnp.einsum('bsd,de->bse', h, w_in[l])
        (na_q, na_k, na_v, df_q, df_k, df_v, gq_q, gq_k, gq_v,
         ml_cq, ml_ckv, ml_kr) = jnp.split(proj, offsets, axis=-1)

        a_out = neighbourhood_attention(split_heads(na_q, NA_HEADS), split_heads(na_k, NA_HEADS),
                                        split_heads(na_v, NA_HEADS), na_rel_bias[l])

        lambda_init = 0.8 - 0.6 * math.exp(-0.3 * l)
        dq = df_q.reshape(b, s, DIFF_HEADS, 2, DIFF_QK_DIM).transpose(3, 0, 2, 1, 4)
        dk = df_k.reshape(b, s, DIFF_HEADS, 2, DIFF_QK_DIM).transpose(3, 0, 2, 1, 4)
        lam = (jnp.exp(jnp.sum(diff_lambda_q1[l].astype(jnp.float32) * diff_lambda_k1[l].astype(jnp.float32)))
               - jnp.exp(jnp.sum(diff_lambda_q2[l].astype(jnp.float32) * diff_lambda_k2[l].astype(jnp.float32)))
               + lambda_init)
        b_out = diff_attention(dq[0], dq[1], dk[0], dk[1], split_heads(df_v, DIFF_HEADS), lam, alibi_slopes)
        b_out = rms_norm(b_out, diff_subln[l]) * (1.0 - lambda_init)

        cq = axial_rope(rms_norm(split_heads(gq_q, GQA_HEADS), gqa_q_norm[l]), grid_row, grid_col)
        ck = axial_rope(rms_norm(split_heads(gq_k, GQA_KV_HEADS), gqa_k_norm[l]), grid_row, grid_col)
        c_out = gqa_attention(cq, ck, split_heads(gq_v, GQA_KV_HEADS))

        c_q = rms_norm(ml_cq, mla_q_norm[l])
        q_full = split_heads(jnp.einsum('bsr,re->bse', c_q, mla_w_uq[l]), MLA_HEADS)
        q_nope, q_rope = q_full[..., :MLA_NOPE_DIM], rope(q_full[..., MLA_NOPE_DIM:], pos)
        c_kv = rms_norm(ml_ckv, mla_kv_norm[l])
        kv = split_heads(jnp.einsum('bsr,re->bse', c_kv, mla_w_ukv[l]), MLA_HEADS)
        k_nope, mla_v = kv[..., :MLA_NOPE_DIM], kv[..., MLA_NOPE_DIM:]
        k_rope = rope(ml_kr, pos)
        d_out = mla_attention(q_nope, q_rope, k_nope, k_rope, mla_v)

        mix = jnp.concatenate([merge_heads(a_out), merge_heads(b_out),
                               merge_heads(c_out), merge_heads(d_out)], axis=-1)
        mix = jnp.einsum('bse,ed->bsd', mix, w_o[l])
        x = x + rms_norm(mix, post_mix_norm[l])

        h = rms_norm(x, pre_ffn_norm[l])
        gate, up = jnp.split(jnp.einsum('bsd,df->bsf', h, ffn_w_gate_up[l]), 2, axis=-1)
        f = jnp.einsum('bsf,fd->bsd', jax.nn.silu(gate) * up, ffn_w_down[l])
        x = x + rms_norm(f, post_ffn_norm[l])
    return x
```

```python
import math
import contextlib
import numpy as np
import concourse.bass as bass
import concourse.mybir as mybir
from concourse.bass_utils import run_bass_kernel_spmd

F32 = mybir.dt.float32
BF16 = mybir.dt.bfloat16
AF = mybir.ActivationFunctionType
ALU = mybir.AluOpType
AX = mybir.AxisListType

S = 2048
D = 1024
NT = 16
DFF = 2816
NJ = 22
NL = 4
EPS = 1e-6
NEG = -30000.0
GW = 3968


class Buf:
    __slots__ = ("name", "w", "r", "excl")

    def __init__(self, name, excl=False):
        self.name = name
        self.w = {}
        self.r = {}
        self.excl = excl


class Op:
    __slots__ = ("eng", "fn", "deps", "sig", "sigval", "dma", "dmaval", "tag")


class Prog:
    ENGS = ("pe", "act", "dve", "pool", "sp")

    def __init__(self, nc):
        self.nc = nc
        self.ops = {e: [] for e in self.ENGS}
        self.dma_cnt = {}
        self.n = 0
        self.tag = ""
        self.names = {}

    def buf(self, name=None):
        self.n += 1
        return Buf(name or f"b{self.n}")

    def bufs(self, n, name="b"):
        return [self.buf(f"{name}{i}") for i in range(n)]

    def add(self, eng, fn, reads=(), writes=(), dma=None):
        op = Op()
        op.eng = eng
        op.fn = fn
        op.sig = False
        op.sigval = 0
        op.dma = dma
        op.dmaval = 0
        op.tag = self.tag
        if dma is not None:
            self.dma_cnt[dma] = self.dma_cnt.get(dma, 0) + 16
            op.dmaval = self.dma_cnt[dma]
        deps = {}
        for b in reads:
            for d in b.w.values():
                deps[d] = True
            if b.excl:
                for k2, d in b.r.items():
                    if k2 != eng:
                        deps.setdefault(d, False)
        for b in writes:
            for d in b.w.values():
                deps.setdefault(d, False)
            for d in b.r.values():
                deps.setdefault(d, False)
        op.deps = deps
        key = ("dma", dma) if dma is not None else eng
        for b in writes:
            b.w[key] = op
        for b in reads:
            b.r[key] = op
        self.ops[eng].append(op)
        return op

    @staticmethod
    def _skip(d, raw, ename):
        if d.dma is not None:
            return False
        if d.eng != ename:
            return False
        return ename == "pe"

    def emit(self, es):
        nc = self.nc
        for e in self.ENGS:
            for op in self.ops[e]:
                for d, raw in op.deps.items():
                    if d.dma is None and not self._skip(d, raw, e):
                        d.sig = True
        for e in self.ENGS:
            c = 0
            for op in self.ops[e]:
                if op.sig and op.dma is None:
                    c += 1
                    op.sigval = c
        engsem = {e: es.enter_context(nc.semaphore(f"sem_{e}")) for e in self.ENGS}
        dmasem = {k: es.enter_context(nc.semaphore(f"dsem_{k}")) for k in self.dma_cnt}
        block = es.enter_context(nc.Block())

        def run(ename):
            def body(eng):
                waited = {}
                for op in self.ops[ename]:
                    need = {}
                    for d, raw in op.deps.items():
                        if d.dma is not None:
                            key = ("d", d.dma)
                            v = d.dmaval
                        else:
                            if self._skip(d, raw, ename):
                                continue
                            key = ("e", d.eng)
                            v = d.sigval
                        if need.get(key, 0) < v:
                            need[key] = v
                    for key, v in need.items():
                        if waited.get(key, 0) < v:
                            sem = dmasem[key[1]] if key[0] == "d" else engsem[key[1]]
                            eng.wait_ge(sem, v)
                            waited[key] = v
                    if op.fn is None:
                        continue
                    inst = op.fn(eng)
                    try:
                        self.names[inst.ins.name] = op.tag
                    except Exception:
                        pass
                    if op.dma is not None:
                        inst.then_inc(dmasem[op.dma], 16)
                    elif op.sig:
                        inst.then_inc(engsem[ename], 1)

            return body

        block.tensor(run("pe"))
        block.scalar(run("act"))
        block.vector(run("dve"))
        block.gpsimd(run("pool"))
        block.sync(run("sp"))


def _rng(a, n):
    return list(range(a, a + n))


def w_in_perm():
    p = []
    for pr in range(2):
        p += _rng(0 + pr * 128, 128) + _rng(256 + pr * 128, 128) + _rng(512 + pr * 128, 128)
    for pr in range(2):
        p += _rng(768 + pr * 128, 128) + _rng(1024 + pr * 128, 128) + _rng(1280 + pr * 128, 128)
    for g in range(2):
        p += _rng(1536 + g * 128, 128) + _rng(1792 + g * 64, 64) + _rng(1920 + g * 64, 64)
    p += _rng(2048, 384) + _rng(2432, 32)
    return np.asarray(p)


OFF_NA = (0, 384)
OFF_DF = (768, 1152)
OFF_GQ = (1536, 1792)
OFF_MLA = 2048
OFF_KR = 2432

NA_CLASSES = ((0, 0), (1, 0), (2, 0), (3, 0), (4, 0), (29, 24), (30, 24), (31, 24))


def na_class(r):
    if r <= 3:
        return r
    if r >= 29:
        return 5 + (r - 29)
    return 4


def na_index_tables():
    p = np.arange(128)[:, None, None, None]
    cls_r = np.asarray([c[0] for c in NA_CLASSES])[None, :, None, None]
    cls_rs = np.asarray([c[1] for c in NA_CLASSES])[None, :, None, None]
    t = np.arange(4)[None, None, :, None]
    c = np.arange(64)[None, None, None, :]
    kr = cls_rs + 2 * t + p // 64
    kc = p % 64
    dr = kr - cls_r
    dc = kc - c
    cs = np.clip(c - 8, 0, 48)
    valid = (kc >= cs) & (kc < cs + 16)
    valid = np.broadcast_to(valid, (128, 8, 4, 64))
    dri = np.broadcast_to(dr + 7, (128, 8, 4, 64))
    dci = np.clip(np.broadcast_to(dc + 15, (128, 8, 4, 64)), 0, 30)
    return dri, dci, valid


def host_constants():
    c = {}
    slopes = [2.0 ** (-8.0 * (i + 1) / 4) for i in range(4)]
    pp = np.arange(128, dtype=np.float64)[:, None]
    cc = np.arange(GW, dtype=np.float64)[None, :]
    g = np.stack([np.exp(-s * np.abs(cc - pp - 1920.0)) for s in slopes]).astype(np.float32)
    c["alibi_g"] = g
    pos = np.arange(S)
    row = (pos // 64).astype(np.float32)
    col = (pos % 64).astype(np.float32)
    inv32 = (10000.0 ** (-np.arange(0, 32, 2, dtype=np.float32) / 32)).astype(np.float32)

    def cs_tables(p):
        ang = p[:, None] * inv32[None, :]
        cosv = np.cos(ang).astype(np.float32)
        sinv = np.sin(ang).astype(np.float32)
        return np.concatenate([cosv, cosv], 1), np.concatenate([-sinv, sinv], 1)

    cr, sr = cs_tables(row)
    cc_, sc_ = cs_tables(col)
    cg = np.concatenate([cr, cc_], 1)
    sg = np.concatenate([sr, sc_], 1)
    cm, sm = cs_tables(pos.astype(np.float32))

    def tm(a):
        return np.ascontiguousarray(a.reshape(NT, 128, -1).transpose(1, 0, 2))

    c["rope_g"] = np.concatenate([tm(cg), tm(sg)], 2)
    c["rope_m"] = np.concatenate([tm(cm), tm(sm)], 2)
    c["ident"] = np.eye(128, dtype=np.float32)
    c["ident8"] = (8.0 * np.eye(128)).astype(np.float32)
    return c


def prep_shared(inp):
    o = {}
    w_in = np.asarray(inp["w_in"], np.float32)
    perm = w_in_perm()
    w1 = w_in[:, :, perm].reshape(NL, 8, 128, 2464).transpose(0, 2, 1, 3)
    o["w1"] = np.ascontiguousarray(w1)
    o["wo"] = np.ascontiguousarray(np.asarray(inp["w_o"], np.float32).reshape(NL, 8, 128, 1024).transpose(0, 2, 1, 3))
    gu = np.asarray(inp["ffn_w_gate_up"], np.float32).reshape(NL, 8, 128, 2, NJ, 128)
    o["gu"] = np.ascontiguousarray(gu.transpose(0, 4, 2, 1, 3, 5)).reshape(NL, NJ, 128, 8 * 256)
    o["wd"] = np.ascontiguousarray(np.asarray(inp["ffn_w_down"], np.float32).reshape(NL, NJ, 128, 1024))
    o["uq"] = np.ascontiguousarray(np.asarray(inp["mla_w_uq"], np.float32).reshape(NL, 2, 128, 384).transpose(0, 2, 1, 3))
    o["ukv"] = np.ascontiguousarray(np.asarray(inp["mla_w_ukv"], np.float32))

    def pc(a):
        return np.ascontiguousarray(np.asarray(a, np.float32).reshape(NL, 8, 128).transpose(2, 0, 1))

    o["g_pre"] = np.ascontiguousarray(np.concatenate([pc(inp["pre_mix_norm"]), pc(inp["pre_ffn_norm"])], 1))
    o["g_post"] = np.ascontiguousarray(np.stack([np.asarray(inp["post_mix_norm"], np.float32),
                                                 np.asarray(inp["post_ffn_norm"], np.float32)], 1))
    qn = np.asarray(inp["gqa_q_norm"], np.float32)
    kn = np.asarray(inp["gqa_k_norm"], np.float32)
    o["g_gqa"] = np.ascontiguousarray(np.concatenate([qn, qn, kn, kn], 1))
    o["g_mla"] = np.ascontiguousarray(np.concatenate([np.asarray(inp["mla_q_norm"], np.float32),
                                                      np.asarray(inp["mla_kv_norm"], np.float32)], 1))
    sub = np.asarray(inp["diff_subln"], np.float32)
    o["g_sub"] = np.ascontiguousarray(np.concatenate([sub, sub], 1).T)
    lamv = np.stack([np.asarray(inp[k], np.float32) for k in
                     ("diff_lambda_q1", "diff_lambda_k1", "diff_lambda_q2", "diff_lambda_k2")], 0)
    o["lam_in"] = np.ascontiguousarray(lamv.reshape(1, 4 * NL * 32))
    rb = np.asarray(inp["na_rel_bias"], np.float32)
    dri, dci, valid = na_index_tables()
    nab = np.empty((NL, 2, 128, 2, 8, 4, 64), np.float32)
    for l in range(NL):
        for h in range(4):
            tbl = rb[l, h][dri, dci]
            tbl = np.where(valid, tbl, np.float32(NEG))
            nab[l, h // 2, :, h % 2] = tbl
    o["nab"] = nab.reshape(NL, 2, 128, 4096)
    o.update(host_constants())
    return o


def lambda_init(l):
    return 0.8 - 0.6 * math.exp(-0.3 * l)


class Builder:
    def __init__(self, nseq, layers, mixers=(0, 1, 2, 3), do_ffn=True, dbg=None):
        self.nseq = nseq
        self.layers = list(layers)
        self.mixers = mixers
        self.do_ffn = do_ffn
        self.dbg = dbg

    def mm(self, out, lhsT, rhs, start, stop, reads, writes):
        self.P.add("pe", lambda e: e.matmul(out, lhsT=lhsT, rhs=rhs, start=start, stop=stop), reads, writes)

    def tr(self, out, in_, reads, writes):
        idn = self.ident[0:in_.shape[0], 0:in_.shape[0]]
        self.P.add("pe", lambda e: e.transpose(out, in_, idn), reads, writes)

    def act(self, out, in_, func, reads, writes, **kw):
        self.P.add("act", lambda e: e.activation(out=out, in_=in_, func=func, **kw), reads, writes)

    def ts(self, eng, out, in0, s1, s2, op0, op1, reads, writes):
        if op1 is None:
            self.P.add(eng, lambda e: e.tensor_scalar(out=out, in0=in0, scalar1=s1, scalar2=None, op0=op0), reads, writes)
        else:
            self.P.add(eng, lambda e: e.tensor_scalar(out=out, in0=in0, scalar1=s1, scalar2=s2, op0=op0, op1=op1), reads, writes)

    def tt(self, eng, out, in0, in1, op, reads, writes):
        self.P.add(eng, lambda e: e.tensor_tensor(out=out, in0=in0, in1=in1, op=op), reads, writes)

    def stt(self, out, in0, scalar, in1, op0, op1, reads, writes):
        self.P.add("dve", lambda e: e.scalar_tensor_tensor(out=out, in0=in0, scalar=scalar, in1=in1, op0=op0, op1=op1), reads, writes)

    def cp(self, eng, out, in_, reads, writes):
        if eng == "act":
            self.P.add("act", lambda e: e.activation(out=out, in_=in_, func=AF.Copy), reads, writes)
        else:
            self.P.add(eng, lambda e: e.tensor_copy(out=out, in_=in_), reads, writes)

    def dma(self, q, out, in_, reads, writes, sem):
        self.P.add(q, lambda e: e.dma_start(out=out, in_=in_), reads, writes, dma=sem)

    def recip(self, out, in_, reads, writes):
        self.P.add("dve", lambda e: e.reciprocal(out=out, in_=in_), reads, writes)

    def build(self):
        nc = bass.Bass("TRN2", target_bir_lowering=False)
        self.nc = nc
        ns = self.nseq
        dt = nc.dram_tensor
        I = {}

        def inp(name, shape, dtype=F32):
            I[name] = dt(name, list(shape), dtype, kind="ExternalInput").ap()

        inp("x", [ns, S, D])
        inp("w1", [NL, 128, 8 * 2464])
        inp("wo", [NL, 128, 8 * 1024])
        inp("gu", [NL, NJ, 128, 2048])
        inp("wd", [NL, NJ, 128, 1024])
        inp("uq", [NL, 128, 768])
        inp("ukv", [NL, 128, 512])
        inp("g_pre", [128, 2 * NL * 8])
        inp("g_post", [NL, 2, 1024])
        inp("g_gqa", [NL, 256])
        inp("g_mla", [NL, 384])
        inp("g_sub", [128, NL])
        inp("lam_in", [1, 4 * NL * 32])
        inp("nab", [NL, 2, 128, 4096])
        inp("alibi_g", [4, 128, GW])
        inp("rope_g", [128, 16 * 128])
        inp("rope_m", [128, 16 * 64])
        inp("ident", [128, 128])
        inp("ident8", [128, 128])
        self.I = I
        y = dt("y", [ns, S, D], F32, kind="ExternalOutput").ap()
        self.y = y
        if self.dbg:
            self.dbg_out = dt("dbg", list(self.dbg[1]), self.dbg[2], kind="ExternalOutput").ap()
        sc = {}
        for name, shape in (("w1", [NL, 128, 8 * 2464]), ("wo", [NL, 128, 8 * 1024]), ("gu", [NL, NJ, 128, 2048]),
                            ("wd", [NL, NJ, 128, 1024]), ("uq", [NL, 128, 768]), ("ukv", [NL, 128, 512]),
                            ("nab", [NL, 2, 128, 4096]), ("alibi_g", [4, 128, GW])):
            sc[name] = dt("sc_" + name, shape, BF16, kind="Internal").ap()
        self.sc = sc

        with contextlib.ExitStack() as es:
            P = Prog(nc)
            self.P = P
            sb = lambda name, shape, dtype: es.enter_context(nc.sbuf_tensor(name, list(shape), dtype))
            self.x_sb = sb("x_sb", [128, NT, D], F32)
            self.hT = sb("hT", [128, 8, S], BF16)
            self.mixT = sb("mixT", [128, 8, S], BF16)
            self.QT = sb("QT", [128, 2, S], BF16)
            self.KT = sb("KT", [128, 2, S], BF16)
            self.VA = sb("VA", [128, 2, NT, 192], BF16)
            self.TAB = sb("TAB", [128, 4096], BF16)
            self.NPT = 3
            self.PT = sb("PT", [128, self.NPT, 512], BF16)
            self.WS = sb("WS", [128, 2, 8 * 384], BF16)
            self.GB = sb("GB", [128, 1024], F32)
            self.SCR = sb("SCR", [128, 4, 512], F32)
            self.SCRB = sb("SCRB", [128, 2, 512], BF16)
            self.HN = sb("HN", [128, 1, 1024], BF16)
            self.RB = sb("RB", [128, 1, 512], F32)
            self.ROPG = sb("ROPG", [128, 16, 128], BF16)
            self.ROPM = sb("ROPM", [128, 16, 64], BF16)
            self.ident = sb("identb", [128, 128], BF16)
            self.ident8 = sb("ident8b", [128, 128], BF16)
            self.onesb = sb("onesb", [128, 128], BF16)
            self.onesf = sb("onesf", [1, 128], F32)
            self.GPRE = sb("GPRE", [128, 2 * NL * 8], F32)
            self.GMLA = sb("GMLA", [128, 384], F32)
            self.GGQA = self.GMLA
            self.GSUB = sb("GSUB", [128, NL], F32)
            self.NLAM = sb("NLAM", [128, NL], F32)
            self.STAT = sb("STAT", [128, 64], F32)
            self.LT = sb("LT", [128, 3, 128], BF16)
            self.PS = es.enter_context(nc.psum_tensor("PS", [128, 8, 512], F32))

            B = {}
            self.B = B
            for k in ("QT0", "QT1", "KT0", "KT1", "TAB", "GB", "HN0", "HN1", "RB0", "RB1", "ROP", "CONST", "GPRE",
                      "GGQA", "GMLA", "GSUB", "NLAM", "STAT", "LT", "WS0", "WS1", "SCRB0", "SCRB1", "OUT", "DBG"):
                B[k] = P.buf(k)
            B["X"] = P.bufs(NT, "X")
            B["HT"] = P.bufs(4, "HT")
            B["MIX"] = [P.bufs(4, f"MIX{c}_") for c in range(8)]
            B["VA"] = [P.bufs(NT, f"VA{a}_") for a in range(2)]
            B["PT"] = P.bufs(self.NPT, "PT")
            B["PTX"] = P.bufs(6, "PTX")
            vflat = self.VA[:, 1, :, :].rearrange("p t n -> p (t n)")
            self.pt_base = [(self.PT[:, i, :], B["PT"][i]) for i in range(self.NPT)]
            self.pt_ext = self.pt_base + [(vflat[:, k * 512:(k + 1) * 512], B["PTX"][k]) for k in range(6)]
            self.pt_cur = self.pt_base
            B["SCR"] = P.bufs(4, "SCR")
            B["PSB"] = P.bufs(8, "PS")
            for b_ in B["PSB"]:
                b_.excl = True
            B["TABS"] = P.bufs(4, "TABS")
            B["SC"] = {k: [P.buf(f"sc_{k}{i}") for i in range(NL)] for k in sc}
            self.stat_i = 0

            P.tag = "pro"
            self.prologue()
            for s in range(ns):
                self.load_x(s)
                for li, l in enumerate(self.layers):
                    self.layer(s, l, last=(li == len(self.layers) - 1))
            if self.dbg:
                self.dbg[0](self)
            P.add("sp", None, reads=[B["OUT"], B["DBG"]])
            P.emit(es)
        return nc

    def prologue(self):
        P, B, I, sc = self.P, self.B, self.I, self.sc
        nc = self.nc

        def cast(name, l, src, dst, rows, cols):
            nch = max(1, (rows + 8191) // 8192)
            step = (rows + nch - 1) // nch
            for r0 in range(0, rows, step):
                r1 = min(rows, r0 + step)
                self.dma("pool", dst[r0:r1, :], src[r0:r1, :], [], [B["SC"][name][l]], f"c_{name}{l}")

        def flat(ap, cols):
            names = " ".join(f"d{i}" for i in range(ap.ndim))
            a = ap.rearrange(f"{names} -> ({names})")
            return a.rearrange("(r c) -> r c", c=cols)

        P.add("pool", lambda e: e.memset(self.onesb[:], 1.0), [], [B["CONST"]])
        P.add("pool", lambda e: e.memset(self.onesf[:], 1.0), [], [B["CONST"]])
        P.add("pool", lambda e: e.memset(self.VA[:], 1.0), [], [b for a in B["VA"] for b in a])
        self.dma("pool", self.ident[:], I["ident"], [], [B["CONST"]], "const")
        self.dma("pool", self.ident8[:], I["ident8"], [], [B["CONST"]], "const")
        self.dma("pool", self.ROPG[:].rearrange("p t n -> p (t n)"), I["rope_g"], [], [B["ROP"]], "rop")
        self.dma("pool", self.ROPM[:].rearrange("p t n -> p (t n)"), I["rope_m"], [], [B["ROP"]], "rop")
        self.dma("sp", self.GPRE[:], I["g_pre"], [], [B["GPRE"]], "gpre")
        self.dma("sp", self.GSUB[:], I["g_sub"], [], [B["GSUB"]], "gsub")
        self.dma("sp", self.SCR[0:1, 0, :], I["lam_in"], [], [B["SCR"][0], B["SCR"][1]], "lamr")
        for l in self.layers:
            cast("w1", l, flat(I["w1"][l], 1232), flat(sc["w1"][l], 1232), 2048, 1232)
            cast("nab", l, flat(I["nab"][l], 2048), flat(sc["nab"][l], 2048), 512, 2048)
            if l == self.layers[0]:
                cast("alibi_g", 0, flat(I["alibi_g"], 1984), flat(sc["alibi_g"], 1984), 1024, 1984)
            cast("uq", l, flat(I["uq"][l], 2048), flat(sc["uq"][l], 2048), 48, 2048)
            cast("ukv", l, flat(I["ukv"][l], 2048), flat(sc["ukv"][l], 2048), 32, 2048)
            cast("wo", l, flat(I["wo"][l], 2048), flat(sc["wo"][l], 2048), 512, 2048)
            cast("gu", l, flat(I["gu"][l], 2048), flat(sc["gu"][l], 2048), NJ * 128, 2048)
            cast("wd", l, flat(I["wd"][l], 2048), flat(sc["wd"][l], 2048), 1408, 2048)
        L = self.SCR[0:1, 0:2, :].rearrange("p a n -> p (a n)")
        n = NL * 32
        self.tt("dve", L[:, 512:512 + n], L[:, 0:n], L[:, n:2 * n], ALU.mult, [B["SCR"][0], B["SCR"][1]], [B["SCR"][0], B["SCR"][1]])
        self.tt("dve", L[:, 0:n], L[:, 2 * n:3 * n], L[:, 3 * n:4 * n], ALU.mult, [B["SCR"][0], B["SCR"][1]], [B["SCR"][0], B["SCR"][1]])
        P.add("dve", lambda e: e.tensor_reduce(out=L[:, 256:256 + NL], in_=L[:, 512:512 + n].rearrange("p (l k) -> p l k", k=32),
                                               axis=AX.X, op=ALU.add), [B["SCR"][0], B["SCR"][1]], [B["SCR"][0], B["SCR"][1]])
        P.add("dve", lambda e: e.tensor_reduce(out=L[:, 264:264 + NL], in_=L[:, 0:n].rearrange("p (l k) -> p l k", k=32),
                                               axis=AX.X, op=ALU.add), [B["SCR"][0], B["SCR"][1]], [B["SCR"][0], B["SCR"][1]])
        self.act(L[:, 272:272 + NL], L[:, 256:256 + NL], AF.Exp, [B["SCR"][0], B["SCR"][1]], [B["SCR"][0], B["SCR"][1]])
        self.act(L[:, 280:280 + NL], L[:, 264:264 + NL], AF.Exp, [B["SCR"][0], B["SCR"][1]], [B["SCR"][0], B["SCR"][1]])
        self.tt("dve", L[:, 288:288 + NL], L[:, 280:280 + NL], L[:, 272:272 + NL], ALU.subtract, [B["SCR"][0], B["SCR"][1]], [B["SCR"][0], B["SCR"][1]])
        for l in range(NL):
            self.ts("dve", L[:, 296 + l:297 + l], L[:, 288 + l:289 + l], -lambda_init(l), None, ALU.add, None,
                    [B["SCR"][0], B["SCR"][1]], [B["SCR"][0], B["SCR"][1]])
        ps = self.PS[:, 7, 0:NL]
        self.mm(ps, self.onesf[0:1, :], L[0:1, 296:296 + NL], True, True, [B["SCR"][0], B["SCR"][1], B["CONST"]], [B["PSB"][7]])
        self.cp("dve", self.NLAM[:], ps, [B["PSB"][7]], [B["NLAM"]])
        for l in range(NL):
            self.ts("dve", self.GSUB[:, l:l + 1], self.GSUB[:, l:l + 1], 1.0 - lambda_init(l), None, ALU.mult, None,
                    [B["GSUB"]], [B["GSUB"]])

    def load_x(self, s):
        for t in range(NT):
            self.dma("sp", self.x_sb[:, t, :], self.I["x"][s, t * 128:(t + 1) * 128, :], [], [self.B["X"][t]], f"x{t}")

    def store_x(self, s, t):
        self.dma("sp", self.y[s, t * 128:(t + 1) * 128, :], self.x_sb[:, t, :], [self.B["X"][t]], [self.B["OUT"]], f"y{t}")

    def stat(self, n=1):
        i = self.stat_i
        if i + n > 64:
            i = 0
        self.stat_i = i + n
        return self.STAT[:, i:i + n]

    def norm_to_hT(self, gcol):
        for t in range(NT):
            self.norm_tile(t, gcol, 6 + (t % 2))

    def norm_tile(self, t, gcol, bank):
        P, B = self.P, self.B
        gain = self.GPRE[:, gcol * 8:(gcol + 1) * 8].unsqueeze(2).broadcast_to([128, 8, 128])
        psb = self.PS[:, bank, :].bitcast(BF16)
        hn = self.HN[:, 0, :]
        hb = B["HN0"]
        ss = self.stat()
        self.act(hn, self.x_sb[:, t, :], AF.Square, [B["X"][t]], [hb, B["STAT"]], accum_out=ss)
        self.act(ss, ss, AF.Sqrt, [B["STAT"]], [B["STAT"]], scale=1.0 / D, bias=EPS)
        self.recip(ss, ss, [B["STAT"]], [B["STAT"]])
        self.ts("dve", hn, self.x_sb[:, t, :], ss, None, ALU.mult, None, [B["X"][t], B["STAT"]], [hb])
        for c in range(8):
            self.tr(psb[:, c * 128:(c + 1) * 128], hn[:, c * 128:(c + 1) * 128], [hb, B["CONST"]], [B["PSB"][bank]])
        self.tt("dve", self.hT[:, :, t * 128:(t + 1) * 128], psb.rearrange("p (c n) -> p c n", c=8), gain, ALU.mult,
                [B["PSB"][bank], B["GPRE"]], [B["HT"][t // 4]])

    def load_ws(self, slot, name, l, off, ncols, total):
        src = self.sc[name][l].rearrange("p (c n) -> p c n", c=8)[:, :, off:off + ncols]
        dst = self.WS[:, slot, 0:8 * ncols].rearrange("p (c n) -> p c n", c=8)
        self.dma("sp", dst, src, [self.B["SC"][name][l]], [self.B[f"WS{slot}"]], f"ws{slot}")
        return dst

    def proj_fm(self, w, c0, m, evac):
        B = self.B
        for tb in range(4):
            bank = 5 + (tb % 2)
            ps = self.PS[0:m, bank, :]
            for c in range(8):
                self.mm(ps, w[:, c, c0:c0 + m], self.hT[:, c, tb * 512:(tb + 1) * 512], c == 0, c == 7,
                        [B["HT"][tb], self.wsbuf], [B["PSB"][bank]])
            evac(tb, self.PS, bank)

    def proj_tm(self, w, c0, n, evac, tok0=0, ntiles=NT, banks=(5, 6)):
        B = self.B
        for t in range(ntiles):
            bank = banks[t % len(banks)]
            ps = self.PS[:, bank, 0:n]
            a = tok0 + t * 128
            tbs = sorted({a // 512, (a + 127) // 512})
            for c in range(8):
                self.mm(ps, self.hT[:, c, a:a + 128], w[:, c, c0:c0 + n], c == 0, c == 7,
                        [B["HT"][i] for i in tbs] + [self.wsbuf], [B["PSB"][bank]])
            evac(t, ps, bank)

    def evac_v(self, al):
        B = self.B

        def f(t, ps, bank):
            dst = self.VA[:, al, t, :].rearrange("p (a b) -> p a b", b=64)[:, 0:3:2, :]
            src = ps.rearrange("p (a b) -> p a b", b=64)
            self.cp("dve", dst, src, [B["PSB"][bank]], [B["VA"][al][t]])
        return f

    def normalize(self, bank, hh, dst, dst_bufs, dst_eng="dve"):
        B = self.B
        orows = slice(0, 64) if hh == 0 else slice(64, 128)
        srows = slice(64, 128) if hh == 0 else slice(0, 64)
        rb = self.RB[srows, 0, :]
        rbb = B["RB0"]
        self.act(rb, self.PS[srows, bank, :], AF.Ln, [B["PSB"][bank]], [rbb])
        self.act(rb, rb, AF.Exp, [rbb], [rbb], scale=-1.0)
        self.tt("dve", dst, self.PS[orows, bank, :], rb, ALU.mult, [B["PSB"][bank], rbb], dst_bufs)

    def attn_dense(self, qf, kf, vf, krows, scale, tabf, outf, reads):
        B = self.B
        sbanks = (0, 1, 2, 5, 6)
        obanks = (3, 4)
        look = 4
        pts = self.pt_cur
        cnt = getattr(self, "acnt", 0)
        for qb in range(4):
            ob = obanks[qb % 2]

            def qk(kt):
                sbk = sbanks[(cnt + kt) % len(sbanks)]
                self.mm(self.PS[:, sbk, :], kf(kt), qf(qb), True, True, reads, [B["PSB"][sbk]])

            def rest(kt):
                i = cnt + kt
                sbk = sbanks[i % len(sbanks)]
                pt, ptb = pts[i % len(pts)]
                self.act(pt, self.PS[:, sbk, :], AF.Exp, [B["PSB"][sbk]], [ptb], scale=scale)
                v, vb = vf(kt)
                self.mm(self.PS[:, ob, :], v, pt, kt == 0, kt == NT - 1, [ptb, vb], [B["PSB"][ob]])

            for kt in range(min(look, NT)):
                qk(kt)
            for kt in range(NT):
                if kt + look < NT:
                    qk(kt + look)
                rest(kt)
            cnt += NT
            outf(qb, ob)
        self.acnt = cnt

    def layer(self, s, l, last):
        P, B, sc = self.P, self.B, self.sc
        P.tag = "norm1"
        if not getattr(self, "norm_done", False):
            self.norm_to_hT(l)
        self.norm_done = False
        self.cur_last = last
        self.next_l = None if last else self.layers[self.layers.index(l) + 1]
        self.dma("sp", self.GB[:], self.I["g_post"][l, 0:1, :].broadcast_to([128, 1024]), [], [B["GB"]], "gb")
        slot = 0
        if 0 in self.mixers:
            for pr in range(2):
                P.tag = f"na{pr}"
                self.mixer_na(l, pr, slot)
                slot ^= 1
        else:
            self.zero_mix(0)
        if 1 in self.mixers:
            for pr in range(2):
                P.tag = f"diff{pr}"
                self.mixer_diff(l, pr, slot)
                slot ^= 1
        else:
            self.zero_mix(1)
        if 2 in self.mixers:
            self.dma("sp", self.GMLA[:, 0:256], self.I["g_gqa"][l:l + 1, :].broadcast_to([128, 256]), [], [B["GMLA"]], "gmla")
            for g in range(2):
                P.tag = f"gqa{g}"
                self.mixer_gqa(l, g, slot)
                slot ^= 1
        else:
            self.zero_mix(2)
        if 3 in self.mixers:
            self.dma("sp", self.GMLA[:], self.I["g_mla"][l:l + 1, :].broadcast_to([128, 384]), [], [B["GMLA"]], "gmla")
            for pr in range(2):
                P.tag = f"mla{pr}"
                self.mixer_mla(l, pr)
        else:
            self.zero_mix(3)
        P.tag = "wo"
        self.wo_residual(l)
        if self.do_ffn:
            P.tag = "ffn"
            self.ffn(s, l, last)
        elif last:
            for t in range(NT):
                self.store_x(s, t)

    def fence_ptx(self):
        self.P.add("pool", lambda e: e.memset(self.VA[0:1, 1, 15, 190:192], 1.0), [], self.B["VA"][1] + self.B["PTX"])
        self.pt_cur = self.pt_ext

    def zero_mix(self, m):
        for c in (2 * m, 2 * m + 1):
            self.P.add("pool", lambda e, c=c: e.memset(self.mixT[:, c, :], 0.0), [], self.B["MIX"][c])

    def mixer_na(self, l, pr, slot):
        P, B = self.P, self.B
        w = self.load_ws(slot, "w1", l, OFF_NA[pr], 384, 2464)
        self.wsbuf = B[f"WS{slot}"]
        self.dma("sp", self.TAB[:], self.sc["nab"][l, pr], [B["SC"]["nab"][l]], B["TABS"], "tab")
        chunk = pr

        def evq(dstT, dbuf):
            def f(tb, PS, bank):
                self.cp("dve", dstT[:, 0, tb * 512:(tb + 1) * 512], PS[:, bank, :], [B["PSB"][bank]], [dbuf])
            return f
        P.add("pool", lambda e: e.memset(self.KT[64:128, 0, :], 0.0), [], [B["KT0"]])
        P.add("pool", lambda e: e.memset(self.KT[0:64, 1, :], 0.0), [], [B["KT1"]])
        P.add("pool", lambda e: e.memset(self.VA[:, 1, :, 64:128], 1.0), [], B["VA"][1] + B["PTX"])
        self.pt_cur = self.pt_base

        def evk(tb, PS, bank):
            self.cp("dve", self.KT[0:64, 0, tb * 512:(tb + 1) * 512], PS[0:64, bank, :], [B["PSB"][bank]], [B["KT0"]])
            self.cp("dve", self.KT[64:128, 1, tb * 512:(tb + 1) * 512], PS[64:128, bank, :], [B["PSB"][bank]], [B["KT1"]])
        self.proj_fm(w, 0, 128, evq(self.QT, B["QT0"]))
        self.proj_fm(w, 128, 128, evk)
        self.proj_tm(w, 256, 128, self.evac_v(0))
        self.proj_tm(w, 256, 128, self.evac_v(1), tok0=64, ntiles=NT - 1)
        scale = 1.0 / 8.0
        for hh in range(2):
            hs = slice(hh * 64, hh * 64 + 64)
            for qb in range(4):
                ob = 3 + (qb % 2)
                for rp in range(4):
                    i = (hh * 4 + qb) * 4 + rp
                    sbk = i % 3
                    pi = i % self.NPT
                    rows = (qb * 8 + rp * 2, qb * 8 + rp * 2 + 1)
                    for ri, r in enumerate(rows):
                        cls = na_class(r)
                        tcol = (hh * 8 + cls) * 256
                        self.mm(self.PS[:, sbk, ri * 256:(ri + 1) * 256], self.ident8[:], self.TAB[:, tcol:tcol + 256],
                                ri == 0, False, B["TABS"] + [B["CONST"]], [B["PSB"][sbk]])
                    for ri, r in enumerate(rows):
                        rs = min(max(r - 4, 0), 24)
                        for t in range(4):
                            ks = (rs + 2 * t) * 64
                            self.mm(self.PS[:, sbk, ri * 256 + t * 64:ri * 256 + (t + 1) * 64], self.KT[:, hh, ks:ks + 128],
                                    self.QT[:, 0, r * 64:(r + 1) * 64], False, (ri == 1 and t == 3),
                                    [B["QT0"], B[f"KT{hh}"]], [B["PSB"][sbk]])
                    pt, ptb = self.pt_base[pi]
                    self.act(pt, self.PS[:, sbk, :], AF.Exp, [B["PSB"][sbk]], [ptb], scale=scale)
                    for ri, r in enumerate(rows):
                        rs = min(max(r - 4, 0), 24)
                        al = rs % 2
                        for t in range(4):
                            j = (rs + 2 * t) // 2
                            qc = (r % 8) * 64
                            self.mm(self.PS[:, ob, qc:qc + 64], self.VA[:, al, j, hh * 64:hh * 64 + 128],
                                    pt[:, ri * 256 + t * 64:ri * 256 + (t + 1) * 64], t == 0, t == 3,
                                    [B["PT"][pi], B["VA"][al][j]], [B["PSB"][ob]])
                orows = slice(0, 64) if hh == 0 else slice(64, 128)
                self.normalize(ob, hh, self.mixT[orows, chunk, qb * 512:(qb + 1) * 512], [B["MIX"][chunk][qb]])

    def mixer_diff(self, l, pr, slot):
        P, B = self.P, self.B
        w = self.load_ws(slot, "w1", l, OFF_DF[pr], 384, 2464)
        self.wsbuf = B[f"WS{slot}"]
        chunk = 2 + pr
        self.fence_ptx()
        for m in range(2):
            P.add("pool", lambda e, m=m: e.memset(self.KT[:, m, :], 0.0), [], [B[f"KT{m}"]])
        P.add("pool", lambda e: e.memset(self.QT[64:128, 0, :], 0.0), [], [B["QT0"]])
        P.add("pool", lambda e: e.memset(self.QT[0:64, 1, :], 0.0), [], [B["QT1"]])

        def evq(tb, PS, bank):
            self.cp("dve", self.QT[0:64, 0, tb * 512:(tb + 1) * 512], PS[0:64, bank, :], [B["PSB"][bank]], [B["QT0"]])
            self.cp("act", self.QT[64:128, 1, tb * 512:(tb + 1) * 512], PS[64:128, bank, :], [B["PSB"][bank]], [B["QT1"]])

        def evk(tb, PS, bank):
            for i in range(4):
                rs_ = slice(i * 32, (i + 1) * 32)
                m = i % 2
                self.cp("dve" if i < 2 else "act", self.KT[rs_, m, tb * 512:(tb + 1) * 512], PS[rs_, bank, :],
                        [B["PSB"][bank]], [B[f"KT{m}"]])
        self.proj_fm(w, 0, 128, evq)
        self.proj_fm(w, 128, 128, evk)
        self.proj_tm(w, 256, 128, self.evac_v(0))
        scale = 32.0 ** -0.5
        for hh in range(2):
            h = 2 * pr + hh
            self.dma("sp", self.TAB[:, 0:GW], self.sc["alibi_g"][h], [B["SC"]["alibi_g"][0]], B["TABS"], "tab")
            self.attn_diff_head(l, hh, scale, chunk)

    def attn_diff_head(self, l, hh, scale, chunk):
        B = self.B
        orows = slice(0, 64) if hh == 0 else slice(64, 128)
        sbanks = (0, 1, 2, 5, 6)
        look = 4
        pts = self.pt_cur
        cnt = getattr(self, "acnt", 0)
        for qb in range(4):
            obs = (3, 4)
            seq = [(m, kt) for m in range(2) for kt in range(NT)]

            def qk(idx):
                m, kt = seq[idx]
                sbk = sbanks[(cnt + idx) % len(sbanks)]
                self.mm(self.PS[:, sbk, :], self.KT[:, m, kt * 128:(kt + 1) * 128], self.QT[:, hh, qb * 512:(qb + 1) * 512],
                        True, True, [B[f"QT{hh}"], B[f"KT{m}"]], [B["PSB"][sbk]])

            def rest(idx):
                m, kt = seq[idx]
                i = cnt + idx
                sbk = sbanks[i % len(sbanks)]
                pt, ptb = pts[i % len(pts)]
                self.act(pt, self.PS[:, sbk, :], AF.Exp, [B["PSB"][sbk]], [ptb], scale=scale)
                c0 = qb * 512 - kt * 128 + 1920
                eng = "pool" if (i % 4 == 0) else "dve"
                self.tt(eng, pt, pt, self.TAB[:, c0:c0 + 512], ALU.mult, [ptb] + B["TABS"], [ptb])
                ob = obs[m]
                self.mm(self.PS[:, ob, :], self.VA[:, 0, kt, hh * 64:hh * 64 + 128], pt, kt == 0, kt == NT - 1,
                        [ptb, B["VA"][0][kt]], [B["PSB"][ob]])

            for idx in range(look):
                qk(idx)
            for idx in range(len(seq)):
                if idx + look < len(seq):
                    qk(idx + look)
                rest(idx)
            cnt += len(seq)
            for m in range(2):
                self.normalize(obs[m], hh, self.SCR[orows, m, :], [B["SCR"][m]])
            self.diff_finish(l, hh, qb, chunk)
        self.acnt = cnt

    def diff_finish(self, l, hh, qb, chunk):
        B = self.B
        orows = slice(0, 64) if hh == 0 else slice(64, 128)
        d1 = self.SCR[orows, 0, :]
        d2 = self.SCR[orows, 1, :]
        dd = self.SCR[orows, 2, :]
        rs = self.SCR[orows, 3, :]
        sq = self.SCRB[orows, 0, :]
        self.stt(dd, d2, self.NLAM[orows, l:l + 1], d1, ALU.mult, ALU.add, [B["SCR"][0], B["SCR"][1], B["NLAM"]], [B["SCR"][2]])
        self.tt("dve", sq, dd, dd, ALU.mult, [B["SCR"][2]], [B["SCRB0"]])
        bank = 7
        self.mm(self.PS[:, bank, :], self.onesb[orows, :], sq, True, True, [B["SCRB0"], B["CONST"]], [B["PSB"][bank]])
        self.act(rs, self.PS[orows, bank, :], AF.Ln, [B["PSB"][bank]], [B["SCR"][3]], scale=1.0 / 64.0, bias=EPS)
        self.act(rs, rs, AF.Exp, [B["SCR"][3]], [B["SCR"][3]], scale=-0.5)
        self.stt(self.mixT[orows, chunk, qb * 512:(qb + 1) * 512], dd, self.GSUB[orows, l:l + 1], rs, ALU.mult, ALU.mult,
                 [B["SCR"][2], B["SCR"][3], B["GSUB"]], [B["MIX"][chunk][qb]])

    def mixer_gqa(self, l, g, slot):
        P, B = self.P, self.B
        w = self.load_ws(slot, "w1", l, OFF_GQ[g], 256, 2464)
        self.wsbuf = B[f"WS{slot}"]
        chunk = 4 + g

        def ev(t, ps, bank):
            pb = B["PSB"][bank]
            dst = self.VA[:, 0, t, :].rearrange("p (a b) -> p a b", b=64)[:, 0:3:2, :]
            self.cp("dve", dst, ps[:, 192:256].unsqueeze(1).broadcast_to([128, 2, 64]), [pb], [B["VA"][0][t]])
            import os
            st = int(os.environ.get("DBG_EV", "99"))
            if st <= 0:
                return
            raw = self.SCR[:, 0, 0:192]
            self.cp(os.environ.get("DBG_RAWENG", "act"), raw, ps[:, 0:192], [pb] + ([B["VA"][0][t]] if os.environ.get("DBG_SER") else []), [B["SCR"][0]])
            if st <= 1:
                return
            sq = self.SCR[:, 1, 0:192]
            self.tt("dve", sq, raw, raw, ALU.mult, [B["SCR"][0]], [B["SCR"][1]])
            ss = self.stat(3)
            P.add("dve", lambda e: e.tensor_reduce(out=ss, in_=sq.rearrange("p (h d) -> p h d", d=64), axis=AX.X, op=ALU.add),
                  [B["SCR"][1]], [B["STAT"]])
            self.act(ss, ss, AF.Sqrt, [B["STAT"]], [B["STAT"]], scale=1.0 / 64.0, bias=EPS)
            self.recip(ss, ss, [B["STAT"]], [B["STAT"]])
            if st <= 2:
                return
            y = self.SCR[:, 1, 0:256]
            for h3 in range(3):
                self.stt(y[:, h3 * 64:(h3 + 1) * 64], raw[:, h3 * 64:(h3 + 1) * 64], ss[:, h3:h3 + 1],
                         self.GGQA[:, h3 * 64:(h3 + 1) * 64], ALU.mult, ALU.mult, [B["SCR"][0], B["STAT"], B["GMLA"]], [B["SCR"][1]])
            self.stt(y[:, 192:256], raw[:, 128:192], ss[:, 2:3], self.GGQA[:, 192:256], ALU.mult, ALU.mult,
                     [B["SCR"][0], B["STAT"], B["GMLA"]], [B["SCR"][1]])
            if st <= 3:
                return
            C = self.ROPG[:, t, 0:64].unsqueeze(1).broadcast_to([128, 4, 64])
            Sg = self.ROPG[:, t, 64:128]
            y3 = y.rearrange("p (h d) -> p h d", d=64)
            a = self.SCR[:, 2, 0:256].rearrange("p (h d) -> p h d", d=64)
            bsw = self.SCR[:, 3, 0:256].rearrange("p (h d) -> p h d", d=64)
            self.tt("dve", a, y3, C, ALU.mult, [B["SCR"][1], B["ROP"]], [B["SCR"][2]])
            y5 = y.rearrange("p (h a r d) -> p h a r d", a=2, r=2, d=16)
            b5 = self.SCR[:, 3, 0:256].rearrange("p (h a r d) -> p h a r d", a=2, r=2, d=16)
            s5 = Sg.rearrange("p (a r d) -> p a r d", a=2, r=2, d=16)
            for r in range(2):
                self.tt("dve", b5[:, :, :, r, :], y5[:, :, :, 1 - r, :], s5[:, :, r, :].unsqueeze(1).broadcast_to([128, 4, 2, 16]),
                        ALU.mult, [B["SCR"][1], B["ROP"]], [B["SCR"][3]])
            k = t % 2
            qk = self.SCRB[:, k, 0:256]
            qkb = B[f"SCRB{k}"]
            self.tt("dve", qk.rearrange("p (h d) -> p h d", d=64), a, bsw, ALU.add, [B["SCR"][2], B["SCR"][3]], [qkb])
            if st <= 4:
                return
            tbank = 6 + ((t // 4) % 2)
            psb = self.PS[:, tbank, :].bitcast(BF16)
            tq = t % 4
            self.tr(psb[:, tq * 128:(tq + 1) * 128], qk[:, 0:128], [qkb, B["CONST"]], [B["PSB"][tbank]])
            self.tr(psb[:, 512 + tq * 128:512 + (tq + 1) * 128], qk[:, 128:256], [qkb, B["CONST"]], [B["PSB"][tbank]])
            if tq == 3:
                tb = t // 4
                self.cp("act", self.QT[:, 0, tb * 512:(tb + 1) * 512], psb[:, 0:512], [B["PSB"][tbank]], [B["QT0"]])
                self.cp("act", self.KT[0:64, 0, tb * 512:(tb + 1) * 512], psb[0:64, 512:1024], [B["PSB"][tbank]], [B["KT0"]])
                self.cp("dve", self.KT[64:128, 1, tb * 512:(tb + 1) * 512], psb[64:128, 512:1024], [B["PSB"][tbank]], [B["KT1"]])

        P.add("pool", lambda e: e.memset(self.KT[64:128, 0, :], 0.0), [], [B["KT0"]])
        P.add("pool", lambda e: e.memset(self.KT[0:64, 1, :], 0.0), [], [B["KT1"]])
        self.fence_ptx()
        self.proj_tm(w, 0, 256, ev, banks=(5,))
        import os
        if os.environ.get("DBG_STAGE") == "1":
            self.P.add("pool", lambda e: e.memset(self.mixT[:, chunk, :], 0.0), [], self.B["MIX"][chunk])
            return
        P.tag = P.tag[:4] + "a"
        for hh in range(2):
            hs = slice(hh * 64, hh * 64 + 64)
            orows = hs

            def outf(qb, bank, hh=hh, orows=orows):
                self.normalize(bank, hh, self.mixT[orows, chunk, qb * 512:(qb + 1) * 512], [B["MIX"][chunk][qb]])

            self.attn_dense(lambda qb: self.QT[:, 0, qb * 512:(qb + 1) * 512],
                            lambda kt, hh=hh: self.KT[:, hh, kt * 128:(kt + 1) * 128],
                            lambda kt, hh=hh: (self.VA[:, 0, kt, hh * 64:hh * 64 + 128], B["VA"][0][kt]),
                            128, 1.0 / 8.0, None, outf, [B["QT0"], B[f"KT{hh}"]])

    def mixer_mla(self, l, pr):
        P, B = self.P, self.B
        wA = self.load_ws(0, "w1", l, OFF_MLA, 384, 2464)
        ws1 = self.WS[:, 1, :]
        self.dma("sp", ws1[:, 0:256].rearrange("p (c n) -> p c n", c=8),
                 self.sc["w1"][l].rearrange("p (c n) -> p c n", c=8)[:, :, OFF_KR:OFF_KR + 32], [B["SC"]["w1"][l]], [B["WS1"]], "ws1")
        self.dma("sp", ws1[:, 256:1024], self.sc["uq"][l], [B["SC"]["uq"][l]], [B["WS1"]], "ws1")
        self.dma("sp", ws1[:, 1024:1536], self.sc["ukv"][l], [B["SC"]["ukv"][l]], [B["WS1"]], "ws1")
        wkr = ws1[:, 0:256].rearrange("p (c n) -> p c n", c=8)
        wuq = ws1[:, 256:1024].rearrange("p (j n) -> p j n", j=2)
        wukv = ws1[:, 1024:1536]
        chunk = 6 + pr
        scale = 96.0 ** -0.5
        self.fence_ptx()
        for t in range(NT):
            bank = 5
            pb = B["PSB"][bank]
            ps = self.PS[:, bank, :]
            tb = t // 4
            for c in range(8):
                self.mm(ps[:, 0:384], self.hT[:, c, t * 128:(t + 1) * 128], wA[:, c, :], c == 0, c == 7,
                        [B["HT"][tb], B["WS0"]], [pb])
            for c in range(8):
                self.mm(ps[:, 384:416], self.hT[:, c, t * 128:(t + 1) * 128], wkr[:, c, :], c == 0, c == 7,
                        [B["HT"][tb], B["WS1"]], [pb])
            raw = self.SCR[:, 0, 0:416]
            self.cp("act", raw, ps[:, 0:416], [pb], [B["SCR"][0]])
            ss = self.stat(2)
            junk = self.SCR[:, 1, 0:384]
            self.act(junk[:, 0:256], raw[:, 0:256], AF.Square, [B["SCR"][0]], [B["SCR"][1], B["STAT"]], accum_out=ss[:, 0:1])
            self.act(junk[:, 256:384], raw[:, 256:384], AF.Square, [B["SCR"][0]], [B["SCR"][1], B["STAT"]], accum_out=ss[:, 1:2])
            self.act(ss[:, 0:1], ss[:, 0:1], AF.Sqrt, [B["STAT"]], [B["STAT"]], scale=1.0 / 256.0, bias=EPS)
            self.act(ss[:, 1:2], ss[:, 1:2], AF.Sqrt, [B["STAT"]], [B["STAT"]], scale=1.0 / 128.0, bias=EPS)
            self.recip(ss, ss, [B["STAT"]], [B["STAT"]])
            lat = self.SCRB[:, 0, 0:384]
            self.stt(lat[:, 0:256], raw[:, 0:256], ss[:, 0:1], self.GMLA[:, 0:256], ALU.mult, ALU.mult,
                     [B["SCR"][0], B["STAT"], B["GMLA"]], [B["SCRB0"]])
            self.stt(lat[:, 256:384], raw[:, 256:384], ss[:, 1:2], self.GMLA[:, 256:384], ALU.mult, ALU.mult,
                     [B["SCR"][0], B["STAT"], B["GMLA"]], [B["SCRB0"]])
            tbank = 6
            psb = self.PS[:, tbank, :].bitcast(BF16)
            for j in range(3):
                self.tr(psb[:, j * 128:(j + 1) * 128], lat[:, j * 128:(j + 1) * 128], [B["SCRB0"], B["CONST"]], [B["PSB"][tbank]])
            self.cp("dve", self.LT[:].rearrange("p j n -> p (j n)"), psb[:, 0:384], [B["PSB"][tbank]], [B["LT"]])
            qbank = 7
            pq = self.PS[:, qbank, :]
            for j in range(2):
                self.mm(pq[:, 0:192], self.LT[:, j, :], wuq[:, j, pr * 192:(pr + 1) * 192], j == 0, j == 1,
                        [B["LT"], B["WS1"]], [B["PSB"][qbank]])
            self.mm(pq[:, 192:448], self.LT[:, 2, :], wukv[:, pr * 256:(pr + 1) * 256], True, True,
                    [B["LT"], B["WS1"]], [B["PSB"][qbank]])
            pqb = B["PSB"][qbank]
            dst = self.VA[:, 0, t, :].rearrange("p (a b) -> p a b", b=64)[:, 0:3:2, :]
            srcv = pq[:, 192:448].rearrange("p (h a b) -> p h a b", h=2, a=2)[:, :, 1, :]
            self.cp("dve", dst, srcv, [pqb], [B["VA"][0][t]])
            qk = self.SCRB[:, 1, 0:384]
            qk4 = qk.rearrange("p (x d) -> p x d", d=96)
            self.cp("act", qk4[:, 0:2, 0:64], pq[:, 0:192].rearrange("p (h d) -> p h d", d=96)[:, :, 0:64], [pqb], [B["SCRB1"]])
            self.cp("act", qk4[:, 2:4, 0:64], pq[:, 192:448].rearrange("p (h a b) -> p h a b", h=2, a=2)[:, :, 0, :], [pqb], [B["SCRB1"]])
            rin = self.SCR[:, 2, 0:96].rearrange("p (x d) -> p x d", d=32)
            self.cp("dve", rin[:, 0:2, :], pq[:, 0:192].rearrange("p (h d) -> p h d", d=96)[:, :, 64:96], [pqb], [B["SCR"][2]])
            self.cp("dve", rin[:, 2, :], raw[:, 384:416], [B["SCR"][0]], [B["SCR"][2]])
            C = self.ROPM[:, t, 0:32].unsqueeze(1).broadcast_to([128, 3, 32])
            Sm = self.ROPM[:, t, 32:64].rearrange("p (r d) -> p r d", r=2)
            ra = self.SCR[:, 3, 0:96].rearrange("p (x d) -> p x d", d=32)
            rbm = self.SCR[:, 3, 96:192].rearrange("p (x r d) -> p x r d", x=3, r=2)
            rin4 = self.SCR[:, 2, 0:96].rearrange("p (x r d) -> p x r d", x=3, r=2)
            self.tt("dve", ra, rin, C, ALU.mult, [B["SCR"][2], B["ROP"]], [B["SCR"][3]])
            for r in range(2):
                self.tt("dve", rbm[:, :, r, :], rin4[:, :, 1 - r, :], Sm[:, r, :].unsqueeze(1).broadcast_to([128, 3, 16]),
                        ALU.mult, [B["SCR"][2], B["ROP"]], [B["SCR"][3]])
            rb3 = self.SCR[:, 3, 96:192].rearrange("p (x d) -> p x d", d=32)
            self.tt("dve", qk4[:, 0:2, 64:96], ra[:, 0:2, :], rb3[:, 0:2, :], ALU.add, [B["SCR"][3]], [B["SCRB1"]])
            for hk in range(2):
                self.tt("dve", qk4[:, 2 + hk, 64:96], ra[:, 2, :], rb3[:, 2, :], ALU.add, [B["SCR"][3]], [B["SCRB1"]])
            for x4 in range(4):
                self.tr(psb[0:96, 512 + x4 * 128:512 + (x4 + 1) * 128], qk4[:, x4, :], [B["SCRB1"], B["CONST"]], [B["PSB"][tbank]])
            for hh in range(2):
                self.cp("dve", self.QT[0:96, hh, t * 128:(t + 1) * 128], psb[0:96, 512 + hh * 128:512 + (hh + 1) * 128],
                        [B["PSB"][tbank]], [B[f"QT{hh}"]])
                self.cp("act", self.KT[0:96, hh, t * 128:(t + 1) * 128], psb[0:96, 512 + (2 + hh) * 128:512 + (3 + hh) * 128],
                        [B["PSB"][tbank]], [B[f"KT{hh}"]])
        P.tag = P.tag[:4] + "a"
        for hh in range(2):
            orows = slice(hh * 64, hh * 64 + 64)

            def outf(qb, bank, hh=hh, orows=orows):
                self.normalize(bank, hh, self.mixT[orows, chunk, qb * 512:(qb + 1) * 512], [B["MIX"][chunk][qb]])

            self.attn_dense(lambda qb, hh=hh: self.QT[0:96, hh, qb * 512:(qb + 1) * 512],
                            lambda kt, hh=hh: self.KT[0:96, hh, kt * 128:(kt + 1) * 128],
                            lambda kt, hh=hh: (self.VA[:, 0, kt, hh * 64:hh * 64 + 128], B["VA"][0][kt]),
                            96, scale, None, outf, [B[f"QT{hh}"], B[f"KT{hh}"]])

    def post_residual(self, t, b0):
        B = self.B
        fps = self.PS[:, b0:b0 + 2, :].rearrange("p a n -> p (a n)")
        pbs = [B["PSB"][b0], B["PSB"][b0 + 1]]
        ss = self.stat()
        junk = self.SCRB[:].rearrange("p a n -> p (a n)")
        self.act(junk, fps, AF.Square, pbs, [B["SCRB0"], B["SCRB1"], B["STAT"]], accum_out=ss)
        self.act(ss, ss, AF.Sqrt, [B["STAT"]], [B["STAT"]], scale=1.0 / D, bias=EPS)
        self.recip(ss, ss, [B["STAT"]], [B["STAT"]])
        tmp = self.SCR[:, 0:2, :].rearrange("p a n -> p (a n)")
        self.tt("dve", tmp, fps, self.GB[:], ALU.mult, pbs + [B["GB"]], [B["SCR"][0], B["SCR"][1]])
        self.stt(self.x_sb[:, t, :], tmp, ss, self.x_sb[:, t, :], ALU.mult, ALU.add,
                 [B["SCR"][0], B["SCR"][1], B["STAT"], B["X"][t]], [B["X"][t]])

    def wo_residual(self, l):
        P, B = self.P, self.B
        wsrc = self.sc["wo"][l].rearrange("p (c n) -> p c n", c=8)
        w0 = self.WS[:, 0, :].rearrange("p (c n) -> p c n", c=8)
        w1 = self.WS[:, 1, :].rearrange("p (c n) -> p c n", c=8)
        w2 = self.TAB[:, 0:2048].rearrange("p (c n) -> p c n", c=8)
        self.dma("sp", w0, wsrc[:, :, 0:384], [B["SC"]["wo"][l]], [B["WS0"]], "ws0")
        self.dma("sp", w1, wsrc[:, :, 384:768], [B["SC"]["wo"][l]], [B["WS1"]], "ws1")
        self.dma("sp", w2, wsrc[:, :, 768:1024], [B["SC"]["wo"][l]], B["TABS"][0:2], "tab")
        segs = ((0, 384, w0, 0, B["WS0"]), (384, 128, w1, 0, B["WS1"]), (512, 256, w1, 128, B["WS1"]), (768, 256, w2, 0, B["TABS"][0]))
        for t in range(NT):
            b0 = 0 if t % 2 == 0 else 2
            tb = t // 4
            for (o0, n, wt, wo_, wb) in segs:
                bank = b0 + o0 // 512
                col = o0 % 512
                for c in range(8):
                    self.mm(self.PS[:, bank, col:col + n], self.mixT[:, c, t * 128:(t + 1) * 128], wt[:, c, wo_:wo_ + n],
                            c == 0, c == 7, [B["MIX"][c][tb], wb, B["TABS"][1]], [B["PSB"][bank]])
            self.post_residual(t, b0)
            if self.do_ffn and t >= 1:
                self.norm_tile(t - 1, NL + l, 6 + ((t - 1) % 2))
        if self.do_ffn:
            self.norm_tile(NT - 1, NL + l, 6 + ((NT - 1) % 2))

    def ffn(self, s, l, last):
        P, B = self.P, self.B
        self.dma("sp", self.GB[:], self.I["g_post"][l, 1:2, :].broadcast_to([128, 1024]), [], [B["GB"]], "gb")
        actT = self.mixT[:].rearrange("p c n -> p (c n)")[:, 0:NJ * 512].rearrange("p (j n) -> p j n", j=NJ)
        abufs = [b for c in range(8) for b in B["MIX"][c]]
        gcnt = 0
        for tb in range(4):
            for j in range(NJ):
                slot = gcnt % 2
                gcnt += 1
                wv = self.WS[:, slot, 0:2048].rearrange("p (c n) -> p c n", c=8)
                self.dma("sp", self.WS[:, slot, 0:2048], self.sc["gu"][l, j], [B["SC"]["gu"][l]], [B[f"WS{slot}"]], f"ws{slot}")
                gb = 4 + 2 * (j % 2)
                ub = gb + 1
                for c in range(8):
                    self.mm(self.PS[:, gb, :], wv[:, c, 0:128], self.hT[:, c, tb * 512:(tb + 1) * 512], c == 0, c == 7,
                            [B["HT"][tb], B[f"WS{slot}"]], [B["PSB"][gb]])
                for c in range(8):
                    self.mm(self.PS[:, ub, :], wv[:, c, 128:256], self.hT[:, c, tb * 512:(tb + 1) * 512], c == 0, c == 7,
                            [B["HT"][tb], B[f"WS{slot}"]], [B["PSB"][ub]])
                si = 2 + (j % 2)
                sg = self.SCR[:, si, :]
                self.act(sg, self.PS[:, gb, :], AF.Silu, [B["PSB"][gb]], [B["SCR"][si]])
                self.tt("dve", actT[:, j, :], self.PS[:, ub, :], sg, ALU.mult, [B["PSB"][ub], B["SCR"][si]], [self.abuf(j)])
            for j in range(NJ):
                ts_ = j % 4
                wdt = self.TAB[:, ts_ * 1024:(ts_ + 1) * 1024]
                self.dma("sp", wdt, self.sc["wd"][l, j], [B["SC"]["wd"][l]], [B["TABS"][ts_]], f"tabs{ts_}")
                for tt_ in range(4):
                    for ch in range(2):
                        bank = tt_ * 2 + ch
                        self.mm(self.PS[:, bank, :], actT[:, j, tt_ * 128:(tt_ + 1) * 128], wdt[:, ch * 512:(ch + 1) * 512],
                                j == 0, j == NJ - 1, [self.abuf(j), B["TABS"][ts_]], [B["PSB"][bank]])
            for tt_ in range(4):
                t = tb * 4 + tt_
                self.post_residual(t, tt_ * 2)
                if last:
                    self.store_x(s, t)
                elif self.next_l is not None and tt_ >= 1:
                    self.norm_tile(t - 1, self.next_l, (tt_ - 1) % 2)
            if not last and self.next_l is not None:
                self.norm_tile(tb * 4 + 3, self.next_l, 1)
            if tb == 3 and not last and self.next_l is not None:
                self.norm_done = True

    def abuf(self, j):
        return self.B["MIX"][j // 4][j % 4]


_CACHE = {}


def kernel(**inputs):
    ncores = 8
    x = np.ascontiguousarray(np.asarray(inputs["x"], np.float32))
    nb = x.shape[0]
    per = nb // ncores
    shared = prep_shared(inputs)
    key = ("full", per)
    if key not in _CACHE:
        _CACHE[key] = Builder(per, range(NL)).build()
    nc = _CACHE[key]
    in_maps = []
    for c in range(ncores):
        m = dict(shared)
        m["x"] = x[c * per:(c + 1) * per]
        m["w1"] = shared["w1"].reshape(NL, 128, 8 * 2464)
        m["wo"] = shared["wo"].reshape(NL, 128, 8 * 1024)
        m["uq"] = shared["uq"].reshape(NL, 128, 768)
        m["g_pre"] = shared["g_pre"].reshape(128, 2 * NL * 8)
        m["rope_g"] = shared["rope_g"].reshape(128, 16 * 128)
        m["rope_m"] = shared["rope_m"].reshape(128, 16 * 64)
        in_maps.append(m)
    res = run_bass_kernel_spmd(nc, in_maps, core_ids=list(range(ncores)))
    out = np.concatenate([np.asarray(r["y"]) for r in res.results], axis=0)
    return out.astype(np.float32)
```

```python
import math
import contextlib
import numpy as np
import concourse.bass as bass
import concourse.mybir as mybir
from concourse.bass_utils import run_bass_kernel_spmd

F32 = mybir.dt.float32
BF16 = mybir.dt.bfloat16
AF = mybir.ActivationFunctionType
ALU = mybir.AluOpType
AX = mybir.AxisListType

S = 2048
D = 1024
NT = 16
DFF = 2816
NJ = 22
NL = 4
EPS = 1e-6
NEG = -30000.0
GW = 3968


class Buf:
    __slots__ = ("name", "w", "r", "excl")

    def __init__(self, name, excl=False):
        self.name = name
        self.w = {}
        self.r = {}
        self.excl = excl


class Op:
    __slots__ = ("eng", "fn", "deps", "sig", "sigval", "dma", "dmaval", "tag")


class Prog:
    ENGS = ("pe", "act", "dve", "pool", "sp")

    def __init__(self, nc):
        self.nc = nc
        self.ops = {e: [] for e in self.ENGS}
        self.dma_cnt = {}
        self.n = 0
        self.tag = ""
        self.names = {}

    def buf(self, name=None):
        self.n += 1
        return Buf(name or f"b{self.n}")

    def bufs(self, n, name="b"):
        return [self.buf(f"{name}{i}") for i in range(n)]

    def add(self, eng, fn, reads=(), writes=(), dma=None):
        op = Op()
        op.eng = eng
        op.fn = fn
        op.sig = False
        op.sigval = 0
        op.dma = dma
        op.dmaval = 0
        op.tag = self.tag
        if dma is not None:
            self.dma_cnt[dma] = self.dma_cnt.get(dma, 0) + 16
            op.dmaval = self.dma_cnt[dma]
        deps = {}
        for b in reads:
            for d in b.w.values():
                deps[d] = True
            if b.excl:
                for k2, d in b.r.items():
                    if k2 != eng:
                        deps.setdefault(d, False)
        for b in writes:
            for d in b.w.values():
                deps.setdefault(d, False)
            for d in b.r.values():
                deps.setdefault(d, False)
        op.deps = deps
        key = ("dma", dma) if dma is not None else eng
        for b in writes:
            b.w[key] = op
        for b in reads:
            b.r[key] = op
        self.ops[eng].append(op)
        return op

    @staticmethod
    def _skip(d, raw, ename):
        if d.dma is not None:
            return False
        if d.eng != ename:
            return False
        return ename == "pe"

    def emit(self, es):
        nc = self.nc
        for e in self.ENGS:
            for op in self.ops[e]:
                for d, raw in op.deps.items():
                    if d.dma is None and not self._skip(d, raw, e):
                        d.sig = True
        for e in self.ENGS:
            c = 0
            for op in self.ops[e]:
                if op.sig and op.dma is None:
                    c += 1
                    op.sigval = c
        engsem = {e: es.enter_context(nc.semaphore(f"sem_{e}")) for e in self.ENGS}
        dmasem = {k: es.enter_context(nc.semaphore(f"dsem_{k}")) for k in self.dma_cnt}
        block = es.enter_context(nc.Block())

        def run(ename):
            def body(eng):
                waited = {}
                for op in self.ops[ename]:
                    need = {}
                    for d, raw in op.deps.items():
                        if d.dma is not None:
                            key = ("d", d.dma)
                            v = d.dmaval
                        else:
                            if self._skip(d, raw, ename):
                                continue
                            key = ("e", d.eng)
                            v = d.sigval
                        if need.get(key, 0) < v:
                            need[key] = v
                    for key, v in need.items():
                        if waited.get(key, 0) < v:
                            sem = dmasem[key[1]] if key[0] == "d" else engsem[key[1]]
                            eng.wait_ge(sem, v)
                            waited[key] = v
                    if op.fn is None:
                        continue
                    inst = op.fn(eng)
                    try:
                        self.names[inst.ins.name] = op.tag
                    except Exception:
                        pass
                    if op.dma is not None:
                        inst.then_inc(dmasem[op.dma], 16)
                    elif op.sig:
                        inst.then_inc(engsem[ename], 1)

            return body

        block.tensor(run("pe"))
        block.scalar(run("act"))
        block.vector(run("dve"))
        block.gpsimd(run("pool"))
        block.sync(run("sp"))


def _rng(a, n):
    return list(range(a, a + n))


def w_in_perm():
    p = []
    for pr in range(2):
        p += _rng(0 + pr * 128, 128) + _rng(256 + pr * 128, 128) + _rng(512 + pr * 128, 128)
    for pr in range(2):
        p += _rng(768 + pr * 128, 128) + _rng(1024 + pr * 128, 128) + _rng(1280 + pr * 128, 128)
    for g in range(2):
        p += _rng(1536 + g * 128, 128) + _rng(1792 + g * 64, 64) + _rng(1920 + g * 64, 64)
    p += _rng(2048, 384) + _rng(2432, 32)
    return np.asarray(p)


OFF_NA = (0, 384)
OFF_DF = (768, 1152)
OFF_GQ = (1536, 1792)
OFF_MLA = 2048
OFF_KR = 2432

NA_CLASSES = ((0, 0), (1, 0), (2, 0), (3, 0), (4, 0), (29, 24), (30, 24), (31, 24))


def na_class(r):
    if r <= 3:
        return r
    if r >= 29:
        return 5 + (r - 29)
    return 4


def na_index_tables():
    p = np.arange(128)[:, None, None, None]
    cls_r = np.asarray([c[0] for c in NA_CLASSES])[None, :, None, None]
    cls_rs = np.asarray([c[1] for c in NA_CLASSES])[None, :, None, None]
    t = np.arange(4)[None, None, :, None]
    c = np.arange(64)[None, None, None, :]
    kr = cls_rs + 2 * t + p // 64
    kc = p % 64
    dr = kr - cls_r
    dc = kc - c
    cs = np.clip(c - 8, 0, 48)
    valid = (kc >= cs) & (kc < cs + 16)
    valid = np.broadcast_to(valid, (128, 8, 4, 64))
    dri = np.broadcast_to(dr + 7, (128, 8, 4, 64))
    dci = np.clip(np.broadcast_to(dc + 15, (128, 8, 4, 64)), 0, 30)
    return dri, dci, valid


def host_constants():
    c = {}
    slopes = [2.0 ** (-8.0 * (i + 1) / 4) for i in range(4)]
    pp = np.arange(128, dtype=np.float64)[:, None]
    cc = np.arange(GW, dtype=np.float64)[None, :]
    g = np.stack([np.exp(-s * np.abs(cc - pp - 1920.0)) for s in slopes]).astype(np.float32)
    c["alibi_g"] = g
    pos = np.arange(S)
    row = (pos // 64).astype(np.float32)
    col = (pos % 64).astype(np.float32)
    inv32 = (10000.0 ** (-np.arange(0, 32, 2, dtype=np.float32) / 32)).astype(np.float32)

    def cs_tables(p):
        ang = p[:, None] * inv32[None, :]
        cosv = np.cos(ang).astype(np.float32)
        sinv = np.sin(ang).astype(np.float32)
        return np.concatenate([cosv, cosv], 1), np.concatenate([-sinv, sinv], 1)

    cr, sr = cs_tables(row)
    cc_, sc_ = cs_tables(col)
    cg = np.concatenate([cr, cc_], 1)
    sg = np.concatenate([sr, sc_], 1)
    cm, sm = cs_tables(pos.astype(np.float32))

    def tm(a):
        return np.ascontiguousarray(a.reshape(NT, 128, -1).transpose(1, 0, 2))

    c["rope_g"] = np.concatenate([tm(cg), tm(sg)], 2)
    c["rope_m"] = np.concatenate([tm(cm), tm(sm)], 2)
    c["ident"] = np.eye(128, dtype=np.float32)
    c["ident8"] = (8.0 * np.eye(128)).astype(np.float32)
    return c


def prep_shared(inp):
    o = {}
    w_in = np.asarray(inp["w_in"], np.float32)
    perm = w_in_perm()
    w1 = w_in[:, :, perm].reshape(NL, 8, 128, 2464).transpose(0, 2, 1, 3)
    o["w1"] = np.ascontiguousarray(w1)
    o["wo"] = np.ascontiguousarray(np.asarray(inp["w_o"], np.float32).reshape(NL, 8, 128, 1024).transpose(0, 2, 1, 3))
    gu = np.asarray(inp["ffn_w_gate_up"], np.float32).reshape(NL, 8, 128, 2, NJ, 128)
    o["gu"] = np.ascontiguousarray(gu.transpose(0, 4, 2, 1, 3, 5)).reshape(NL, NJ, 128, 8 * 256)
    o["wd"] = np.ascontiguousarray(np.asarray(inp["ffn_w_down"], np.float32).reshape(NL, NJ, 128, 1024))
    o["uq"] = np.ascontiguousarray(np.asarray(inp["mla_w_uq"], np.float32).reshape(NL, 2, 128, 384).transpose(0, 2, 1, 3))
    o["ukv"] = np.ascontiguousarray(np.asarray(inp["mla_w_ukv"], np.float32))

    def pc(a):
        return np.ascontiguousarray(np.asarray(a, np.float32).reshape(NL, 8, 128).transpose(2, 0, 1))

    o["g_pre"] = np.ascontiguousarray(np.concatenate([pc(inp["pre_mix_norm"]), pc(inp["pre_ffn_norm"])], 1))
    o["g_post"] = np.ascontiguousarray(np.stack([np.asarray(inp["post_mix_norm"], np.float32),
                                                 np.asarray(inp["post_ffn_norm"], np.float32)], 1))
    qn = np.asarray(inp["gqa_q_norm"], np.float32)
    kn = np.asarray(inp["gqa_k_norm"], np.float32)
    o["g_gqa"] = np.ascontiguousarray(np.concatenate([qn, qn, kn, kn], 1))
    o["g_mla"] = np.ascontiguousarray(np.concatenate([np.asarray(inp["mla_q_norm"], np.float32),
                                                      np.asarray(inp["mla_kv_norm"], np.float32)], 1))
    sub = np.asarray(inp["diff_subln"], np.float32)
    o["g_sub"] = np.ascontiguousarray(np.concatenate([sub, sub], 1).T)
    lamv = np.stack([np.asarray(inp[k], np.float32) for k in
                     ("diff_lambda_q1", "diff_lambda_k1", "diff_lambda_q2", "diff_lambda_k2")], 0)
    o["lam_in"] = np.ascontiguousarray(lamv.reshape(1, 4 * NL * 32))
    rb = np.asarray(inp["na_rel_bias"], np.float32)
    dri, dci, valid = na_index_tables()
    nab = np.empty((NL, 2, 128, 2, 8, 4, 64), np.float32)
    for l in range(NL):
        for h in range(4):
            tbl = rb[l, h][dri, dci]
            tbl = np.where(valid, tbl, np.float32(NEG))
            nab[l, h // 2, :, h % 2] = tbl
    o["nab"] = nab.reshape(NL, 2, 128, 4096)
    o.update(host_constants())
    return o


def lambda_init(l):
    return 0.8 - 0.6 * math.exp(-0.3 * l)


class Builder:
    def __init__(self, nseq, layers, mixers=(0, 1, 2, 3), do_ffn=True, dbg=None):
        self.nseq = nseq
        self.layers = list(layers)
        self.mixers = mixers
        self.do_ffn = do_ffn
        self.dbg = dbg

    def mm(self, out, lhsT, rhs, start, stop, reads, writes):
        self.P.add("pe", lambda e: e.matmul(out, lhsT=lhsT, rhs=rhs, start=start, stop=stop), reads, writes)

    def tr(self, out, in_, reads, writes):
        idn = self.ident[0:in_.shape[0], 0:in_.shape[0]]
        self.P.add("pe", lambda e: e.transpose(out, in_, idn), reads, writes)

    def act(self, out, in_, func, reads, writes, **kw):
        self.P.add("act", lambda e: e.activation(out=out, in_=in_, func=func, **kw), reads, writes)

    def ts(self, eng, out, in0, s1, s2, op0, op1, reads, writes):
        if op1 is None:
            self.P.add(eng, lambda e: e.tensor_scalar(out=out, in0=in0, scalar1=s1, scalar2=None, op0=op0), reads, writes)
        else:
            self.P.add(eng, lambda e: e.tensor_scalar(out=out, in0=in0, scalar1=s1, scalar2=s2, op0=op0, op1=op1), reads, writes)

    def tt(self, eng, out, in0, in1, op, reads, writes):
        self.P.add(eng, lambda e: e.tensor_tensor(out=out, in0=in0, in1=in1, op=op), reads, writes)

    def stt(self, out, in0, scalar, in1, op0, op1, reads, writes):
        self.P.add("dve", lambda e: e.scalar_tensor_tensor(out=out, in0=in0, scalar=scalar, in1=in1, op0=op0, op1=op1), reads, writes)

    def cp(self, eng, out, in_, reads, writes):
        if eng == "act":
            self.P.add("act", lambda e: e.activation(out=out, in_=in_, func=AF.Copy), reads, writes)
        else:
            self.P.add(eng, lambda e: e.tensor_copy(out=out, in_=in_), reads, writes)

    def dma(self, q, out, in_, reads, writes, sem):
        self.P.add(q, lambda e: e.dma_start(out=out, in_=in_), reads, writes, dma=sem)

    def recip(self, out, in_, reads, writes):
        self.P.add("dve", lambda e: e.reciprocal(out=out, in_=in_), reads, writes)

    def build(self):
        nc = bass.Bass("TRN2", target_bir_lowering=False)
        self.nc = nc
        ns = self.nseq
        dt = nc.dram_tensor
        I = {}

        def inp(name, shape, dtype=F32):
            I[name] = dt(name, list(shape), dtype, kind="ExternalInput").ap()

        inp("x", [ns, S, D])
        inp("w1", [NL, 128, 8 * 2464])
        inp("wo", [NL, 128, 8 * 1024])
        inp("gu", [NL, NJ, 128, 2048])
        inp("wd", [NL, NJ, 128, 1024])
        inp("uq", [NL, 128, 768])
        inp("ukv", [NL, 128, 512])
        inp("g_pre", [128, 2 * NL * 8])
        inp("g_post", [NL, 2, 1024])
        inp("g_gqa", [NL, 256])
        inp("g_mla", [NL, 384])
        inp("g_sub", [128, NL])
        inp("lam_in", [1, 4 * NL * 32])
        inp("nab", [NL, 2, 128, 4096])
        inp("alibi_g", [4, 128, GW])
        inp("rope_g", [128, 16 * 128])
        inp("rope_m", [128, 16 * 64])
        inp("ident", [128, 128])
        inp("ident8", [128, 128])
        self.I = I
        y = dt("y", [ns, S, D], F32, kind="ExternalOutput").ap()
        self.y = y
        if self.dbg:
            self.dbg_out = dt("dbg", list(self.dbg[1]), self.dbg[2], kind="ExternalOutput").ap()
        sc = {}
        for name, shape in (("w1", [NL, 128, 8 * 2464]), ("wo", [NL, 128, 8 * 1024]), ("gu", [NL, NJ, 128, 2048]),
                            ("wd", [NL, NJ, 128, 1024]), ("uq", [NL, 128, 768]), ("ukv", [NL, 128, 512]),
                            ("nab", [NL, 2, 128, 4096]), ("alibi_g", [4, 128, GW])):
            sc[name] = dt("sc_" + name, shape, BF16, kind="Internal").ap()
        self.sc = sc

        with contextlib.ExitStack() as es:
            P = Prog(nc)
            self.P = P
            sb = lambda name, shape, dtype: es.enter_context(nc.sbuf_tensor(name, list(shape), dtype))
            self.x_sb = sb("x_sb", [128, NT, D], F32)
            self.hT = sb("hT", [128, 8, S], BF16)
            self.mixT = sb("mixT", [128, 8, S], BF16)
            self.QT = sb("QT", [128, 2, S], BF16)
            self.KT = sb("KT", [128, 2, S], BF16)
            self.VA = sb("VA", [128, 2, NT, 192], BF16)
            self.TAB = sb("TAB", [128, 4096], BF16)
            self.NPT = 3
            self.PT = sb("PT", [128, self.NPT, 512], BF16)
            self.WS = sb("WS", [128, 2, 8 * 384], BF16)
            self.GB = sb("GB", [128, 1024], F32)
            self.SCR = sb("SCR", [128, 4, 512], F32)
            self.SCRB = sb("SCRB", [128, 2, 512], BF16)
            self.HN = sb("HN", [128, 1, 1024], BF16)
            self.RB = sb("RB", [128, 1, 512], F32)
            self.ROPG = sb("ROPG", [128, 16, 128], BF16)
            self.ROPM = sb("ROPM", [128, 16, 64], BF16)
            self.ident = sb("identb", [128, 128], BF16)
            self.ident8 = sb("ident8b", [128, 128], BF16)
            self.onesb = sb("onesb", [128, 128], BF16)
            self.onesf = sb("onesf", [1, 128], F32)
            self.GPRE = sb("GPRE", [128, 2 * NL * 8], F32)
            self.GMLA = sb("GMLA", [128, 384], F32)
            self.GGQA = self.GMLA
            self.GSUB = sb("GSUB", [128, NL], F32)
            self.NLAM = sb("NLAM", [128, NL], F32)
            self.STAT = sb("STAT", [128, 64], F32)
            self.LT = sb("LT", [128, 3, 128], BF16)
            self.PS = es.enter_context(nc.psum_tensor("PS", [128, 8, 512], F32))

            B = {}
            self.B = B
            for k in ("QT0", "QT1", "KT0", "KT1", "TAB", "GB", "HN0", "HN1", "RB0", "RB1", "ROP", "CONST", "GPRE",
                      "GGQA", "GMLA", "GSUB", "NLAM", "STAT", "LT", "WS0", "WS1", "SCRB0", "SCRB1", "OUT", "DBG"):
                B[k] = P.buf(k)
            B["X"] = P.bufs(NT, "X")
            B["HT"] = P.bufs(4, "HT")
            B["MIX"] = [P.bufs(4, f"MIX{c}_") for c in range(8)]
            B["VA"] = [P.bufs(NT, f"VA{a}_") for a in range(2)]
            B["PT"] = P.bufs(self.NPT, "PT")
            B["PTX"] = P.bufs(6, "PTX")
            vflat = self.VA[:, 1, :, :].rearrange("p t n -> p (t n)")
            self.pt_base = [(self.PT[:, i, :], B["PT"][i]) for i in range(self.NPT)]
            self.pt_ext = self.pt_base + [(vflat[:, k * 512:(k + 1) * 512], B["PTX"][k]) for k in range(6)]
            self.pt_cur = self.pt_base
            B["SCR"] = P.bufs(4, "SCR")
            B["PSB"] = P.bufs(8, "PS")
            for b_ in B["PSB"]:
                b_.excl = True
            B["TABS"] = P.bufs(4, "TABS")
            B["SC"] = {k: [P.buf(f"sc_{k}{i}") for i in range(NL)] for k in sc}
            self.stat_pos = {}
            B["STATK"] = {k: P.buf("STAT_" + k) for k in self.STAT_KEYS}
            B["SCRH"] = [[P.buf(f"SCRH{s_}_{p}") for p in range(2)] for s_ in range(4)]
            B["LTP"] = P.bufs(2, "LTP")
            P.add("pool", lambda e: e.memset(self.STAT[:], 0.0), [], [B["STAT"]] + list(B["STATK"].values()))

            P.tag = "pro"
            self.prologue()
            for s in range(ns):
                self.load_x(s)
                for li, l in enumerate(self.layers):
                    self.layer(s, l, last=(li == len(self.layers) - 1))
            if self.dbg:
                self.dbg[0](self)
            P.add("sp", None, reads=[B["OUT"], B["DBG"]])
            P.emit(es)
        return nc

    def prologue(self):
        P, B, I, sc = self.P, self.B, self.I, self.sc
        nc = self.nc

        def cast(name, l, src, dst, rows, cols):
            nch = max(1, (rows + 8191) // 8192)
            step = (rows + nch - 1) // nch
            for r0 in range(0, rows, step):
                r1 = min(rows, r0 + step)
                self.dma("pool", dst[r0:r1, :], src[r0:r1, :], [], [B["SC"][name][l]], f"c_{name}{l}")

        def flat(ap, cols):
            names = " ".join(f"d{i}" for i in range(ap.ndim))
            a = ap.rearrange(f"{names} -> ({names})")
            return a.rearrange("(r c) -> r c", c=cols)

        P.add("pool", lambda e: e.memset(self.onesb[:], 1.0), [], [B["CONST"]])
        P.add("pool", lambda e: e.memset(self.onesf[:], 1.0), [], [B["CONST"]])
        P.add("pool", lambda e: e.memset(self.VA[:], 1.0), [], [b for a in B["VA"] for b in a])
        self.dma("pool", self.ident[:], I["ident"], [], [B["CONST"]], "const")
        self.dma("pool", self.ident8[:], I["ident8"], [], [B["CONST"]], "const")
        self.dma("pool", self.ROPG[:].rearrange("p t n -> p (t n)"), I["rope_g"], [], [B["ROP"]], "rop")
        self.dma("pool", self.ROPM[:].rearrange("p t n -> p (t n)"), I["rope_m"], [], [B["ROP"]], "rop")
        self.dma("sp", self.GPRE[:], I["g_pre"], [], [B["GPRE"]], "gpre")
        self.dma("sp", self.GSUB[:], I["g_sub"], [], [B["GSUB"]], "gsub")
        self.dma("sp", self.SCR[0:1, 0, :], I["lam_in"], [], [B["SCR"][0], B["SCR"][1]], "lamr")
        for l in self.layers:
            cast("w1", l, flat(I["w1"][l], 1232), flat(sc["w1"][l], 1232), 2048, 1232)
            cast("nab", l, flat(I["nab"][l], 2048), flat(sc["nab"][l], 2048), 512, 2048)
            if l == self.layers[0]:
                cast("alibi_g", 0, flat(I["alibi_g"], 1984), flat(sc["alibi_g"], 1984), 1024, 1984)
            cast("uq", l, flat(I["uq"][l], 2048), flat(sc["uq"][l], 2048), 48, 2048)
            cast("ukv", l, flat(I["ukv"][l], 2048), flat(sc["ukv"][l], 2048), 32, 2048)
            cast("wo", l, flat(I["wo"][l], 2048), flat(sc["wo"][l], 2048), 512, 2048)
            cast("gu", l, flat(I["gu"][l], 2048), flat(sc["gu"][l], 2048), NJ * 128, 2048)
            cast("wd", l, flat(I["wd"][l], 2048), flat(sc["wd"][l], 2048), 1408, 2048)
        L = self.SCR[0:1, 0:2, :].rearrange("p a n -> p (a n)")
        n = NL * 32
        self.tt("dve", L[:, 512:512 + n], L[:, 0:n], L[:, n:2 * n], ALU.mult, [B["SCR"][0], B["SCR"][1]], [B["SCR"][0], B["SCR"][1]])
        self.tt("dve", L[:, 0:n], L[:, 2 * n:3 * n], L[:, 3 * n:4 * n], ALU.mult, [B["SCR"][0], B["SCR"][1]], [B["SCR"][0], B["SCR"][1]])
        P.add("dve", lambda e: e.tensor_reduce(out=L[:, 256:256 + NL], in_=L[:, 512:512 + n].rearrange("p (l k) -> p l k", k=32),
                                               axis=AX.X, op=ALU.add), [B["SCR"][0], B["SCR"][1]], [B["SCR"][0], B["SCR"][1]])
        P.add("dve", lambda e: e.tensor_reduce(out=L[:, 264:264 + NL], in_=L[:, 0:n].rearrange("p (l k) -> p l k", k=32),
                                               axis=AX.X, op=ALU.add), [B["SCR"][0], B["SCR"][1]], [B["SCR"][0], B["SCR"][1]])
        self.act(L[:, 272:272 + NL], L[:, 256:256 + NL], AF.Exp, [B["SCR"][0], B["SCR"][1]], [B["SCR"][0], B["SCR"][1]])
        self.act(L[:, 280:280 + NL], L[:, 264:264 + NL], AF.Exp, [B["SCR"][0], B["SCR"][1]], [B["SCR"][0], B["SCR"][1]])
        self.tt("dve", L[:, 288:288 + NL], L[:, 280:280 + NL], L[:, 272:272 + NL], ALU.subtract, [B["SCR"][0], B["SCR"][1]], [B["SCR"][0], B["SCR"][1]])
        for l in range(NL):
            self.ts("dve", L[:, 296 + l:297 + l], L[:, 288 + l:289 + l], -lambda_init(l), None, ALU.add, None,
                    [B["SCR"][0], B["SCR"][1]], [B["SCR"][0], B["SCR"][1]])
        ps = self.PS[:, 7, 0:NL]
        self.mm(ps, self.onesf[0:1, :], L[0:1, 296:296 + NL], True, True, [B["SCR"][0], B["SCR"][1], B["CONST"]], [B["PSB"][7]])
        self.cp("dve", self.NLAM[:], ps, [B["PSB"][7]], [B["NLAM"]])
        for l in range(NL):
            self.ts("dve", self.GSUB[:, l:l + 1], self.GSUB[:, l:l + 1], 1.0 - lambda_init(l), None, ALU.mult, None,
                    [B["GSUB"]], [B["GSUB"]])

    def load_x(self, s):
        for t in range(NT):
            self.dma("sp", self.x_sb[:, t, :], self.I["x"][s, t * 128:(t + 1) * 128, :], [], [self.B["X"][t]], f"x{t}")

    def store_x(self, s, t):
        self.dma("sp", self.y[s, t * 128:(t + 1) * 128, :], self.x_sb[:, t, :], [self.B["X"][t]], [self.B["OUT"]], f"y{t}")

    STAT_KEYS = {"norm": (0, 4), "n0": (32, 4), "n1": (36, 4), "post": (4, 4), "p0": (8, 4), "p1": (12, 4), "g0": (16, 4), "g1": (20, 4), "m0": (24, 4), "m1": (28, 4)}

    def stat(self, n=1, key="norm"):
        base, size = self.STAT_KEYS[key]
        i = self.stat_pos.get(key, 0)
        if i + n > size:
            i = 0
        self.stat_pos[key] = i + n
        return self.STAT[:, base + i:base + i + n], self.B["STATK"][key]

    def pipeline(self, stages, n=NT):
        npair = n // 2
        ns = len(stages)
        for step in range(npair + ns - 1):
            for si in reversed(range(ns)):
                pi = step - si
                if not (0 <= pi < npair):
                    continue
                gens = [stages[si](2 * pi), stages[si](2 * pi + 1)]
                while gens:
                    for g_ in list(gens):
                        try:
                            next(g_)
                        except StopIteration:
                            gens.remove(g_)

    def fence_scr(self):
        B = self.B
        toks = list(B["SCR"]) + [B["SCRH"][s_][p] for s_ in range(4) for p in range(2)] + [B["HN0"], B["LTP"][0], B["LTP"][1]]
        self.P.add("dve", lambda e: e.tensor_copy(out=self.STAT[:, 62:63], in_=self.STAT[:, 63:64]), toks, toks)

    def norm_to_hT(self, gcol):
        for t in range(NT):
            self.norm_tile(t, gcol, 6 + (t % 2))

    def norm_tile(self, t, gcol, bank):
        for _ in self.norm_gen(t, gcol, bank, 0):
            pass

    def norm_gen(self, t, gcol, bank, par):
        P, B = self.P, self.B
        gain = self.GPRE[:, gcol * 8:(gcol + 1) * 8].unsqueeze(2).broadcast_to([128, 8, 128])
        psb = self.PS[:, bank, :].bitcast(BF16)
        if par == 0:
            hn, hb, key = self.HN[:, 0, :], B["HN0"], "norm"
        else:
            hn, hb, key = self.QT[:, 0, 0:1024], B["QT0"], "n1"
        ss, sk = self.stat(1, key)
        yield self.act(hn, self.x_sb[:, t, :], AF.Square, [B["X"][t]], [hb, sk], accum_out=ss)
        yield self.act(ss, ss, AF.Sqrt, [sk], [sk], scale=1.0 / D, bias=EPS)
        yield self.recip(ss, ss, [sk], [sk])
        yield self.ts("dve", hn, self.x_sb[:, t, :], ss, None, ALU.mult, None, [B["X"][t], sk], [hb])
        for c in range(8):
            yield self.tr(psb[:, c * 128:(c + 1) * 128], hn[:, c * 128:(c + 1) * 128], [hb, B["CONST"]], [B["PSB"][bank]])
        yield self.tt("dve", self.hT[:, :, t * 128:(t + 1) * 128], psb.rearrange("p (c n) -> p c n", c=8), gain, ALU.mult,
                      [B["PSB"][bank], B["GPRE"]], [B["HT"][t // 4]])

    @staticmethod
    def run_gens(gens):
        gens = list(gens)
        while gens:
            for g_ in list(gens):
                try:
                    next(g_)
                except StopIteration:
                    gens.remove(g_)

    def load_ws(self, slot, name, l, off, ncols, total):
        src = self.sc[name][l].rearrange("p (c n) -> p c n", c=8)[:, :, off:off + ncols]
        dst = self.WS[:, slot, 0:8 * ncols].rearrange("p (c n) -> p c n", c=8)
        self.dma("sp", dst, src, [self.B["SC"][name][l]], [self.B[f"WS{slot}"]], f"ws{slot}")
        return dst

    def proj_fm(self, w, c0, m, evac):
        B = self.B
        for tb in range(4):
            bank = 5 + (tb % 2)
            ps = self.PS[0:m, bank, :]
            for c in range(8):
                self.mm(ps, w[:, c, c0:c0 + m], self.hT[:, c, tb * 512:(tb + 1) * 512], c == 0, c == 7,
                        [B["HT"][tb], self.wsbuf], [B["PSB"][bank]])
            evac(tb, self.PS, bank)

    def proj_tm(self, w, c0, n, evac, tok0=0, ntiles=NT, banks=(5, 6)):
        B = self.B
        for t in range(ntiles):
            bank = banks[t % len(banks)]
            ps = self.PS[:, bank, 0:n]
            a = tok0 + t * 128
            tbs = sorted({a // 512, (a + 127) // 512})
            for c in range(8):
                self.mm(ps, self.hT[:, c, a:a + 128], w[:, c, c0:c0 + n], c == 0, c == 7,
                        [B["HT"][i] for i in tbs] + [self.wsbuf], [B["PSB"][bank]])
            evac(t, ps, bank)

    def evac_v(self, al):
        B = self.B

        def f(t, ps, bank):
            dst = self.VA[:, al, t, :].rearrange("p (a b) -> p a b", b=64)[:, 0:3:2, :]
            src = ps.rearrange("p (a b) -> p a b", b=64)
            self.cp("dve", dst, src, [B["PSB"][bank]], [B["VA"][al][t]])
        return f

    def normalize(self, bank, hh, dst, dst_bufs, dst_eng="dve"):
        B = self.B
        orows = slice(0, 64) if hh == 0 else slice(64, 128)
        srows = slice(64, 128) if hh == 0 else slice(0, 64)
        rb = self.RB[srows, 0, :]
        rbb = B["RB0"]
        self.act(rb, self.PS[srows, bank, :], AF.Ln, [B["PSB"][bank]], [rbb])
        self.act(rb, rb, AF.Exp, [rbb], [rbb], scale=-1.0)
        self.tt("dve", dst, self.PS[orows, bank, :], rb, ALU.mult, [B["PSB"][bank], rbb], dst_bufs)

    def attn_dense(self, qf, kf, vf, krows, scale, tabf, outf, reads):
        B = self.B
        sbanks = (0, 1, 2, 5, 6)
        obanks = (3, 4)
        look = 4
        pts = self.pt_cur
        cnt = getattr(self, "acnt", 0)
        for qb in range(4):
            ob = obanks[qb % 2]

            def qk(kt):
                sbk = sbanks[(cnt + kt) % len(sbanks)]
                self.mm(self.PS[:, sbk, :], kf(kt), qf(qb), True, True, reads, [B["PSB"][sbk]])

            def rest(kt):
                i = cnt + kt
                sbk = sbanks[i % len(sbanks)]
                pt, ptb = pts[i % len(pts)]
                self.act(pt, self.PS[:, sbk, :], AF.Exp, [B["PSB"][sbk]], [ptb], scale=scale)
                v, vb = vf(kt)
                self.mm(self.PS[:, ob, :], v, pt, kt == 0, kt == NT - 1, [ptb, vb], [B["PSB"][ob]])

            for kt in range(min(look, NT)):
                qk(kt)
            for kt in range(NT):
                if kt + look < NT:
                    qk(kt + look)
                rest(kt)
            cnt += NT
            outf(qb, ob)
        self.acnt = cnt

    def layer(self, s, l, last):
        P, B, sc = self.P, self.B, self.sc
        P.tag = "norm1"
        if not getattr(self, "norm_done", False):
            self.norm_to_hT(l)
        self.norm_done = False
        self.cur_last = last
        self.next_l = None if last else self.layers[self.layers.index(l) + 1]
        self.dma("sp", self.GB[:], self.I["g_post"][l, 0:1, :].broadcast_to([128, 1024]), [], [B["GB"]], "gb")
        slot = 0
        if 0 in self.mixers:
            for pr in range(2):
                P.tag = f"na{pr}"
                self.mixer_na(l, pr, slot)
                slot ^= 1
        else:
            self.zero_mix(0)
        if 1 in self.mixers:
            for pr in range(2):
                P.tag = f"diff{pr}"
                self.mixer_diff(l, pr, slot)
                slot ^= 1
        else:
            self.zero_mix(1)
        if 2 in self.mixers:
            self.dma("sp", self.GMLA[:, 0:256], self.I["g_gqa"][l:l + 1, :].broadcast_to([128, 256]), [], [B["GMLA"]], "gmla")
            for g in range(2):
                P.tag = f"gqa{g}"
                self.mixer_gqa(l, g, slot)
                slot ^= 1
        else:
            self.zero_mix(2)
        if 3 in self.mixers:
            self.dma("sp", self.GMLA[:], self.I["g_mla"][l:l + 1, :].broadcast_to([128, 384]), [], [B["GMLA"]], "gmla")
            for pr in range(2):
                P.tag = f"mla{pr}"
                self.mixer_mla(l, pr)
        else:
            self.zero_mix(3)
        P.tag = "wo"
        self.wo_residual(l)
        if self.do_ffn:
            P.tag = "ffn"
            self.ffn(s, l, last)
        elif last:
            for t in range(NT):
                self.store_x(s, t)

    def fence_ptx(self):
        self.P.add("pool", lambda e: e.memset(self.VA[0:1, 1, 15, 190:192], 1.0), [], self.B["VA"][1] + self.B["PTX"])
        self.pt_cur = self.pt_ext

    def zero_mix(self, m):
        for c in (2 * m, 2 * m + 1):
            self.P.add("pool", lambda e, c=c: e.memset(self.mixT[:, c, :], 0.0), [], self.B["MIX"][c])

    def mixer_na(self, l, pr, slot):
        P, B = self.P, self.B
        w = self.load_ws(slot, "w1", l, OFF_NA[pr], 384, 2464)
        self.wsbuf = B[f"WS{slot}"]
        self.dma("sp", self.TAB[:], self.sc["nab"][l, pr], [B["SC"]["nab"][l]], B["TABS"], "tab")
        chunk = pr

        def evq(dstT, dbuf):
            def f(tb, PS, bank):
                self.cp("dve", dstT[:, 0, tb * 512:(tb + 1) * 512], PS[:, bank, :], [B["PSB"][bank]], [dbuf])
            return f
        P.add("pool", lambda e: e.memset(self.KT[64:128, 0, :], 0.0), [], [B["KT0"]])
        P.add("pool", lambda e: e.memset(self.KT[0:64, 1, :], 0.0), [], [B["KT1"]])
        P.add("pool", lambda e: e.memset(self.VA[:, 1, :, 64:128], 1.0), [], B["VA"][1] + B["PTX"])
        self.pt_cur = self.pt_base

        def evk(tb, PS, bank):
            self.cp("dve", self.KT[0:64, 0, tb * 512:(tb + 1) * 512], PS[0:64, bank, :], [B["PSB"][bank]], [B["KT0"]])
            self.cp("dve", self.KT[64:128, 1, tb * 512:(tb + 1) * 512], PS[64:128, bank, :], [B["PSB"][bank]], [B["KT1"]])
        self.proj_fm(w, 0, 128, evq(self.QT, B["QT0"]))
        self.proj_fm(w, 128, 128, evk)
        self.proj_tm(w, 256, 128, self.evac_v(0))
        self.proj_tm(w, 256, 128, self.evac_v(1), tok0=64, ntiles=NT - 1)
        scale = 1.0 / 8.0
        for hh in range(2):
            hs = slice(hh * 64, hh * 64 + 64)
            for qb in range(4):
                ob = 3 + (qb % 2)
                for rp in range(4):
                    i = (hh * 4 + qb) * 4 + rp
                    sbk = i % 3
                    pi = i % self.NPT
                    rows = (qb * 8 + rp * 2, qb * 8 + rp * 2 + 1)
                    for ri, r in enumerate(rows):
                        cls = na_class(r)
                        tcol = (hh * 8 + cls) * 256
                        self.mm(self.PS[:, sbk, ri * 256:(ri + 1) * 256], self.ident8[:], self.TAB[:, tcol:tcol + 256],
                                ri == 0, False, B["TABS"] + [B["CONST"]], [B["PSB"][sbk]])
                    for ri, r in enumerate(rows):
                        rs = min(max(r - 4, 0), 24)
                        for t in range(4):
                            ks = (rs + 2 * t) * 64
                            self.mm(self.PS[:, sbk, ri * 256 + t * 64:ri * 256 + (t + 1) * 64], self.KT[:, hh, ks:ks + 128],
                                    self.QT[:, 0, r * 64:(r + 1) * 64], False, (ri == 1 and t == 3),
                                    [B["QT0"], B[f"KT{hh}"]], [B["PSB"][sbk]])
                    pt, ptb = self.pt_base[pi]
                    self.act(pt, self.PS[:, sbk, :], AF.Exp, [B["PSB"][sbk]], [ptb], scale=scale)
                    for ri, r in enumerate(rows):
                        rs = min(max(r - 4, 0), 24)
                        al = rs % 2
                        for t in range(4):
                            j = (rs + 2 * t) // 2
                            qc = (r % 8) * 64
                            self.mm(self.PS[:, ob, qc:qc + 64], self.VA[:, al, j, hh * 64:hh * 64 + 128],
                                    pt[:, ri * 256 + t * 64:ri * 256 + (t + 1) * 64], t == 0, t == 3,
                                    [B["PT"][pi], B["VA"][al][j]], [B["PSB"][ob]])
                orows = slice(0, 64) if hh == 0 else slice(64, 128)
                self.normalize(ob, hh, self.mixT[orows, chunk, qb * 512:(qb + 1) * 512], [B["MIX"][chunk][qb]])

    def mixer_diff(self, l, pr, slot):
        P, B = self.P, self.B
        w = self.load_ws(slot, "w1", l, OFF_DF[pr], 384, 2464)
        self.wsbuf = B[f"WS{slot}"]
        chunk = 2 + pr
        self.fence_ptx()
        for m in range(2):
            P.add("pool", lambda e, m=m: e.memset(self.KT[:, m, :], 0.0), [], [B[f"KT{m}"]])
        P.add("pool", lambda e: e.memset(self.QT[64:128, 0, :], 0.0), [], [B["QT0"]])
        P.add("pool", lambda e: e.memset(self.QT[0:64, 1, :], 0.0), [], [B["QT1"]])

        def evq(tb, PS, bank):
            self.cp("dve", self.QT[0:64, 0, tb * 512:(tb + 1) * 512], PS[0:64, bank, :], [B["PSB"][bank]], [B["QT0"]])
            self.cp("act", self.QT[64:128, 1, tb * 512:(tb + 1) * 512], PS[64:128, bank, :], [B["PSB"][bank]], [B["QT1"]])

        def evk(tb, PS, bank):
            for i in range(4):
                rs_ = slice(i * 32, (i + 1) * 32)
                m = i % 2
                self.cp("dve" if i < 2 else "act", self.KT[rs_, m, tb * 512:(tb + 1) * 512], PS[rs_, bank, :],
                        [B["PSB"][bank]], [B[f"KT{m}"]])
        self.proj_fm(w, 0, 128, evq)
        self.proj_fm(w, 128, 128, evk)
        self.proj_tm(w, 256, 128, self.evac_v(0))
        scale = 32.0 ** -0.5
        for hh in range(2):
            h = 2 * pr + hh
            self.dma("sp", self.TAB[:, 0:GW], self.sc["alibi_g"][h], [B["SC"]["alibi_g"][0]], B["TABS"], "tab")
            self.attn_diff_head(l, hh, scale, chunk)

    def attn_diff_head(self, l, hh, scale, chunk):
        B = self.B
        orows = slice(0, 64) if hh == 0 else slice(64, 128)
        sbanks = (0, 1, 2, 5, 6)
        look = 4
        pts = self.pt_cur
        cnt = getattr(self, "acnt", 0)
        for qb in range(4):
            obs = (3, 4)
            seq = [(m, kt) for m in range(2) for kt in range(NT)]

            def qk(idx):
                m, kt = seq[idx]
                sbk = sbanks[(cnt + idx) % len(sbanks)]
                self.mm(self.PS[:, sbk, :], self.KT[:, m, kt * 128:(kt + 1) * 128], self.QT[:, hh, qb * 512:(qb + 1) * 512],
                        True, True, [B[f"QT{hh}"], B[f"KT{m}"]], [B["PSB"][sbk]])

            def rest(idx):
                m, kt = seq[idx]
                i = cnt + idx
                sbk = sbanks[i % len(sbanks)]
                pt, ptb = pts[i % len(pts)]
                self.act(pt, self.PS[:, sbk, :], AF.Exp, [B["PSB"][sbk]], [ptb], scale=scale)
                c0 = qb * 512 - kt * 128 + 1920
                eng = "pool" if (i % 8 == 0) else "dve"
                self.tt(eng, pt, pt, self.TAB[:, c0:c0 + 512], ALU.mult, [ptb] + B["TABS"], [ptb])
                ob = obs[m]
                self.mm(self.PS[:, ob, :], self.VA[:, 0, kt, hh * 64:hh * 64 + 128], pt, kt == 0, kt == NT - 1,
                        [ptb, B["VA"][0][kt]], [B["PSB"][ob]])

            for idx in range(look):
                qk(idx)
            for idx in range(len(seq)):
                if idx + look < len(seq):
                    qk(idx + look)
                rest(idx)
            cnt += len(seq)
            for m in range(2):
                self.normalize(obs[m], hh, self.SCR[orows, m, :], [B["SCR"][m]])
            self.diff_finish(l, hh, qb, chunk)
        self.acnt = cnt

    def diff_finish(self, l, hh, qb, chunk):
        B = self.B
        orows = slice(0, 64) if hh == 0 else slice(64, 128)
        d1 = self.SCR[orows, 0, :]
        d2 = self.SCR[orows, 1, :]
        dd = self.SCR[orows, 2, :]
        rs = self.SCR[orows, 3, :]
        sq = self.SCRB[orows, 0, :]
        self.stt(dd, d2, self.NLAM[orows, l:l + 1], d1, ALU.mult, ALU.add, [B["SCR"][0], B["SCR"][1], B["NLAM"]], [B["SCR"][2]])
        self.tt("dve", sq, dd, dd, ALU.mult, [B["SCR"][2]], [B["SCRB0"]])
        bank = 7
        self.mm(self.PS[:, bank, :], self.onesb[orows, :], sq, True, True, [B["SCRB0"], B["CONST"]], [B["PSB"][bank]])
        self.act(rs, self.PS[orows, bank, :], AF.Ln, [B["PSB"][bank]], [B["SCR"][3]], scale=1.0 / 64.0, bias=EPS)
        self.act(rs, rs, AF.Exp, [B["SCR"][3]], [B["SCR"][3]], scale=-0.5)
        self.stt(self.mixT[orows, chunk, qb * 512:(qb + 1) * 512], dd, self.GSUB[orows, l:l + 1], rs, ALU.mult, ALU.mult,
                 [B["SCR"][2], B["SCR"][3], B["GSUB"]], [B["MIX"][chunk][qb]])

    def mixer_gqa(self, l, g, slot):
        P, B = self.P, self.B
        w = self.load_ws(slot, "w1", l, OFF_GQ[g], 256, 2464)
        wsb = B[f"WS{slot}"]
        chunk = 4 + g
        P.add("pool", lambda e: e.memset(self.KT[64:128, 0, :], 0.0), [], [B["KT0"]])
        P.add("pool", lambda e: e.memset(self.KT[0:64, 1, :], 0.0), [], [B["KT1"]])
        self.fence_ptx()
        self.fence_scr()
        st_ = {}

        def bufs(t):
            p = t % 2
            H = B["SCRH"]
            return dict(p=p, bank=p, raw=self.SCR[:, 0, p * 256:p * 256 + 192], rawb=H[0][p],
                        y=self.SCR[:, 1, p * 256:(p + 1) * 256], yb=H[1][p],
                        a=self.SCR[:, 2, p * 256:(p + 1) * 256], ab=H[2][p],
                        bs=self.SCR[:, 3, p * 256:(p + 1) * 256], bsb=H[3][p],
                        qk=self.SCRB[:, p, 0:256], qkb=B[f"SCRB{p}"])

        def stage_a(t):
            u = bufs(t)
            pb = B["PSB"][u["bank"]]
            ps = self.PS[:, u["bank"], 0:256]
            for c in range(8):
                yield self.mm(ps, self.hT[:, c, t * 128:(t + 1) * 128], w[:, c, 0:256], c == 0, c == 7, [B["HT"][t // 4], wsb], [pb])
            dst = self.VA[:, 0, t, :].rearrange("p (a b) -> p a b", b=64)[:, 0:3:2, :]
            yield self.cp("dve", dst, ps[:, 192:256].unsqueeze(1).broadcast_to([128, 2, 64]), [pb], [B["VA"][0][t]])
            yield self.cp("act", u["raw"], ps[:, 0:192], [pb], [u["rawb"]])
            sq = u["y"][:, 0:192]
            yield self.tt("dve", sq, u["raw"], u["raw"], ALU.mult, [u["rawb"]], [u["yb"]])
            ss, sk = self.stat(3, f"g{u['p']}")
            st_[t] = (ss, sk)
            yield P.add("dve", lambda e: e.tensor_reduce(out=ss, in_=sq.rearrange("p (h d) -> p h d", d=64), axis=AX.X, op=ALU.add),
                  [u["yb"]], [sk])
            yield self.act(ss, ss, AF.Sqrt, [sk], [sk], scale=1.0 / 64.0, bias=EPS)

        def stage_b(t):
            u = bufs(t)
            ss, sk = st_.pop(t)
            raw, y = u["raw"], u["y"]
            yield self.recip(ss, ss, [sk], [sk])
            for h3 in range(3):
                yield self.stt(y[:, h3 * 64:(h3 + 1) * 64], raw[:, h3 * 64:(h3 + 1) * 64], ss[:, h3:h3 + 1],
                         self.GGQA[:, h3 * 64:(h3 + 1) * 64], ALU.mult, ALU.mult, [u["rawb"], sk, B["GMLA"]], [u["yb"]])
            yield self.stt(y[:, 192:256], raw[:, 128:192], ss[:, 2:3], self.GGQA[:, 192:256], ALU.mult, ALU.mult,
                     [u["rawb"], sk, B["GMLA"]], [u["yb"]])
            C = self.ROPG[:, t, 0:64].unsqueeze(1).broadcast_to([128, 4, 64])
            Sg = self.ROPG[:, t, 64:128]
            y3 = y.rearrange("p (h d) -> p h d", d=64)
            a3 = u["a"].rearrange("p (h d) -> p h d", d=64)
            b3 = u["bs"].rearrange("p (h d) -> p h d", d=64)
            yield self.tt("dve", a3, y3, C, ALU.mult, [u["yb"], B["ROP"]], [u["ab"]])
            y5 = y.rearrange("p (h a r d) -> p h a r d", a=2, r=2, d=16)
            b5 = u["bs"].rearrange("p (h a r d) -> p h a r d", a=2, r=2, d=16)
            s5 = Sg.rearrange("p (a r d) -> p a r d", a=2, r=2, d=16)
            for r in range(2):
                yield self.tt("dve", b5[:, :, :, r, :], y5[:, :, :, 1 - r, :], s5[:, :, r, :].unsqueeze(1).broadcast_to([128, 4, 2, 16]),
                        ALU.mult, [u["yb"], B["ROP"]], [u["bsb"]])
            qk, qkb = u["qk"], u["qkb"]
            yield self.tt("dve", qk.rearrange("p (h d) -> p h d", d=64), a3, b3, ALU.add, [u["ab"], u["bsb"]], [qkb])
            tbank = 6 + ((t // 4) % 2)
            psb = self.PS[:, tbank, :].bitcast(BF16)
            tq = t % 4
            yield self.tr(psb[:, tq * 128:(tq + 1) * 128], qk[:, 0:128], [qkb, B["CONST"]], [B["PSB"][tbank]])
            yield self.tr(psb[:, 512 + tq * 128:512 + (tq + 1) * 128], qk[:, 128:256], [qkb, B["CONST"]], [B["PSB"][tbank]])
            if tq == 3:
                tb = t // 4
                yield self.cp("act", self.QT[:, 0, tb * 512:(tb + 1) * 512], psb[:, 0:512], [B["PSB"][tbank]], [B["QT0"]])
                yield self.cp("act", self.KT[0:64, 0, tb * 512:(tb + 1) * 512], psb[0:64, 512:1024], [B["PSB"][tbank]], [B["KT0"]])
                yield self.cp("dve", self.KT[64:128, 1, tb * 512:(tb + 1) * 512], psb[64:128, 512:1024], [B["PSB"][tbank]], [B["KT1"]])

        self.pipeline([stage_a, stage_b])
        self.fence_scr()
        P.tag = P.tag[:4] + "a"
        for hh in range(2):
            orows = slice(hh * 64, hh * 64 + 64)

            def outf(qb, bank, hh=hh, orows=orows):
                self.normalize(bank, hh, self.mixT[orows, chunk, qb * 512:(qb + 1) * 512], [B["MIX"][chunk][qb]])

            self.attn_dense(lambda qb: self.QT[:, 0, qb * 512:(qb + 1) * 512],
                            lambda kt, hh=hh: self.KT[:, hh, kt * 128:(kt + 1) * 128],
                            lambda kt, hh=hh: (self.VA[:, 0, kt, hh * 64:hh * 64 + 128], B["VA"][0][kt]),
                            128, 1.0 / 8.0, None, outf, [B["QT0"], B[f"KT{hh}"]])

    def mixer_mla(self, l, pr):
        P, B = self.P, self.B
        wA = self.load_ws(0, "w1", l, OFF_MLA, 384, 2464)
        ws1 = self.WS[:, 1, :]
        self.dma("sp", ws1[:, 0:256].rearrange("p (c n) -> p c n", c=8),
                 self.sc["w1"][l].rearrange("p (c n) -> p c n", c=8)[:, :, OFF_KR:OFF_KR + 32], [B["SC"]["w1"][l]], [B["WS1"]], "ws1")
        self.dma("sp", ws1[:, 256:1024], self.sc["uq"][l], [B["SC"]["uq"][l]], [B["WS1"]], "ws1")
        self.dma("sp", ws1[:, 1024:1536], self.sc["ukv"][l], [B["SC"]["ukv"][l]], [B["WS1"]], "ws1")
        wkr = ws1[:, 0:256].rearrange("p (c n) -> p c n", c=8)
        wuq = ws1[:, 256:1024].rearrange("p (j n) -> p j n", j=2)
        wukv = ws1[:, 1024:1536]
        chunk = 6 + pr
        scale = 96.0 ** -0.5
        self.fence_ptx()
        self.fence_scr()
        st_ = {}
        H = B["SCRH"]

        def bufs(t):
            p = t % 2
            return dict(p=p, lbank=p, tbank=6 + p, qbank=(2, 5)[p],
                        raw=self.SCR[:, p, 0:416], rawb=B["SCR"][p],
                        lat=self.SCRB[:, p, 0:384], latb=B[f"SCRB{p}"],
                        lt=self.HN[:, 0, p * 384:(p + 1) * 384].rearrange("p (j n) -> p j n", j=3), ltb=B["LTP"][p],
                        qk=self.PT[:, p, 0:384], qkb=B["PT"][p],
                        rin=self.SCR[:, 2, p * 256:p * 256 + 96], rinb=H[2][p],
                        ra=self.SCR[:, 3, p * 256:p * 256 + 96], rb=self.SCR[:, 3, p * 256 + 96:p * 256 + 192], rab=H[3][p])

        def stage_a(t):
            u = bufs(t)
            pb = B["PSB"][u["lbank"]]
            ps = self.PS[:, u["lbank"], :]
            tb = t // 4
            for c in range(8):
                yield self.mm(ps[:, 0:384], self.hT[:, c, t * 128:(t + 1) * 128], wA[:, c, :], c == 0, c == 7, [B["HT"][tb], B["WS0"]], [pb])
            for c in range(8):
                yield self.mm(ps[:, 384:416], self.hT[:, c, t * 128:(t + 1) * 128], wkr[:, c, :], c == 0, c == 7, [B["HT"][tb], B["WS1"]], [pb])
            raw = u["raw"]
            yield self.cp("act", raw, ps[:, 0:416], [pb], [u["rawb"]])
            ss, sk = self.stat(2, f"m{u['p']}")
            st_[t] = (ss, sk)
            junk = u["lat"]
            yield self.act(junk[:, 0:256], raw[:, 0:256], AF.Square, [u["rawb"]], [u["latb"], sk], accum_out=ss[:, 0:1])
            yield self.act(junk[:, 256:384], raw[:, 256:384], AF.Square, [u["rawb"]], [u["latb"], sk], accum_out=ss[:, 1:2])
            yield self.act(ss[:, 0:1], ss[:, 0:1], AF.Sqrt, [sk], [sk], scale=1.0 / 256.0, bias=EPS)
            yield self.act(ss[:, 1:2], ss[:, 1:2], AF.Sqrt, [sk], [sk], scale=1.0 / 128.0, bias=EPS)
            yield self.recip(ss, ss, [sk], [sk])
            lat = u["lat"]
            yield self.stt(lat[:, 0:256], raw[:, 0:256], ss[:, 0:1], self.GMLA[:, 0:256], ALU.mult, ALU.mult,
                     [u["rawb"], sk, B["GMLA"]], [u["latb"]])
            yield self.stt(lat[:, 256:384], raw[:, 256:384], ss[:, 1:2], self.GMLA[:, 256:384], ALU.mult, ALU.mult,
                     [u["rawb"], sk, B["GMLA"]], [u["latb"]])

        def stage_b(t):
            u = bufs(t)
            st_.pop(t)
            lat, raw = u["lat"], u["raw"]
            tbk = B["PSB"][u["tbank"]]
            psb = self.PS[:, u["tbank"], :].bitcast(BF16)
            for j in range(3):
                yield self.tr(psb[:, j * 128:(j + 1) * 128], lat[:, j * 128:(j + 1) * 128], [u["latb"], B["CONST"]], [tbk])
            yield self.cp("dve", u["lt"].rearrange("p j n -> p (j n)"), psb[:, 0:384], [tbk], [u["ltb"]])
            pq = self.PS[:, u["qbank"], :]
            pqb = B["PSB"][u["qbank"]]
            for j in range(2):
                yield self.mm(pq[:, 0:192], u["lt"][:, j, :], wuq[:, j, pr * 192:(pr + 1) * 192], j == 0, j == 1, [u["ltb"], B["WS1"]], [pqb])
            yield self.mm(pq[:, 192:448], u["lt"][:, 2, :], wukv[:, pr * 256:(pr + 1) * 256], True, True, [u["ltb"], B["WS1"]], [pqb])
            dst = self.VA[:, 0, t, :].rearrange("p (a b) -> p a b", b=64)[:, 0:3:2, :]
            srcv = pq[:, 192:448].rearrange("p (h a b) -> p h a b", h=2, a=2)[:, :, 1, :]
            yield self.cp("dve", dst, srcv, [pqb], [B["VA"][0][t]])
            qk4 = u["qk"].rearrange("p (x d) -> p x d", d=96)
            yield self.cp("act", qk4[:, 0:2, 0:64], pq[:, 0:192].rearrange("p (h d) -> p h d", d=96)[:, :, 0:64], [pqb], [u["qkb"]])
            yield self.cp("act", qk4[:, 2:4, 0:64], pq[:, 192:448].rearrange("p (h a b) -> p h a b", h=2, a=2)[:, :, 0, :], [pqb], [u["qkb"]])
            rin = u["rin"].rearrange("p (x d) -> p x d", d=32)
            yield self.cp("dve", rin[:, 0:2, :], pq[:, 0:192].rearrange("p (h d) -> p h d", d=96)[:, :, 64:96], [pqb], [u["rinb"]])
            yield self.cp("dve", rin[:, 2, :], raw[:, 384:416], [u["rawb"]], [u["rinb"]])

        def stage_c(t):
            u = bufs(t)
            tbk = B["PSB"][u["tbank"]]
            psb = self.PS[:, u["tbank"], :].bitcast(BF16)
            qk4 = u["qk"].rearrange("p (x d) -> p x d", d=96)
            rin = u["rin"].rearrange("p (x d) -> p x d", d=32)
            rin4 = u["rin"].rearrange("p (x r d) -> p x r d", x=3, r=2)
            C = self.ROPM[:, t, 0:32].unsqueeze(1).broadcast_to([128, 3, 32])
            Sm = self.ROPM[:, t, 32:64].rearrange("p (r d) -> p r d", r=2)
            ra = u["ra"].rearrange("p (x d) -> p x d", d=32)
            rbm = u["rb"].rearrange("p (x r d) -> p x r d", x=3, r=2)
            rb3 = u["rb"].rearrange("p (x d) -> p x d", d=32)
            yield self.tt("dve", ra, rin, C, ALU.mult, [u["rinb"], B["ROP"]], [u["rab"]])
            for r in range(2):
                yield self.tt("dve", rbm[:, :, r, :], rin4[:, :, 1 - r, :], Sm[:, r, :].unsqueeze(1).broadcast_to([128, 3, 16]),
                        ALU.mult, [u["rinb"], B["ROP"]], [u["rab"]])
            yield self.tt("dve", qk4[:, 0:2, 64:96], ra[:, 0:2, :], rb3[:, 0:2, :], ALU.add, [u["rab"]], [u["qkb"]])
            for hk in range(2):
                yield self.tt("dve", qk4[:, 2 + hk, 64:96], ra[:, 2, :], rb3[:, 2, :], ALU.add, [u["rab"]], [u["qkb"]])
            for x4 in range(4):
                yield self.tr(psb[0:96, 512 + x4 * 128:512 + (x4 + 1) * 128], qk4[:, x4, :], [u["qkb"], B["CONST"]], [tbk])
            for hh in range(2):
                yield self.cp("dve", self.QT[0:96, hh, t * 128:(t + 1) * 128], psb[0:96, 512 + hh * 128:512 + (hh + 1) * 128],
                        [tbk], [B[f"QT{hh}"]])
                yield self.cp("act", self.KT[0:96, hh, t * 128:(t + 1) * 128], psb[0:96, 512 + (2 + hh) * 128:512 + (3 + hh) * 128],
                        [tbk], [B[f"KT{hh}"]])

        self.pipeline([stage_a, stage_b, stage_c])
        self.fence_scr()
        P.tag = P.tag[:4] + "a"
        for hh in range(2):
            orows = slice(hh * 64, hh * 64 + 64)

            def outf(qb, bank, hh=hh, orows=orows):
                self.normalize(bank, hh, self.mixT[orows, chunk, qb * 512:(qb + 1) * 512], [B["MIX"][chunk][qb]])

            self.attn_dense(lambda qb, hh=hh: self.QT[0:96, hh, qb * 512:(qb + 1) * 512],
                            lambda kt, hh=hh: self.KT[0:96, hh, kt * 128:(kt + 1) * 128],
                            lambda kt, hh=hh: (self.VA[:, 0, kt, hh * 64:hh * 64 + 128], B["VA"][0][kt]),
                            96, scale, None, outf, [B[f"QT{hh}"], B[f"KT{hh}"]])

    def post_residual(self, t, b0):
        for _ in self.post_gen(t, b0, 0):
            pass

    def post_gen(self, t, b0, par):
        B = self.B
        fps = self.PS[:, b0:b0 + 2, :].rearrange("p a n -> p (a n)")
        pbs = [B["PSB"][b0], B["PSB"][b0 + 1]]
        if par == 0:
            junk, jb = self.SCRB[:].rearrange("p a n -> p (a n)"), [B["SCRB0"], B["SCRB1"]]
            tmp, tb_ = self.SCR[:, 0:2, :].rearrange("p a n -> p (a n)"), [B["SCR"][0], B["SCR"][1]]
            key = "p0"
        else:
            junk, jb = self.PT[:, 0:2, :].rearrange("p a n -> p (a n)"), [B["PT"][0], B["PT"][1]]
            tmp, tb_ = self.SCR[:, 2:4, :].rearrange("p a n -> p (a n)"), [B["SCR"][2], B["SCR"][3]]
            key = "p1"
        ss, sk = self.stat(1, key)
        yield self.act(junk, fps, AF.Square, pbs, jb + [sk], accum_out=ss)
        yield self.tt("dve", tmp, fps, self.GB[:], ALU.mult, pbs + [B["GB"]], tb_)
        yield self.act(ss, ss, AF.Sqrt, [sk], [sk], scale=1.0 / D, bias=EPS)
        yield self.recip(ss, ss, [sk], [sk])
        yield self.stt(self.x_sb[:, t, :], tmp, ss, self.x_sb[:, t, :], ALU.mult, ALU.add, tb_ + [sk, B["X"][t]], [B["X"][t]])

    def wo_residual(self, l):
        P, B = self.P, self.B
        wsrc = self.sc["wo"][l].rearrange("p (c n) -> p c n", c=8)
        w0 = self.WS[:, 0, :].rearrange("p (c n) -> p c n", c=8)
        w1 = self.WS[:, 1, :].rearrange("p (c n) -> p c n", c=8)
        w2 = self.TAB[:, 0:2048].rearrange("p (c n) -> p c n", c=8)
        self.dma("sp", w0, wsrc[:, :, 0:384], [B["SC"]["wo"][l]], [B["WS0"]], "ws0")
        self.dma("sp", w1, wsrc[:, :, 384:768], [B["SC"]["wo"][l]], [B["WS1"]], "ws1")
        self.dma("sp", w2, wsrc[:, :, 768:1024], [B["SC"]["wo"][l]], B["TABS"][0:2], "tab")
        segs = ((0, 384, w0, 0, B["WS0"]), (384, 128, w1, 0, B["WS1"]), (512, 256, w1, 128, B["WS1"]), (768, 256, w2, 0, B["TABS"][0]))
        self.fence_scr()
        prev = None
        for pi in range(NT // 2):
            tiles = (2 * pi, 2 * pi + 1)
            for t in tiles:
                b0 = 0 if t % 2 == 0 else 2
                tb = t // 4
                for (o0, n, wt, wo_, wb) in segs:
                    bank = b0 + o0 // 512
                    col = o0 % 512
                    for c in range(8):
                        self.mm(self.PS[:, bank, col:col + n], self.mixT[:, c, t * 128:(t + 1) * 128], wt[:, c, wo_:wo_ + n],
                                c == 0, c == 7, [B["MIX"][c][tb], wb, B["TABS"][1]], [B["PSB"][bank]])
            gens = [self.post_gen(t, 0 if t % 2 == 0 else 2, t % 2) for t in tiles]
            if self.do_ffn and prev is not None:
                gens += [self.norm_gen(t, NL + l, 6 + (t % 2), t % 2) for t in prev]
            self.run_gens(gens)
            prev = tiles
        if self.do_ffn:
            self.run_gens([self.norm_gen(t, NL + l, 6 + (t % 2), t % 2) for t in prev])
        self.fence_scr()

    def ffn(self, s, l, last):
        P, B = self.P, self.B
        self.dma("sp", self.GB[:], self.I["g_post"][l, 1:2, :].broadcast_to([128, 1024]), [], [B["GB"]], "gb")
        actT = self.mixT[:].rearrange("p c n -> p (c n)")[:, 0:NJ * 512].rearrange("p (j n) -> p j n", j=NJ)
        abufs = [b for c in range(8) for b in B["MIX"][c]]
        gcnt = 0
        for tb in range(4):
            for j in range(NJ):
                slot = gcnt % 2
                gcnt += 1
                wv = self.WS[:, slot, 0:2048].rearrange("p (c n) -> p c n", c=8)
                self.dma("sp", self.WS[:, slot, 0:2048], self.sc["gu"][l, j], [B["SC"]["gu"][l]], [B[f"WS{slot}"]], f"ws{slot}")
                gb = 4 + 2 * (j % 2)
                ub = gb + 1
                for c in range(8):
                    self.mm(self.PS[:, gb, :], wv[:, c, 0:128], self.hT[:, c, tb * 512:(tb + 1) * 512], c == 0, c == 7,
                            [B["HT"][tb], B[f"WS{slot}"]], [B["PSB"][gb]])
                for c in range(8):
                    self.mm(self.PS[:, ub, :], wv[:, c, 128:256], self.hT[:, c, tb * 512:(tb + 1) * 512], c == 0, c == 7,
                            [B["HT"][tb], B[f"WS{slot}"]], [B["PSB"][ub]])
                si = 2 + (j % 2)
                sg = self.SCR[:, si, :]
                self.act(sg, self.PS[:, gb, :], AF.Silu, [B["PSB"][gb]], [B["SCR"][si]])
                self.tt("dve", actT[:, j, :], self.PS[:, ub, :], sg, ALU.mult, [B["PSB"][ub], B["SCR"][si]], [self.abuf(j)])
            for j in range(NJ):
                ts_ = j % 4
                wdt = self.TAB[:, ts_ * 1024:(ts_ + 1) * 1024]
                self.dma("sp", wdt, self.sc["wd"][l, j], [B["SC"]["wd"][l]], [B["TABS"][ts_]], f"tabs{ts_}")
                for tt_ in range(4):
                    for ch in range(2):
                        bank = tt_ * 2 + ch
                        self.mm(self.PS[:, bank, :], actT[:, j, tt_ * 128:(tt_ + 1) * 128], wdt[:, ch * 512:(ch + 1) * 512],
                                j == 0, j == NJ - 1, [self.abuf(j), B["TABS"][ts_]], [B["PSB"][bank]])
            if tb == 0:
                self.fence_scr()
            prev = None
            for half in range(2):
                tl = (tb * 4 + 2 * half, tb * 4 + 2 * half + 1)
                gens = [self.post_gen(t, (t % 4) * 2, t % 2) for t in tl]
                if prev is not None and not last and self.next_l is not None:
                    gens += [self.norm_gen(t, self.next_l, t % 2, t % 2) for t in prev]
                self.run_gens(gens)
                if last:
                    for t in tl:
                        self.store_x(s, t)
                prev = tl
            if not last and self.next_l is not None:
                self.run_gens([self.norm_gen(t, self.next_l, t % 2, t % 2) for t in prev])
            if tb == 3:
                self.fence_scr()
                if not last and self.next_l is not None:
                    self.norm_done = True

    def abuf(self, j):
        return self.B["MIX"][j // 4][j % 4]


_CACHE = {}


def kernel(**inputs):
    ncores = 8
    x = np.ascontiguousarray(np.asarray(inputs["x"], np.float32))
    nb = x.shape[0]
    per = nb // ncores
    shared = prep_shared(inputs)
    key = ("full", per)
    if key not in _CACHE:
        _CACHE[key] = Builder(per, range(NL)).build()
    nc = _CACHE[key]
    in_maps = []
    for c in range(ncores):
        m = dict(shared)
        m["x"] = x[c * per:(c + 1) * per]
        m["w1"] = shared["w1"].reshape(NL, 128, 8 * 2464)
        m["wo"] = shared["wo"].reshape(NL, 128, 8 * 1024)
        m["uq"] = shared["uq"].reshape(NL, 128, 768)
        m["g_pre"] = shared["g_pre"].reshape(128, 2 * NL * 8)
        m["rope_g"] = shared["rope_g"].reshape(128, 16 * 128)
        m["rope_m"] = shared["rope_m"].reshape(128, 16 * 64)
        in_maps.append(m)
    res = run_bass_kernel_spmd(nc, in_maps, core_ids=list(range(ncores)))
    out = np.concatenate([np.asarray(r["y"]) for r in res.results], axis=0)
    return out.astype(np.float32)
```

```python
import math
import contextlib
import numpy as np
import concourse.bass as bass
import concourse.mybir as mybir
from concourse.bass_utils import run_bass_kernel_spmd

F32 = mybir.dt.float32
BF16 = mybir.dt.bfloat16
AF = mybir.ActivationFunctionType
ALU = mybir.AluOpType
AX = mybir.AxisListType

S = 2048
D = 1024
NT = 16
DFF = 2816
NJ = 22
NL = 4
EPS = 1e-6
NEG = -30000.0
GW = 3968


class Buf:
    __slots__ = ("name", "w", "r", "excl")

    def __init__(self, name, excl=False):
        self.name = name
        self.w = {}
        self.r = {}
        self.excl = excl


class Op:
    __slots__ = ("eng", "fn", "deps", "sig", "sigval", "dma", "dmaval", "tag")


class Prog:
    ENGS = ("pe", "act", "dve", "pool", "sp")

    def __init__(self, nc):
        self.nc = nc
        self.ops = {e: [] for e in self.ENGS}
        self.dma_cnt = {}
        self.n = 0
        self.tag = ""
        self.names = {}

    def buf(self, name=None):
        self.n += 1
        return Buf(name or f"b{self.n}")

    def bufs(self, n, name="b"):
        return [self.buf(f"{name}{i}") for i in range(n)]

    def add(self, eng, fn, reads=(), writes=(), dma=None):
        op = Op()
        op.eng = eng
        op.fn = fn
        op.sig = False
        op.sigval = 0
        op.dma = dma
        op.dmaval = 0
        op.tag = self.tag
        if dma is not None:
            self.dma_cnt[dma] = self.dma_cnt.get(dma, 0) + 16
            op.dmaval = self.dma_cnt[dma]
        deps = {}
        for b in reads:
            for d in b.w.values():
                deps[d] = True
            if b.excl:
                for k2, d in b.r.items():
                    if k2 != eng:
                        deps.setdefault(d, False)
        for b in writes:
            for d in b.w.values():
                deps.setdefault(d, False)
            for d in b.r.values():
                deps.setdefault(d, False)
        op.deps = deps
        key = ("dma", dma) if dma is not None else eng
        for b in writes:
            b.w[key] = op
        for b in reads:
            b.r[key] = op
        self.ops[eng].append(op)
        return op

    @staticmethod
    def _skip(d, raw, ename):
        if d.dma is not None:
            return False
        if d.eng != ename:
            return False
        return ename == "pe"

    def emit(self, es):
        nc = self.nc
        for e in self.ENGS:
            for op in self.ops[e]:
                for d, raw in op.deps.items():
                    if d.dma is None and not self._skip(d, raw, e):
                        d.sig = True
        for e in self.ENGS:
            c = 0
            for op in self.ops[e]:
                if op.sig and op.dma is None:
                    c += 1
                    op.sigval = c
        engsem = {e: es.enter_context(nc.semaphore(f"sem_{e}")) for e in self.ENGS}
        dmasem = {k: es.enter_context(nc.semaphore(f"dsem_{k}")) for k in self.dma_cnt}
        block = es.enter_context(nc.Block())

        def run(ename):
            def body(eng):
                waited = {}
                for op in self.ops[ename]:
                    need = {}
                    for d, raw in op.deps.items():
                        if d.dma is not None:
                            key = ("d", d.dma)
                            v = d.dmaval
                        else:
                            if self._skip(d, raw, ename):
                                continue
                            key = ("e", d.eng)
                            v = d.sigval
                        if need.get(key, 0) < v:
                            need[key] = v
                    for key, v in need.items():
                        if waited.get(key, 0) < v:
                            sem = dmasem[key[1]] if key[0] == "d" else engsem[key[1]]
                            eng.wait_ge(sem, v)
                            waited[key] = v
                    if op.fn is None:
                        continue
                    inst = op.fn(eng)
                    try:
                        self.names[inst.ins.name] = op.tag
                    except Exception:
                        pass
                    if op.dma is not None:
                        inst.then_inc(dmasem[op.dma], 16)
                    elif op.sig:
                        inst.then_inc(engsem[ename], 1)

            return body

        block.tensor(run("pe"))
        block.scalar(run("act"))
        block.vector(run("dve"))
        block.gpsimd(run("pool"))
        block.sync(run("sp"))


def _rng(a, n):
    return list(range(a, a + n))


def w_in_perm():
    p = []
    for pr in range(2):
        p += _rng(0 + pr * 128, 128) + _rng(256 + pr * 128, 128) + _rng(512 + pr * 128, 128)
    for pr in range(2):
        p += _rng(768 + pr * 128, 128) + _rng(1024 + pr * 128, 128) + _rng(1280 + pr * 128, 128)
    for g in range(2):
        p += _rng(1536 + g * 128, 128) + _rng(1792 + g * 64, 64) + _rng(1920 + g * 64, 64)
    p += _rng(2048, 384) + _rng(2432, 32)
    return np.asarray(p)


OFF_NA = (0, 384)
OFF_DF = (768, 1152)
OFF_GQ = (1536, 1792)
OFF_MLA = 2048
OFF_KR = 2432

NA_CLASSES = ((0, 0), (1, 0), (2, 0), (3, 0), (4, 0), (29, 24), (30, 24), (31, 24))


def na_class(r):
    if r <= 3:
        return r
    if r >= 29:
        return 5 + (r - 29)
    return 4


def na_index_tables():
    p = np.arange(128)[:, None, None, None]
    cls_r = np.asarray([c[0] for c in NA_CLASSES])[None, :, None, None]
    cls_rs = np.asarray([c[1] for c in NA_CLASSES])[None, :, None, None]
    t = np.arange(4)[None, None, :, None]
    c = np.arange(64)[None, None, None, :]
    kr = cls_rs + 2 * t + p // 64
    kc = p % 64
    dr = kr - cls_r
    dc = kc - c
    cs = np.clip(c - 8, 0, 48)
    valid = (kc >= cs) & (kc < cs + 16)
    valid = np.broadcast_to(valid, (128, 8, 4, 64))
    dri = np.broadcast_to(dr + 7, (128, 8, 4, 64))
    dci = np.clip(np.broadcast_to(dc + 15, (128, 8, 4, 64)), 0, 30)
    return dri, dci, valid


def host_constants():
    c = {}
    slopes = [2.0 ** (-8.0 * (i + 1) / 4) for i in range(4)]
    pp = np.arange(128, dtype=np.float64)[:, None]
    cc = np.arange(GW, dtype=np.float64)[None, :]
    g = np.stack([np.exp(-s * np.abs(cc - pp - 1920.0)) for s in slopes]).astype(np.float32)
    c["alibi_g"] = g
    pos = np.arange(S)
    row = (pos // 64).astype(np.float32)
    col = (pos % 64).astype(np.float32)
    inv32 = (10000.0 ** (-np.arange(0, 32, 2, dtype=np.float32) / 32)).astype(np.float32)

    def cs_tables(p):
        ang = p[:, None] * inv32[None, :]
        cosv = np.cos(ang).astype(np.float32)
        sinv = np.sin(ang).astype(np.float32)
        return np.concatenate([cosv, cosv], 1), np.concatenate([-sinv, sinv], 1)

    cr, sr = cs_tables(row)
    cc_, sc_ = cs_tables(col)
    cg = np.concatenate([cr, cc_], 1)
    sg = np.concatenate([sr, sc_], 1)
    cm, sm = cs_tables(pos.astype(np.float32))

    def tm(a):
        return np.ascontiguousarray(a.reshape(NT, 128, -1).transpose(1, 0, 2))

    c["rope_g"] = np.concatenate([tm(cg), tm(sg)], 2)
    c["rope_m"] = np.concatenate([tm(cm), tm(sm)], 2)
    c["ident"] = np.eye(128, dtype=np.float32)
    c["ident8"] = (8.0 * np.eye(128)).astype(np.float32)
    return c


def prep_shared(inp):
    o = {}
    w_in = np.asarray(inp["w_in"], np.float32)
    perm = w_in_perm()
    w1 = w_in[:, :, perm].reshape(NL, 8, 128, 2464).transpose(0, 2, 1, 3)
    o["w1"] = np.ascontiguousarray(w1)
    o["wo"] = np.ascontiguousarray(np.asarray(inp["w_o"], np.float32).reshape(NL, 8, 128, 1024).transpose(0, 2, 1, 3))
    gu = np.asarray(inp["ffn_w_gate_up"], np.float32).reshape(NL, 8, 128, 2, NJ, 128)
    o["gu"] = np.ascontiguousarray(gu.transpose(0, 4, 2, 1, 3, 5)).reshape(NL, NJ, 128, 8 * 256)
    o["wd"] = np.ascontiguousarray(np.asarray(inp["ffn_w_down"], np.float32).reshape(NL, NJ, 128, 1024))
    o["uq"] = np.ascontiguousarray(np.asarray(inp["mla_w_uq"], np.float32).reshape(NL, 2, 128, 384).transpose(0, 2, 1, 3))
    o["ukv"] = np.ascontiguousarray(np.asarray(inp["mla_w_ukv"], np.float32))

    def pc(a):
        return np.ascontiguousarray(np.asarray(a, np.float32).reshape(NL, 8, 128).transpose(2, 0, 1))

    o["g_pre"] = np.ascontiguousarray(np.concatenate([pc(inp["pre_mix_norm"]), pc(inp["pre_ffn_norm"])], 1))
    o["g_post"] = np.ascontiguousarray(np.stack([np.asarray(inp["post_mix_norm"], np.float32),
                                                 np.asarray(inp["post_ffn_norm"], np.float32)], 1))
    qn = np.asarray(inp["gqa_q_norm"], np.float32)
    kn = np.asarray(inp["gqa_k_norm"], np.float32)
    o["g_gqa"] = np.ascontiguousarray(np.concatenate([qn, qn, kn, kn], 1))
    o["g_mla"] = np.ascontiguousarray(np.concatenate([np.asarray(inp["mla_q_norm"], np.float32),
                                                      np.asarray(inp["mla_kv_norm"], np.float32)], 1))
    sub = np.asarray(inp["diff_subln"], np.float32)
    o["g_sub"] = np.ascontiguousarray(np.concatenate([sub, sub], 1).T)
    lamv = np.stack([np.asarray(inp[k], np.float32) for k in
                     ("diff_lambda_q1", "diff_lambda_k1", "diff_lambda_q2", "diff_lambda_k2")], 0)
    o["lam_in"] = np.ascontiguousarray(lamv.reshape(1, 4 * NL * 32))
    rb = np.asarray(inp["na_rel_bias"], np.float32)
    dri, dci, valid = na_index_tables()
    nab = np.empty((NL, 2, 128, 2, 8, 4, 64), np.float32)
    for l in range(NL):
        for h in range(4):
            tbl = rb[l, h][dri, dci]
            tbl = np.where(valid, tbl, np.float32(NEG))
            nab[l, h // 2, :, h % 2] = tbl
    o["nab"] = nab.reshape(NL, 2, 128, 4096)
    o.update(host_constants())
    return o


def lambda_init(l):
    return 0.8 - 0.6 * math.exp(-0.3 * l)


class Builder:
    def __init__(self, nseq, layers, mixers=(0, 1, 2, 3), do_ffn=True, dbg=None):
        self.nseq = nseq
        self.layers = list(layers)
        self.mixers = mixers
        self.do_ffn = do_ffn
        self.dbg = dbg

    def mm(self, out, lhsT, rhs, start, stop, reads, writes):
        self.P.add("pe", lambda e: e.matmul(out, lhsT=lhsT, rhs=rhs, start=start, stop=stop), reads, writes)

    def tr(self, out, in_, reads, writes):
        idn = self.ident[0:in_.shape[0], 0:in_.shape[0]]
        self.P.add("pe", lambda e: e.transpose(out, in_, idn), reads, writes)

    def act(self, out, in_, func, reads, writes, **kw):
        self.P.add("act", lambda e: e.activation(out=out, in_=in_, func=func, **kw), reads, writes)

    def ts(self, eng, out, in0, s1, s2, op0, op1, reads, writes):
        if op1 is None:
            self.P.add(eng, lambda e: e.tensor_scalar(out=out, in0=in0, scalar1=s1, scalar2=None, op0=op0), reads, writes)
        else:
            self.P.add(eng, lambda e: e.tensor_scalar(out=out, in0=in0, scalar1=s1, scalar2=s2, op0=op0, op1=op1), reads, writes)

    def tt(self, eng, out, in0, in1, op, reads, writes):
        self.P.add(eng, lambda e: e.tensor_tensor(out=out, in0=in0, in1=in1, op=op), reads, writes)

    def stt(self, out, in0, scalar, in1, op0, op1, reads, writes):
        self.P.add("dve", lambda e: e.scalar_tensor_tensor(out=out, in0=in0, scalar=scalar, in1=in1, op0=op0, op1=op1), reads, writes)

    def cp(self, eng, out, in_, reads, writes):
        if eng == "act":
            self.P.add("act", lambda e: e.activation(out=out, in_=in_, func=AF.Copy), reads, writes)
        else:
            self.P.add(eng, lambda e: e.tensor_copy(out=out, in_=in_), reads, writes)

    def dma(self, q, out, in_, reads, writes, sem):
        self.P.add(q, lambda e: e.dma_start(out=out, in_=in_), reads, writes, dma=sem)

    def recip(self, out, in_, reads, writes):
        self.P.add("dve", lambda e: e.reciprocal(out=out, in_=in_), reads, writes)

    def build(self):
        nc = bass.Bass("TRN2", target_bir_lowering=False)
        self.nc = nc
        ns = self.nseq
        dt = nc.dram_tensor
        I = {}

        def inp(name, shape, dtype=F32):
            I[name] = dt(name, list(shape), dtype, kind="ExternalInput").ap()

        inp("x", [ns, S, D])
        inp("w1", [NL, 128, 8 * 2464])
        inp("wo", [NL, 128, 8 * 1024])
        inp("gu", [NL, NJ, 128, 2048])
        inp("wd", [NL, NJ, 128, 1024])
        inp("uq", [NL, 128, 768])
        inp("ukv", [NL, 128, 512])
        inp("g_pre", [128, 2 * NL * 8])
        inp("g_post", [NL, 2, 1024])
        inp("g_gqa", [NL, 256])
        inp("g_mla", [NL, 384])
        inp("g_sub", [128, NL])
        inp("lam_in", [1, 4 * NL * 32])
        inp("nab", [NL, 2, 128, 4096])
        inp("alibi_g", [4, 128, GW])
        inp("rope_g", [128, 16 * 128])
        inp("rope_m", [128, 16 * 64])
        inp("ident", [128, 128])
        inp("ident8", [128, 128])
        self.I = I
        y = dt("y", [ns, S, D], F32, kind="ExternalOutput").ap()
        self.y = y
        if self.dbg:
            self.dbg_out = dt("dbg", list(self.dbg[1]), self.dbg[2], kind="ExternalOutput").ap()
        sc = {}
        for name, shape in (("w1", [NL, 128, 8 * 2464]), ("wo", [NL, 128, 8 * 1024]), ("gu", [NL, NJ, 128, 2048]),
                            ("wd", [NL, NJ, 128, 1024]), ("uq", [NL, 128, 768]), ("ukv", [NL, 128, 512]),
                            ("nab", [NL, 2, 128, 4096]), ("alibi_g", [4, 128, GW])):
            sc[name] = dt("sc_" + name, shape, BF16, kind="Internal").ap()
        self.sc = sc

        with contextlib.ExitStack() as es:
            P = Prog(nc)
            self.P = P
            sb = lambda name, shape, dtype: es.enter_context(nc.sbuf_tensor(name, list(shape), dtype))
            self.x_sb = sb("x_sb", [128, NT, D], F32)
            self.hT = sb("hT", [128, 8, S], BF16)
            self.mixT = sb("mixT", [128, 8, S], BF16)
            self.QT = sb("QT", [128, 2, S], BF16)
            self.KT = sb("KT", [128, 2, S], BF16)
            self.VA = sb("VA", [128, 2, NT, 192], BF16)
            self.TAB = sb("TAB", [128, 4096], BF16)
            self.NPT = 3
            self.PT = sb("PT", [128, self.NPT, 512], BF16)
            self.WS = sb("WS", [128, 2, 8 * 384], BF16)
            self.GB = sb("GB", [128, 1024], F32)
            self.SCR = sb("SCR", [128, 4, 512], F32)
            self.SCRB = sb("SCRB", [128, 2, 512], BF16)
            self.HN = sb("HN", [128, 1, 1024], BF16)
            self.RB = sb("RB", [128, 1, 512], F32)
            self.ROPG = sb("ROPG", [128, 16, 128], BF16)
            self.ROPM = sb("ROPM", [128, 16, 64], BF16)
            self.ident = sb("identb", [128, 128], BF16)
            self.ident8 = sb("ident8b", [128, 128], BF16)
            self.onesb = sb("onesb", [128, 128], BF16)
            self.onesf = sb("onesf", [1, 128], F32)
            self.GPRE = sb("GPRE", [128, 2 * NL * 8], F32)
            self.GMLA = sb("GMLA", [128, 384], F32)
            self.GGQA = self.GMLA
            self.GSUB = sb("GSUB", [128, NL], F32)
            self.NLAM = sb("NLAM", [128, NL], F32)
            self.STAT = sb("STAT", [128, 64], F32)
            self.LT = sb("LT", [128, 3, 128], BF16)
            self.PS = es.enter_context(nc.psum_tensor("PS", [128, 8, 512], F32))

            B = {}
            self.B = B
            for k in ("QT0", "QT1", "KT0", "KT1", "TAB", "GB", "HN0", "HN1", "RB0", "RB1", "ROP", "CONST", "GPRE",
                      "GGQA", "GMLA", "GSUB", "NLAM", "STAT", "LT", "WS0", "WS1", "SCRB0", "SCRB1", "OUT", "DBG"):
                B[k] = P.buf(k)
            B["X"] = P.bufs(NT, "X")
            B["HT"] = P.bufs(4, "HT")
            B["MIX"] = [P.bufs(4, f"MIX{c}_") for c in range(8)]
            B["VA"] = [P.bufs(NT, f"VA{a}_") for a in range(2)]
            B["PT"] = P.bufs(self.NPT, "PT")
            B["PTX"] = P.bufs(6, "PTX")
            vflat = self.VA[:, 1, :, :].rearrange("p t n -> p (t n)")
            self.pt_base = [(self.PT[:, i, :], B["PT"][i]) for i in range(self.NPT)]
            self.pt_ext = self.pt_base + [(vflat[:, k * 512:(k + 1) * 512], B["PTX"][k]) for k in range(6)]
            self.pt_cur = self.pt_base
            B["SCR"] = P.bufs(4, "SCR")
            B["PSB"] = P.bufs(8, "PS")
            for b_ in B["PSB"]:
                b_.excl = True
            B["TABS"] = P.bufs(4, "TABS")
            B["SC"] = {k: [P.buf(f"sc_{k}{i}") for i in range(NL)] for k in sc}
            self.stat_pos = {}
            B["STATK"] = {k: P.buf("STAT_" + k) for k in self.STAT_KEYS}
            B["SCRH"] = [[P.buf(f"SCRH{s_}_{p}") for p in range(2)] for s_ in range(4)]
            B["LTP"] = P.bufs(2, "LTP")
            P.add("pool", lambda e: e.memset(self.STAT[:], 0.0), [], [B["STAT"]] + list(B["STATK"].values()))

            P.tag = "pro"
            self.prologue()
            for s in range(ns):
                if not getattr(self, "x_loaded", False):
                    self.load_x(s)
                self.x_loaded = False
                for li, l in enumerate(self.layers):
                    self.layer(s, l, last=(li == len(self.layers) - 1))
            if self.dbg:
                self.dbg[0](self)
            P.add("sp", None, reads=[B["OUT"], B["DBG"]])
            P.emit(es)
        return nc

    def prologue(self):
        P, B, I, sc = self.P, self.B, self.I, self.sc
        nc = self.nc

        def cast(name, l, src, dst, rows, cols):
            nch = max(1, (rows + 8191) // 8192)
            step = (rows + nch - 1) // nch
            for r0 in range(0, rows, step):
                r1 = min(rows, r0 + step)
                self.dma("pool", dst[r0:r1, :], src[r0:r1, :], [], [B["SC"][name][l]], f"c_{name}{l}")

        def flat(ap, cols):
            names = " ".join(f"d{i}" for i in range(ap.ndim))
            a = ap.rearrange(f"{names} -> ({names})")
            return a.rearrange("(r c) -> r c", c=cols)

        P.add("pool", lambda e: e.memset(self.onesb[:], 1.0), [], [B["CONST"]])
        P.add("pool", lambda e: e.memset(self.onesf[:], 1.0), [], [B["CONST"]])
        P.add("pool", lambda e: e.memset(self.VA[:], 1.0), [], [b for a in B["VA"] for b in a])
        self.dma("pool", self.ident[:], I["ident"], [], [B["CONST"]], "const")
        self.dma("pool", self.ident8[:], I["ident8"], [], [B["CONST"]], "const")
        self.dma("pool", self.ROPG[:].rearrange("p t n -> p (t n)"), I["rope_g"], [], [B["ROP"]], "rop")
        self.dma("pool", self.ROPM[:].rearrange("p t n -> p (t n)"), I["rope_m"], [], [B["ROP"]], "rop")
        self.dma("sp", self.GPRE[:], I["g_pre"], [], [B["GPRE"]], "gpre")
        self.dma("sp", self.GSUB[:], I["g_sub"], [], [B["GSUB"]], "gsub")
        self.dma("sp", self.SCR[0:1, 0, :], I["lam_in"], [], [B["SCR"][0], B["SCR"][1]], "lamr")
        for l in self.layers:
            cast("w1", l, flat(I["w1"][l], 1232), flat(sc["w1"][l], 1232), 2048, 1232)
            cast("nab", l, flat(I["nab"][l], 2048), flat(sc["nab"][l], 2048), 512, 2048)
            if l == self.layers[0]:
                cast("alibi_g", 0, flat(I["alibi_g"], 1984), flat(sc["alibi_g"], 1984), 1024, 1984)
            cast("uq", l, flat(I["uq"][l], 2048), flat(sc["uq"][l], 2048), 48, 2048)
            cast("ukv", l, flat(I["ukv"][l], 2048), flat(sc["ukv"][l], 2048), 32, 2048)
            cast("wo", l, flat(I["wo"][l], 2048), flat(sc["wo"][l], 2048), 512, 2048)
            cast("gu", l, flat(I["gu"][l], 2048), flat(sc["gu"][l], 2048), NJ * 128, 2048)
            cast("wd", l, flat(I["wd"][l], 2048), flat(sc["wd"][l], 2048), 1408, 2048)
        L = self.SCR[0:1, 0:2, :].rearrange("p a n -> p (a n)")
        n = NL * 32
        self.tt("dve", L[:, 512:512 + n], L[:, 0:n], L[:, n:2 * n], ALU.mult, [B["SCR"][0], B["SCR"][1]], [B["SCR"][0], B["SCR"][1]])
        self.tt("dve", L[:, 0:n], L[:, 2 * n:3 * n], L[:, 3 * n:4 * n], ALU.mult, [B["SCR"][0], B["SCR"][1]], [B["SCR"][0], B["SCR"][1]])
        P.add("dve", lambda e: e.tensor_reduce(out=L[:, 256:256 + NL], in_=L[:, 512:512 + n].rearrange("p (l k) -> p l k", k=32),
                                               axis=AX.X, op=ALU.add), [B["SCR"][0], B["SCR"][1]], [B["SCR"][0], B["SCR"][1]])
        P.add("dve", lambda e: e.tensor_reduce(out=L[:, 264:264 + NL], in_=L[:, 0:n].rearrange("p (l k) -> p l k", k=32),
                                               axis=AX.X, op=ALU.add), [B["SCR"][0], B["SCR"][1]], [B["SCR"][0], B["SCR"][1]])
        self.act(L[:, 272:272 + NL], L[:, 256:256 + NL], AF.Exp, [B["SCR"][0], B["SCR"][1]], [B["SCR"][0], B["SCR"][1]])
        self.act(L[:, 280:280 + NL], L[:, 264:264 + NL], AF.Exp, [B["SCR"][0], B["SCR"][1]], [B["SCR"][0], B["SCR"][1]])
        self.tt("dve", L[:, 288:288 + NL], L[:, 280:280 + NL], L[:, 272:272 + NL], ALU.subtract, [B["SCR"][0], B["SCR"][1]], [B["SCR"][0], B["SCR"][1]])
        for l in range(NL):
            self.ts("dve", L[:, 296 + l:297 + l], L[:, 288 + l:289 + l], -lambda_init(l), None, ALU.add, None,
                    [B["SCR"][0], B["SCR"][1]], [B["SCR"][0], B["SCR"][1]])
        ps = self.PS[:, 7, 0:NL]
        self.mm(ps, self.onesf[0:1, :], L[0:1, 296:296 + NL], True, True, [B["SCR"][0], B["SCR"][1], B["CONST"]], [B["PSB"][7]])
        self.cp("dve", self.NLAM[:], ps, [B["PSB"][7]], [B["NLAM"]])
        for l in range(NL):
            self.ts("dve", self.GSUB[:, l:l + 1], self.GSUB[:, l:l + 1], 1.0 - lambda_init(l), None, ALU.mult, None,
                    [B["GSUB"]], [B["GSUB"]])

    def load_x(self, s):
        for t in range(NT):
            self.dma("sp", self.x_sb[:, t, :], self.I["x"][s, t * 128:(t + 1) * 128, :], [], [self.B["X"][t]], f"x{t}")

    def store_x(self, s, t):
        self.dma("sp", self.y[s, t * 128:(t + 1) * 128, :], self.x_sb[:, t, :], [self.B["X"][t]], [self.B["OUT"]], f"y{t}")

    STAT_KEYS = {"norm": (0, 4), "n0": (32, 4), "n1": (36, 4), "post": (4, 4), "p0": (8, 4), "p1": (12, 4), "g0": (16, 4), "g1": (20, 4), "m0": (24, 4), "m1": (28, 4)}

    def stat(self, n=1, key="norm"):
        base, size = self.STAT_KEYS[key]
        i = self.stat_pos.get(key, 0)
        if i + n > size:
            i = 0
        self.stat_pos[key] = i + n
        return self.STAT[:, base + i:base + i + n], self.B["STATK"][key]

    def pipeline(self, stages, n=NT):
        npair = n // 2
        ns = len(stages)
        for step in range(npair + ns - 1):
            for si in reversed(range(ns)):
                pi = step - si
                if not (0 <= pi < npair):
                    continue
                gens = [stages[si](2 * pi), stages[si](2 * pi + 1)]
                while gens:
                    for g_ in list(gens):
                        try:
                            next(g_)
                        except StopIteration:
                            gens.remove(g_)

    def fence_scr(self):
        B = self.B
        toks = list(B["SCR"]) + [B["SCRH"][s_][p] for s_ in range(4) for p in range(2)] + [B["HN0"], B["LTP"][0], B["LTP"][1]]
        self.P.add("dve", lambda e: e.tensor_copy(out=self.STAT[:, 62:63], in_=self.STAT[:, 63:64]), toks, toks)

    def norm_to_hT(self, gcol):
        for t in range(NT):
            self.norm_tile(t, gcol, 6 + (t % 2))

    def norm_tile(self, t, gcol, bank):
        for _ in self.norm_gen(t, gcol, bank, 0):
            pass

    def norm_gen(self, t, gcol, bank, par):
        P, B = self.P, self.B
        gain = self.GPRE[:, gcol * 8:(gcol + 1) * 8].unsqueeze(2).broadcast_to([128, 8, 128])
        psb = self.PS[:, bank, :].bitcast(BF16)
        if par == 0:
            hn, hb, key = self.HN[:, 0, :], B["HN0"], "norm"
        else:
            hn, hb, key = self.QT[:, 0, 0:1024], B["QT0"], "n1"
        ss, sk = self.stat(1, key)
        yield self.act(hn, self.x_sb[:, t, :], AF.Square, [B["X"][t]], [hb, sk], accum_out=ss)
        yield self.act(ss, ss, AF.Sqrt, [sk], [sk], scale=1.0 / D, bias=EPS)
        yield self.recip(ss, ss, [sk], [sk])
        yield self.ts("dve", hn, self.x_sb[:, t, :], ss, None, ALU.mult, None, [B["X"][t], sk], [hb])
        for c in range(8):
            yield self.tr(psb[:, c * 128:(c + 1) * 128], hn[:, c * 128:(c + 1) * 128], [hb, B["CONST"]], [B["PSB"][bank]])
        yield self.tt("dve", self.hT[:, :, t * 128:(t + 1) * 128], psb.rearrange("p (c n) -> p c n", c=8), gain, ALU.mult,
                      [B["PSB"][bank], B["GPRE"]], [B["HT"][t // 4]])

    @staticmethod
    def run_gens(gens):
        gens = list(gens)
        while gens:
            for g_ in list(gens):
                try:
                    next(g_)
                except StopIteration:
                    gens.remove(g_)

    def load_ws(self, slot, name, l, off, ncols, total):
        src = self.sc[name][l].rearrange("p (c n) -> p c n", c=8)[:, :, off:off + ncols]
        dst = self.WS[:, slot, 0:8 * ncols].rearrange("p (c n) -> p c n", c=8)
        self.dma("sp", dst, src, [self.B["SC"][name][l]], [self.B[f"WS{slot}"]], f"ws{slot}")
        return dst

    def proj_fm(self, w, c0, m, evac):
        B = self.B
        for tb in range(4):
            bank = 5 + (tb % 2)
            ps = self.PS[0:m, bank, :]
            for c in range(8):
                self.mm(ps, w[:, c, c0:c0 + m], self.hT[:, c, tb * 512:(tb + 1) * 512], c == 0, c == 7,
                        [B["HT"][tb], self.wsbuf], [B["PSB"][bank]])
            evac(tb, self.PS, bank)

    def proj_tm(self, w, c0, n, evac, tok0=0, ntiles=NT, banks=(5, 6)):
        B = self.B
        for t in range(ntiles):
            bank = banks[t % len(banks)]
            ps = self.PS[:, bank, 0:n]
            a = tok0 + t * 128
            tbs = sorted({a // 512, (a + 127) // 512})
            for c in range(8):
                self.mm(ps, self.hT[:, c, a:a + 128], w[:, c, c0:c0 + n], c == 0, c == 7,
                        [B["HT"][i] for i in tbs] + [self.wsbuf], [B["PSB"][bank]])
            evac(t, ps, bank)

    def evac_v(self, al):
        B = self.B

        def f(t, ps, bank):
            dst = self.VA[:, al, t, :].rearrange("p (a b) -> p a b", b=64)[:, 0:3:2, :]
            src = ps.rearrange("p (a b) -> p a b", b=64)
            self.cp("dve", dst, src, [B["PSB"][bank]], [B["VA"][al][t]])
        return f

    def normalize(self, bank, hh, dst, dst_bufs, dst_eng="dve"):
        B = self.B
        orows = slice(0, 64) if hh == 0 else slice(64, 128)
        srows = slice(64, 128) if hh == 0 else slice(0, 64)
        rb = self.RB[srows, 0, :]
        rbb = B["RB0"]
        self.act(rb, self.PS[srows, bank, :], AF.Ln, [B["PSB"][bank]], [rbb])
        self.act(rb, rb, AF.Exp, [rbb], [rbb], scale=-1.0)
        self.tt("dve", dst, self.PS[orows, bank, :], rb, ALU.mult, [B["PSB"][bank], rbb], dst_bufs)

    def attn_dense(self, qf, kf, vf, krows, scale, tabf, outf, reads):
        B = self.B
        sbanks = (0, 1, 2, 5, 6)
        obanks = (3, 4)
        look = 4
        pts = self.pt_cur
        cnt = getattr(self, "acnt", 0)
        for qb in range(4):
            ob = obanks[qb % 2]

            def qk(kt):
                sbk = sbanks[(cnt + kt) % len(sbanks)]
                self.mm(self.PS[:, sbk, :], kf(kt), qf(qb), True, True, reads, [B["PSB"][sbk]])

            def rest(kt):
                i = cnt + kt
                sbk = sbanks[i % len(sbanks)]
                pt, ptb = pts[i % len(pts)]
                self.act(pt, self.PS[:, sbk, :], AF.Exp, [B["PSB"][sbk]], [ptb], scale=scale)
                v, vb = vf(kt)
                self.mm(self.PS[:, ob, :], v, pt, kt == 0, kt == NT - 1, [ptb, vb], [B["PSB"][ob]])

            for kt in range(min(look, NT)):
                qk(kt)
            for kt in range(NT):
                if kt + look < NT:
                    qk(kt + look)
                rest(kt)
            cnt += NT
            outf(qb, ob)
        self.acnt = cnt

    def layer(self, s, l, last):
        P, B, sc = self.P, self.B, self.sc
        P.tag = "norm1"
        if not getattr(self, "norm_done", False):
            self.norm_to_hT(l)
        self.norm_done = False
        self.cur_last = last
        self.next_l = None if last else self.layers[self.layers.index(l) + 1]
        self.dma("sp", self.GB[:], self.I["g_post"][l, 0:1, :].broadcast_to([128, 1024]), [], [B["GB"]], "gb")
        slot = 0
        if 0 in self.mixers:
            for pr in range(2):
                P.tag = f"na{pr}"
                self.mixer_na(l, pr, slot)
                slot ^= 1
        else:
            self.zero_mix(0)
        if 1 in self.mixers:
            for pr in range(2):
                P.tag = f"diff{pr}"
                self.mixer_diff(l, pr, slot)
                slot ^= 1
        else:
            self.zero_mix(1)
        if 2 in self.mixers:
            self.dma("sp", self.GMLA[:, 0:256], self.I["g_gqa"][l:l + 1, :].broadcast_to([128, 256]), [], [B["GMLA"]], "gmla")
            for g in range(2):
                P.tag = f"gqa{g}"
                self.mixer_gqa(l, g, slot)
                slot ^= 1
        else:
            self.zero_mix(2)
        if 3 in self.mixers:
            self.dma("sp", self.GMLA[:], self.I["g_mla"][l:l + 1, :].broadcast_to([128, 384]), [], [B["GMLA"]], "gmla")
            for pr in range(2):
                P.tag = f"mla{pr}"
                self.mixer_mla(l, pr)
        else:
            self.zero_mix(3)
        P.tag = "wo"
        self.wo_residual(l)
        if self.do_ffn:
            P.tag = "ffn"
            self.ffn(s, l, last)
        elif last:
            for t in range(NT):
                self.store_x(s, t)

    def fence_ptx(self):
        self.P.add("pool", lambda e: e.memset(self.VA[0:1, 1, 15, 190:192], 1.0), [], self.B["VA"][1] + self.B["PTX"])
        self.pt_cur = self.pt_ext

    def zero_mix(self, m):
        for c in (2 * m, 2 * m + 1):
            self.P.add("pool", lambda e, c=c: e.memset(self.mixT[:, c, :], 0.0), [], self.B["MIX"][c])

    def mixer_na(self, l, pr, slot):
        P, B = self.P, self.B
        w = self.load_ws(slot, "w1", l, OFF_NA[pr], 384, 2464)
        self.wsbuf = B[f"WS{slot}"]
        self.dma("sp", self.TAB[:], self.sc["nab"][l, pr], [B["SC"]["nab"][l]], B["TABS"], "tab")
        chunk = pr

        def evq(dstT, dbuf):
            def f(tb, PS, bank):
                self.cp("dve", dstT[:, 0, tb * 512:(tb + 1) * 512], PS[:, bank, :], [B["PSB"][bank]], [dbuf])
            return f
        P.add("pool", lambda e: e.memset(self.KT[64:128, 0, :], 0.0), [], [B["KT0"]])
        P.add("pool", lambda e: e.memset(self.KT[0:64, 1, :], 0.0), [], [B["KT1"]])
        P.add("pool", lambda e: e.memset(self.VA[:, 1, :, 64:128], 1.0), [], B["VA"][1] + B["PTX"])
        self.pt_cur = self.pt_base

        def evk(tb, PS, bank):
            self.cp("dve", self.KT[0:64, 0, tb * 512:(tb + 1) * 512], PS[0:64, bank, :], [B["PSB"][bank]], [B["KT0"]])
            self.cp("dve", self.KT[64:128, 1, tb * 512:(tb + 1) * 512], PS[64:128, bank, :], [B["PSB"][bank]], [B["KT1"]])
        self.proj_fm(w, 0, 128, evq(self.QT, B["QT0"]))
        self.proj_fm(w, 128, 128, evk)
        self.proj_tm(w, 256, 128, self.evac_v(0))
        self.proj_tm(w, 256, 128, self.evac_v(1), tok0=64, ntiles=NT - 1)
        scale = 1.0 / 8.0
        items = [(hh, qb, rp) for hh in range(2) for qb in range(4) for rp in range(4)]
        sbanks = (0, 1, 2, 5, 6)
        look = 2

        def info(idx):
            hh, qb, rp = items[idx]
            return hh, qb, rp, sbanks[idx % len(sbanks)], idx % self.NPT, (qb * 8 + rp * 2, qb * 8 + rp * 2 + 1)

        def front(idx):
            hh, qb, rp, sbk, pi, rows = info(idx)
            for ri, r in enumerate(rows):
                cls = na_class(r)
                tcol = (hh * 8 + cls) * 256
                self.mm(self.PS[:, sbk, ri * 256:(ri + 1) * 256], self.ident8[:], self.TAB[:, tcol:tcol + 256],
                        ri == 0, False, B["TABS"] + [B["CONST"]], [B["PSB"][sbk]])
            for ri, r in enumerate(rows):
                rs = min(max(r - 4, 0), 24)
                for t in range(4):
                    ks = (rs + 2 * t) * 64
                    self.mm(self.PS[:, sbk, ri * 256 + t * 64:ri * 256 + (t + 1) * 64], self.KT[:, hh, ks:ks + 128],
                            self.QT[:, 0, r * 64:(r + 1) * 64], False, (ri == 1 and t == 3),
                            [B["QT0"], B[f"KT{hh}"]], [B["PSB"][sbk]])

        def back(idx):
            hh, qb, rp, sbk, pi, rows = info(idx)
            ob = 3 + (qb % 2)
            pt, ptb = self.pt_base[pi]
            self.act(pt, self.PS[:, sbk, :], AF.Exp, [B["PSB"][sbk]], [ptb], scale=scale)
            for ri, r in enumerate(rows):
                rs = min(max(r - 4, 0), 24)
                al = rs % 2
                for t in range(4):
                    j = (rs + 2 * t) // 2
                    qc = (r % 8) * 64
                    self.mm(self.PS[:, ob, qc:qc + 64], self.VA[:, al, j, hh * 64:hh * 64 + 128],
                            pt[:, ri * 256 + t * 64:ri * 256 + (t + 1) * 64], t == 0, t == 3,
                            [ptb, B["VA"][al][j]], [B["PSB"][ob]])
            if rp == 3:
                orows = slice(0, 64) if hh == 0 else slice(64, 128)
                self.normalize(ob, hh, self.mixT[orows, chunk, qb * 512:(qb + 1) * 512], [B["MIX"][chunk][qb]])

        for idx in range(min(look, len(items))):
            front(idx)
        for idx in range(len(items)):
            if idx + look < len(items):
                front(idx + look)
            back(idx)

    def mixer_diff(self, l, pr, slot):
        P, B = self.P, self.B
        w = self.load_ws(slot, "w1", l, OFF_DF[pr], 384, 2464)
        self.wsbuf = B[f"WS{slot}"]
        chunk = 2 + pr
        self.fence_ptx()
        for m in range(2):
            P.add("pool", lambda e, m=m: e.memset(self.KT[:, m, :], 0.0), [], [B[f"KT{m}"]])
        P.add("pool", lambda e: e.memset(self.QT[64:128, 0, :], 0.0), [], [B["QT0"]])
        P.add("pool", lambda e: e.memset(self.QT[0:64, 1, :], 0.0), [], [B["QT1"]])

        def evq(tb, PS, bank):
            self.cp("dve", self.QT[0:64, 0, tb * 512:(tb + 1) * 512], PS[0:64, bank, :], [B["PSB"][bank]], [B["QT0"]])
            self.cp("act", self.QT[64:128, 1, tb * 512:(tb + 1) * 512], PS[64:128, bank, :], [B["PSB"][bank]], [B["QT1"]])

        def evk(tb, PS, bank):
            for i in range(4):
                rs_ = slice(i * 32, (i + 1) * 32)
                m = i % 2
                self.cp("dve" if i < 2 else "act", self.KT[rs_, m, tb * 512:(tb + 1) * 512], PS[rs_, bank, :],
                        [B["PSB"][bank]], [B[f"KT{m}"]])
        self.proj_fm(w, 0, 128, evq)
        self.proj_fm(w, 128, 128, evk)
        self.proj_tm(w, 256, 128, self.evac_v(0))
        scale = 32.0 ** -0.5
        for hh in range(2):
            h = 2 * pr + hh
            self.dma("sp", self.TAB[:, 0:GW], self.sc["alibi_g"][h], [B["SC"]["alibi_g"][0]], B["TABS"], "tab")
            self.attn_diff_head(l, hh, scale, chunk)

    def attn_diff_head(self, l, hh, scale, chunk):
        B = self.B
        orows = slice(0, 64) if hh == 0 else slice(64, 128)
        sbanks = (0, 1, 2, 5, 6)
        look = 4
        pts = self.pt_cur
        cnt = getattr(self, "acnt", 0)
        for qb in range(4):
            obs = (3, 4)
            seq = [(m, kt) for m in range(2) for kt in range(NT)]

            def qk(idx):
                m, kt = seq[idx]
                sbk = sbanks[(cnt + idx) % len(sbanks)]
                self.mm(self.PS[:, sbk, :], self.KT[:, m, kt * 128:(kt + 1) * 128], self.QT[:, hh, qb * 512:(qb + 1) * 512],
                        True, True, [B[f"QT{hh}"], B[f"KT{m}"]], [B["PSB"][sbk]])

            def rest(idx):
                m, kt = seq[idx]
                i = cnt + idx
                sbk = sbanks[i % len(sbanks)]
                pt, ptb = pts[i % len(pts)]
                self.act(pt, self.PS[:, sbk, :], AF.Exp, [B["PSB"][sbk]], [ptb], scale=scale)
                c0 = qb * 512 - kt * 128 + 1920
                eng = "pool" if (i % 8 == 0) else "dve"
                self.tt(eng, pt, pt, self.TAB[:, c0:c0 + 512], ALU.mult, [ptb] + B["TABS"], [ptb])
                ob = obs[m]
                self.mm(self.PS[:, ob, :], self.VA[:, 0, kt, hh * 64:hh * 64 + 128], pt, kt == 0, kt == NT - 1,
                        [ptb, B["VA"][0][kt]], [B["PSB"][ob]])

            for idx in range(look):
                qk(idx)
            for idx in range(len(seq)):
                if idx + look < len(seq):
                    qk(idx + look)
                rest(idx)
            cnt += len(seq)
            for m in range(2):
                self.normalize(obs[m], hh, self.SCR[orows, m, :], [B["SCR"][m]])
            self.diff_finish(l, hh, qb, chunk)
        self.acnt = cnt

    def diff_finish(self, l, hh, qb, chunk):
        B = self.B
        orows = slice(0, 64) if hh == 0 else slice(64, 128)
        d1 = self.SCR[orows, 0, :]
        d2 = self.SCR[orows, 1, :]
        dd = self.SCR[orows, 2, :]
        rs = self.SCR[orows, 3, :]
        sq = self.SCRB[orows, 0, :]
        self.stt(dd, d2, self.NLAM[orows, l:l + 1], d1, ALU.mult, ALU.add, [B["SCR"][0], B["SCR"][1], B["NLAM"]], [B["SCR"][2]])
        self.tt("dve", sq, dd, dd, ALU.mult, [B["SCR"][2]], [B["SCRB0"]])
        bank = 7
        self.mm(self.PS[:, bank, :], self.onesb[orows, :], sq, True, True, [B["SCRB0"], B["CONST"]], [B["PSB"][bank]])
        self.act(rs, self.PS[orows, bank, :], AF.Ln, [B["PSB"][bank]], [B["SCR"][3]], scale=1.0 / 64.0, bias=EPS)
        self.act(rs, rs, AF.Exp, [B["SCR"][3]], [B["SCR"][3]], scale=-0.5)
        self.stt(self.mixT[orows, chunk, qb * 512:(qb + 1) * 512], dd, self.GSUB[orows, l:l + 1], rs, ALU.mult, ALU.mult,
                 [B["SCR"][2], B["SCR"][3], B["GSUB"]], [B["MIX"][chunk][qb]])

    def mixer_gqa(self, l, g, slot):
        P, B = self.P, self.B
        w = self.load_ws(slot, "w1", l, OFF_GQ[g], 256, 2464)
        wsb = B[f"WS{slot}"]
        chunk = 4 + g
        P.add("pool", lambda e: e.memset(self.KT[64:128, 0, :], 0.0), [], [B["KT0"]])
        P.add("pool", lambda e: e.memset(self.KT[0:64, 1, :], 0.0), [], [B["KT1"]])
        self.fence_ptx()
        self.fence_scr()
        st_ = {}

        def bufs(t):
            p = t % 2
            H = B["SCRH"]
            return dict(p=p, bank=p, raw=self.SCR[:, 0, p * 256:p * 256 + 192], rawb=H[0][p],
                        y=self.SCR[:, 1, p * 256:(p + 1) * 256], yb=H[1][p],
                        a=self.SCR[:, 2, p * 256:(p + 1) * 256], ab=H[2][p],
                        bs=self.SCR[:, 3, p * 256:(p + 1) * 256], bsb=H[3][p],
                        qk=self.SCRB[:, p, 0:256], qkb=B[f"SCRB{p}"])

        def stage_a(t):
            u = bufs(t)
            pb = B["PSB"][u["bank"]]
            ps = self.PS[:, u["bank"], 0:256]
            for c in range(8):
                yield self.mm(ps, self.hT[:, c, t * 128:(t + 1) * 128], w[:, c, 0:256], c == 0, c == 7, [B["HT"][t // 4], wsb], [pb])
            dst = self.VA[:, 0, t, :].rearrange("p (a b) -> p a b", b=64)[:, 0:3:2, :]
            yield self.cp("dve", dst, ps[:, 192:256].unsqueeze(1).broadcast_to([128, 2, 64]), [pb], [B["VA"][0][t]])
            yield self.cp("act", u["raw"], ps[:, 0:192], [pb], [u["rawb"]])
            sq = u["y"][:, 0:192]
            yield self.tt("dve", sq, u["raw"], u["raw"], ALU.mult, [u["rawb"]], [u["yb"]])
            ss, sk = self.stat(3, f"g{u['p']}")
            st_[t] = (ss, sk)
            yield P.add("dve", lambda e: e.tensor_reduce(out=ss, in_=sq.rearrange("p (h d) -> p h d", d=64), axis=AX.X, op=ALU.add),
                  [u["yb"]], [sk])
            yield self.act(ss, ss, AF.Sqrt, [sk], [sk], scale=1.0 / 64.0, bias=EPS)

        def stage_b(t):
            u = bufs(t)
            ss, sk = st_.pop(t)
            raw, y = u["raw"], u["y"]
            yield self.recip(ss, ss, [sk], [sk])
            for h3 in range(3):
                yield self.stt(y[:, h3 * 64:(h3 + 1) * 64], raw[:, h3 * 64:(h3 + 1) * 64], ss[:, h3:h3 + 1],
                         self.GGQA[:, h3 * 64:(h3 + 1) * 64], ALU.mult, ALU.mult, [u["rawb"], sk, B["GMLA"]], [u["yb"]])
            yield self.stt(y[:, 192:256], raw[:, 128:192], ss[:, 2:3], self.GGQA[:, 192:256], ALU.mult, ALU.mult,
                     [u["rawb"], sk, B["GMLA"]], [u["yb"]])
            C = self.ROPG[:, t, 0:64].unsqueeze(1).broadcast_to([128, 4, 64])
            Sg = self.ROPG[:, t, 64:128]
            y3 = y.rearrange("p (h d) -> p h d", d=64)
            a3 = u["a"].rearrange("p (h d) -> p h d", d=64)
            b3 = u["bs"].rearrange("p (h d) -> p h d", d=64)
            yield self.tt("dve", a3, y3, C, ALU.mult, [u["yb"], B["ROP"]], [u["ab"]])
            y5 = y.rearrange("p (h a r d) -> p h a r d", a=2, r=2, d=16)
            b5 = u["bs"].rearrange("p (h a r d) -> p h a r d", a=2, r=2, d=16)
            s5 = Sg.rearrange("p (a r d) -> p a r d", a=2, r=2, d=16)
            for r in range(2):
                yield self.tt("dve", b5[:, :, :, r, :], y5[:, :, :, 1 - r, :], s5[:, :, r, :].unsqueeze(1).broadcast_to([128, 4, 2, 16]),
                        ALU.mult, [u["yb"], B["ROP"]], [u["bsb"]])
            qk, qkb = u["qk"], u["qkb"]
            yield self.tt("dve", qk.rearrange("p (h d) -> p h d", d=64), a3, b3, ALU.add, [u["ab"], u["bsb"]], [qkb])
            tbank = 6 + ((t // 4) % 2)
            psb = self.PS[:, tbank, :].bitcast(BF16)
            tq = t % 4
            yield self.tr(psb[:, tq * 128:(tq + 1) * 128], qk[:, 0:128], [qkb, B["CONST"]], [B["PSB"][tbank]])
            yield self.tr(psb[:, 512 + tq * 128:512 + (tq + 1) * 128], qk[:, 128:256], [qkb, B["CONST"]], [B["PSB"][tbank]])
            if tq == 3:
                tb = t // 4
                yield self.cp("act", self.QT[:, 0, tb * 512:(tb + 1) * 512], psb[:, 0:512], [B["PSB"][tbank]], [B["QT0"]])
                yield self.cp("act", self.KT[0:64, 0, tb * 512:(tb + 1) * 512], psb[0:64, 512:1024], [B["PSB"][tbank]], [B["KT0"]])
                yield self.cp("dve", self.KT[64:128, 1, tb * 512:(tb + 1) * 512], psb[64:128, 512:1024], [B["PSB"][tbank]], [B["KT1"]])

        self.pipeline([stage_a, stage_b])
        self.fence_scr()
        P.tag = P.tag[:4] + "a"
        for hh in range(2):
            orows = slice(hh * 64, hh * 64 + 64)

            def outf(qb, bank, hh=hh, orows=orows):
                self.normalize(bank, hh, self.mixT[orows, chunk, qb * 512:(qb + 1) * 512], [B["MIX"][chunk][qb]])

            self.attn_dense(lambda qb: self.QT[:, 0, qb * 512:(qb + 1) * 512],
                            lambda kt, hh=hh: self.KT[:, hh, kt * 128:(kt + 1) * 128],
                            lambda kt, hh=hh: (self.VA[:, 0, kt, hh * 64:hh * 64 + 128], B["VA"][0][kt]),
                            128, 1.0 / 8.0, None, outf, [B["QT0"], B[f"KT{hh}"]])

    def mixer_mla(self, l, pr):
        P, B = self.P, self.B
        wA = self.load_ws(0, "w1", l, OFF_MLA, 384, 2464)
        ws1 = self.WS[:, 1, :]
        self.dma("sp", ws1[:, 0:256].rearrange("p (c n) -> p c n", c=8),
                 self.sc["w1"][l].rearrange("p (c n) -> p c n", c=8)[:, :, OFF_KR:OFF_KR + 32], [B["SC"]["w1"][l]], [B["WS1"]], "ws1")
        self.dma("sp", ws1[:, 256:1024], self.sc["uq"][l], [B["SC"]["uq"][l]], [B["WS1"]], "ws1")
        self.dma("sp", ws1[:, 1024:1536], self.sc["ukv"][l], [B["SC"]["ukv"][l]], [B["WS1"]], "ws1")
        wkr = ws1[:, 0:256].rearrange("p (c n) -> p c n", c=8)
        wuq = ws1[:, 256:1024].rearrange("p (j n) -> p j n", j=2)
        wukv = ws1[:, 1024:1536]
        chunk = 6 + pr
        scale = 96.0 ** -0.5
        self.fence_ptx()
        self.fence_scr()
        st_ = {}
        H = B["SCRH"]

        def bufs(t):
            p = t % 2
            return dict(p=p, lbank=p, tbank=6 + p, qbank=(2, 5)[p],
                        raw=self.SCR[:, p, 0:416], rawb=B["SCR"][p],
                        lat=self.SCRB[:, p, 0:384], latb=B[f"SCRB{p}"],
                        lt=self.HN[:, 0, p * 384:(p + 1) * 384].rearrange("p (j n) -> p j n", j=3), ltb=B["LTP"][p],
                        qk=self.PT[:, p, 0:384], qkb=B["PT"][p],
                        rin=self.SCR[:, 2, p * 256:p * 256 + 96], rinb=H[2][p],
                        ra=self.SCR[:, 3, p * 256:p * 256 + 96], rb=self.SCR[:, 3, p * 256 + 96:p * 256 + 192], rab=H[3][p])

        def stage_a(t):
            u = bufs(t)
            pb = B["PSB"][u["lbank"]]
            ps = self.PS[:, u["lbank"], :]
            tb = t // 4
            for c in range(8):
                yield self.mm(ps[:, 0:384], self.hT[:, c, t * 128:(t + 1) * 128], wA[:, c, :], c == 0, c == 7, [B["HT"][tb], B["WS0"]], [pb])
            for c in range(8):
                yield self.mm(ps[:, 384:416], self.hT[:, c, t * 128:(t + 1) * 128], wkr[:, c, :], c == 0, c == 7, [B["HT"][tb], B["WS1"]], [pb])
            raw = u["raw"]
            yield self.cp("act", raw, ps[:, 0:416], [pb], [u["rawb"]])
            ss, sk = self.stat(2, f"m{u['p']}")
            st_[t] = (ss, sk)
            junk = u["lat"]
            yield self.act(junk[:, 0:256], raw[:, 0:256], AF.Square, [u["rawb"]], [u["latb"], sk], accum_out=ss[:, 0:1])
            yield self.act(junk[:, 256:384], raw[:, 256:384], AF.Square, [u["rawb"]], [u["latb"], sk], accum_out=ss[:, 1:2])
            yield self.act(ss[:, 0:1], ss[:, 0:1], AF.Sqrt, [sk], [sk], scale=1.0 / 256.0, bias=EPS)
            yield self.act(ss[:, 1:2], ss[:, 1:2], AF.Sqrt, [sk], [sk], scale=1.0 / 128.0, bias=EPS)
            yield self.recip(ss, ss, [sk], [sk])
            lat = u["lat"]
            yield self.stt(lat[:, 0:256], raw[:, 0:256], ss[:, 0:1], self.GMLA[:, 0:256], ALU.mult, ALU.mult,
                     [u["rawb"], sk, B["GMLA"]], [u["latb"]])
            yield self.stt(lat[:, 256:384], raw[:, 256:384], ss[:, 1:2], self.GMLA[:, 256:384], ALU.mult, ALU.mult,
                     [u["rawb"], sk, B["GMLA"]], [u["latb"]])

        def stage_b(t):
            u = bufs(t)
            st_.pop(t)
            lat, raw = u["lat"], u["raw"]
            tbk = B["PSB"][u["tbank"]]
            psb = self.PS[:, u["tbank"], :].bitcast(BF16)
            for j in range(3):
                yield self.tr(psb[:, j * 128:(j + 1) * 128], lat[:, j * 128:(j + 1) * 128], [u["latb"], B["CONST"]], [tbk])
            yield self.cp("dve", u["lt"].rearrange("p j n -> p (j n)"), psb[:, 0:384], [tbk], [u["ltb"]])
            pq = self.PS[:, u["qbank"], :]
            pqb = B["PSB"][u["qbank"]]
            for j in range(2):
                yield self.mm(pq[:, 0:192], u["lt"][:, j, :], wuq[:, j, pr * 192:(pr + 1) * 192], j == 0, j == 1, [u["ltb"], B["WS1"]], [pqb])
            yield self.mm(pq[:, 192:448], u["lt"][:, 2, :], wukv[:, pr * 256:(pr + 1) * 256], True, True, [u["ltb"], B["WS1"]], [pqb])
            dst = self.VA[:, 0, t, :].rearrange("p (a b) -> p a b", b=64)[:, 0:3:2, :]
            srcv = pq[:, 192:448].rearrange("p (h a b) -> p h a b", h=2, a=2)[:, :, 1, :]
            yield self.cp("dve", dst, srcv, [pqb], [B["VA"][0][t]])
            qk4 = u["qk"].rearrange("p (x d) -> p x d", d=96)
            yield self.cp("act", qk4[:, 0:2, 0:64], pq[:, 0:192].rearrange("p (h d) -> p h d", d=96)[:, :, 0:64], [pqb], [u["qkb"]])
            yield self.cp("act", qk4[:, 2:4, 0:64], pq[:, 192:448].rearrange("p (h a b) -> p h a b", h=2, a=2)[:, :, 0, :], [pqb], [u["qkb"]])
            rin = u["rin"].rearrange("p (x d) -> p x d", d=32)
            yield self.cp("dve", rin[:, 0:2, :], pq[:, 0:192].rearrange("p (h d) -> p h d", d=96)[:, :, 64:96], [pqb], [u["rinb"]])
            yield self.cp("dve", rin[:, 2, :], raw[:, 384:416], [u["rawb"]], [u["rinb"]])

        def stage_c(t):
            u = bufs(t)
            tbk = B["PSB"][u["tbank"]]
            psb = self.PS[:, u["tbank"], :].bitcast(BF16)
            qk4 = u["qk"].rearrange("p (x d) -> p x d", d=96)
            rin = u["rin"].rearrange("p (x d) -> p x d", d=32)
            rin4 = u["rin"].rearrange("p (x r d) -> p x r d", x=3, r=2)
            C = self.ROPM[:, t, 0:32].unsqueeze(1).broadcast_to([128, 3, 32])
            Sm = self.ROPM[:, t, 32:64].rearrange("p (r d) -> p r d", r=2)
            ra = u["ra"].rearrange("p (x d) -> p x d", d=32)
            rbm = u["rb"].rearrange("p (x r d) -> p x r d", x=3, r=2)
            rb3 = u["rb"].rearrange("p (x d) -> p x d", d=32)
            yield self.tt("dve", ra, rin, C, ALU.mult, [u["rinb"], B["ROP"]], [u["rab"]])
            for r in range(2):
                yield self.tt("dve", rbm[:, :, r, :], rin4[:, :, 1 - r, :], Sm[:, r, :].unsqueeze(1).broadcast_to([128, 3, 16]),
                        ALU.mult, [u["rinb"], B["ROP"]], [u["rab"]])
            yield self.tt("dve", qk4[:, 0:2, 64:96], ra[:, 0:2, :], rb3[:, 0:2, :], ALU.add, [u["rab"]], [u["qkb"]])
            for hk in range(2):
                yield self.tt("dve", qk4[:, 2 + hk, 64:96], ra[:, 2, :], rb3[:, 2, :], ALU.add, [u["rab"]], [u["qkb"]])
            for x4 in range(4):
                yield self.tr(psb[0:96, 512 + x4 * 128:512 + (x4 + 1) * 128], qk4[:, x4, :], [u["qkb"], B["CONST"]], [tbk])
            for hh in range(2):
                yield self.cp("dve", self.QT[0:96, hh, t * 128:(t + 1) * 128], psb[0:96, 512 + hh * 128:512 + (hh + 1) * 128],
                        [tbk], [B[f"QT{hh}"]])
                yield self.cp("act", self.KT[0:96, hh, t * 128:(t + 1) * 128], psb[0:96, 512 + (2 + hh) * 128:512 + (3 + hh) * 128],
                        [tbk], [B[f"KT{hh}"]])

        self.pipeline([stage_a, stage_b, stage_c])
        self.fence_scr()
        P.tag = P.tag[:4] + "a"
        for hh in range(2):
            orows = slice(hh * 64, hh * 64 + 64)

            def outf(qb, bank, hh=hh, orows=orows):
                self.normalize(bank, hh, self.mixT[orows, chunk, qb * 512:(qb + 1) * 512], [B["MIX"][chunk][qb]])

            self.attn_dense(lambda qb, hh=hh: self.QT[0:96, hh, qb * 512:(qb + 1) * 512],
                            lambda kt, hh=hh: self.KT[0:96, hh, kt * 128:(kt + 1) * 128],
                            lambda kt, hh=hh: (self.VA[:, 0, kt, hh * 64:hh * 64 + 128], B["VA"][0][kt]),
                            96, scale, None, outf, [B[f"QT{hh}"], B[f"KT{hh}"]])

    def post_residual(self, t, b0):
        for _ in self.post_gen(t, b0, 0):
            pass

    def post_gen(self, t, b0, par):
        B = self.B
        fps = self.PS[:, b0:b0 + 2, :].rearrange("p a n -> p (a n)")
        pbs = [B["PSB"][b0], B["PSB"][b0 + 1]]
        if par == 0:
            junk, jb = self.SCRB[:].rearrange("p a n -> p (a n)"), [B["SCRB0"], B["SCRB1"]]
            tmp, tb_ = self.SCR[:, 0:2, :].rearrange("p a n -> p (a n)"), [B["SCR"][0], B["SCR"][1]]
            key = "p0"
        else:
            junk, jb = self.PT[:, 0:2, :].rearrange("p a n -> p (a n)"), [B["PT"][0], B["PT"][1]]
            tmp, tb_ = self.SCR[:, 2:4, :].rearrange("p a n -> p (a n)"), [B["SCR"][2], B["SCR"][3]]
            key = "p1"
        ss, sk = self.stat(1, key)
        yield self.act(junk, fps, AF.Square, pbs, jb + [sk], accum_out=ss)
        yield self.tt("dve", tmp, fps, self.GB[:], ALU.mult, pbs + [B["GB"]], tb_)
        yield self.act(ss, ss, AF.Sqrt, [sk], [sk], scale=1.0 / D, bias=EPS)
        yield self.recip(ss, ss, [sk], [sk])
        yield self.stt(self.x_sb[:, t, :], tmp, ss, self.x_sb[:, t, :], ALU.mult, ALU.add, tb_ + [sk, B["X"][t]], [B["X"][t]])

    def wo_residual(self, l):
        P, B = self.P, self.B
        wsrc = self.sc["wo"][l].rearrange("p (c n) -> p c n", c=8)
        w0 = self.WS[:, 0, :].rearrange("p (c n) -> p c n", c=8)
        w1 = self.WS[:, 1, :].rearrange("p (c n) -> p c n", c=8)
        w2 = self.TAB[:, 0:2048].rearrange("p (c n) -> p c n", c=8)
        self.dma("sp", w0, wsrc[:, :, 0:384], [B["SC"]["wo"][l]], [B["WS0"]], "ws0")
        self.dma("sp", w1, wsrc[:, :, 384:768], [B["SC"]["wo"][l]], [B["WS1"]], "ws1")
        self.dma("sp", w2, wsrc[:, :, 768:1024], [B["SC"]["wo"][l]], B["TABS"][0:2], "tab")
        segs = ((0, 384, w0, 0, B["WS0"]), (384, 128, w1, 0, B["WS1"]), (512, 256, w1, 128, B["WS1"]), (768, 256, w2, 0, B["TABS"][0]))
        self.fence_scr()
        prev = None
        for pi in range(NT // 2):
            tiles = (2 * pi, 2 * pi + 1)
            for t in tiles:
                b0 = 0 if t % 2 == 0 else 2
                tb = t // 4
                for (o0, n, wt, wo_, wb) in segs:
                    bank = b0 + o0 // 512
                    col = o0 % 512
                    for c in range(8):
                        self.mm(self.PS[:, bank, col:col + n], self.mixT[:, c, t * 128:(t + 1) * 128], wt[:, c, wo_:wo_ + n],
                                c == 0, c == 7, [B["MIX"][c][tb], wb, B["TABS"][1]], [B["PSB"][bank]])
            gens = [self.post_gen(t, 0 if t % 2 == 0 else 2, t % 2) for t in tiles]
            if self.do_ffn and prev is not None:
                gens += [self.norm_gen(t, NL + l, 6 + (t % 2), t % 2) for t in prev]
            self.run_gens(gens)
            prev = tiles
        if self.do_ffn:
            self.run_gens([self.norm_gen(t, NL + l, 6 + (t % 2), t % 2) for t in prev])
        self.fence_scr()

    def ffn(self, s, l, last):
        P, B = self.P, self.B
        self.dma("sp", self.GB[:], self.I["g_post"][l, 1:2, :].broadcast_to([128, 1024]), [], [B["GB"]], "gb")
        actT = self.mixT[:].rearrange("p c n -> p (c n)")[:, 0:NJ * 512].rearrange("p (j n) -> p j n", j=NJ)
        abufs = [b for c in range(8) for b in B["MIX"][c]]
        gcnt = 0
        for tb in range(4):
            for j in range(NJ):
                slot = gcnt % 2
                gcnt += 1
                wv = self.WS[:, slot, 0:2048].rearrange("p (c n) -> p c n", c=8)
                self.dma("sp", self.WS[:, slot, 0:2048], self.sc["gu"][l, j], [B["SC"]["gu"][l]], [B[f"WS{slot}"]], f"ws{slot}")
                gb = 4 + 2 * (j % 2)
                ub = gb + 1
                for c in range(8):
                    self.mm(self.PS[:, gb, :], wv[:, c, 0:128], self.hT[:, c, tb * 512:(tb + 1) * 512], c == 0, c == 7,
                            [B["HT"][tb], B[f"WS{slot}"]], [B["PSB"][gb]])
                for c in range(8):
                    self.mm(self.PS[:, ub, :], wv[:, c, 128:256], self.hT[:, c, tb * 512:(tb + 1) * 512], c == 0, c == 7,
                            [B["HT"][tb], B[f"WS{slot}"]], [B["PSB"][ub]])
                si = 2 + (j % 2)
                sg = self.SCR[:, si, :]
                self.act(sg, self.PS[:, gb, :], AF.Silu, [B["PSB"][gb]], [B["SCR"][si]])
                self.tt("dve", actT[:, j, :], self.PS[:, ub, :], sg, ALU.mult, [B["PSB"][ub], B["SCR"][si]], [self.abuf(j)])
            for j in range(NJ):
                ts_ = j % 4
                wdt = self.TAB[:, ts_ * 1024:(ts_ + 1) * 1024]
                self.dma("sp", wdt, self.sc["wd"][l, j], [B["SC"]["wd"][l]], [B["TABS"][ts_]], f"tabs{ts_}")
                for tt_ in range(4):
                    for ch in range(2):
                        bank = tt_ * 2 + ch
                        self.mm(self.PS[:, bank, :], actT[:, j, tt_ * 128:(tt_ + 1) * 128], wdt[:, ch * 512:(ch + 1) * 512],
                                j == 0, j == NJ - 1, [self.abuf(j), B["TABS"][ts_]], [B["PSB"][bank]])
            if tb == 0:
                self.fence_scr()
            prev = None
            for half in range(2):
                tl = (tb * 4 + 2 * half, tb * 4 + 2 * half + 1)
                gens = [self.post_gen(t, (t % 4) * 2, t % 2) for t in tl]
                if prev is not None and not last and self.next_l is not None:
                    gens += [self.norm_gen(t, self.next_l, t % 2, t % 2) for t in prev]
                self.run_gens(gens)
                if last:
                    for t in tl:
                        self.store_x(s, t)
                        if s + 1 < self.nseq:
                            self.dma("pool", self.x_sb[:, t, :], self.I["x"][s + 1, t * 128:(t + 1) * 128, :], [], [B["X"][t]], f"x{t}")
                    if s + 1 < self.nseq:
                        self.run_gens([self.norm_gen(t, self.layers[0], t % 2, t % 2) for t in tl])
                prev = tl
            if not last and self.next_l is not None:
                self.run_gens([self.norm_gen(t, self.next_l, t % 2, t % 2) for t in prev])
            if tb == 3:
                self.fence_scr()
                if (not last and self.next_l is not None) or (last and s + 1 < self.nseq):
                    self.norm_done = True
                if last and s + 1 < self.nseq:
                    self.x_loaded = True

    def abuf(self, j):
        return self.B["MIX"][j // 4][j % 4]


_CACHE = {}


def kernel(**inputs):
    ncores = 8
    x = np.ascontiguousarray(np.asarray(inputs["x"], np.float32))
    nb = x.shape[0]
    per = nb // ncores
    shared = prep_shared(inputs)
    key = ("full", per)
    if key not in _CACHE:
        _CACHE[key] = Builder(per, range(NL)).build()
    nc = _CACHE[key]
    in_maps = []
    for c in range(ncores):
        m = dict(shared)
        m["x"] = x[c * per:(c + 1) * per]
        m["w1"] = shared["w1"].reshape(NL, 128, 8 * 2464)
        m["wo"] = shared["wo"].reshape(NL, 128, 8 * 1024)
        m["uq"] = shared["uq"].reshape(NL, 128, 768)
        m["g_pre"] = shared["g_pre"].reshape(128, 2 * NL * 8)
        m["rope_g"] = shared["rope_g"].reshape(128, 16 * 128)
        m["rope_m"] = shared["rope_m"].reshape(128, 16 * 64)
        in_maps.append(m)
    res = run_bass_kernel_spmd(nc, in_maps, core_ids=list(range(ncores)))
    out = np.concatenate([np.asarray(r["y"]) for r in res.results], axis=0)
    return out.astype(np.float32)
```

```python
import math
import contextlib
import numpy as np
import concourse.bass as bass
import concourse.mybir as mybir
from concourse.bass_utils import run_bass_kernel_spmd

F32 = mybir.dt.float32
BF16 = mybir.dt.bfloat16
AF = mybir.ActivationFunctionType
ALU = mybir.AluOpType
AX = mybir.AxisListType

S = 2048
D = 1024
NT = 16
DFF = 2816
NJ = 22
NL = 4
EPS = 1e-6
NEG = -30000.0
GW = 3968


class Buf:
    __slots__ = ("name", "w", "r", "excl")

    def __init__(self, name, excl=False):
        self.name = name
        self.w = {}
        self.r = {}
        self.excl = excl


class Op:
    __slots__ = ("eng", "fn", "deps", "sig", "sigval", "dma", "dmaval", "tag")


class Prog:
    ENGS = ("pe", "act", "dve", "pool", "sp")

    def __init__(self, nc):
        self.nc = nc
        self.ops = {e: [] for e in self.ENGS}
        self.dma_cnt = {}
        self.n = 0
        self.tag = ""
        self.names = {}

    def buf(self, name=None):
        self.n += 1
        return Buf(name or f"b{self.n}")

    def bufs(self, n, name="b"):
        return [self.buf(f"{name}{i}") for i in range(n)]

    def add(self, eng, fn, reads=(), writes=(), dma=None):
        op = Op()
        op.eng = eng
        op.fn = fn
        op.sig = False
        op.sigval = 0
        op.dma = dma
        op.dmaval = 0
        op.tag = self.tag
        if dma is not None:
            self.dma_cnt[dma] = self.dma_cnt.get(dma, 0) + 16
            op.dmaval = self.dma_cnt[dma]
        deps = {}
        for b in reads:
            for d in b.w.values():
                deps[d] = True
            if b.excl:
                for k2, d in b.r.items():
                    if k2 != eng:
                        deps.setdefault(d, False)
        for b in writes:
            for d in b.w.values():
                deps.setdefault(d, False)
            for d in b.r.values():
                deps.setdefault(d, False)
        op.deps = deps
        key = ("dma", dma) if dma is not None else eng
        for b in writes:
            b.w[key] = op
        for b in reads:
            b.r[key] = op
        self.ops[eng].append(op)
        return op

    @staticmethod
    def _skip(d, raw, ename):
        if d.dma is not None:
            return False
        if d.eng != ename:
            return False
        return ename == "pe"

    def emit(self, es):
        nc = self.nc
        for e in self.ENGS:
            for op in self.ops[e]:
                for d, raw in op.deps.items():
                    if d.dma is None and not self._skip(d, raw, e):
                        d.sig = True
        for e in self.ENGS:
            c = 0
            for op in self.ops[e]:
                if op.sig and op.dma is None:
                    c += 1
                    op.sigval = c
        engsem = {e: es.enter_context(nc.semaphore(f"sem_{e}")) for e in self.ENGS}
        dmasem = {k: es.enter_context(nc.semaphore(f"dsem_{k}")) for k in self.dma_cnt}
        block = es.enter_context(nc.Block())

        def run(ename):
            def body(eng):
                waited = {}
                for op in self.ops[ename]:
                    need = {}
                    for d, raw in op.deps.items():
                        if d.dma is not None:
                            key = ("d", d.dma)
                            v = d.dmaval
                        else:
                            if self._skip(d, raw, ename):
                                continue
                            key = ("e", d.eng)
                            v = d.sigval
                        if need.get(key, 0) < v:
                            need[key] = v
                    for key, v in need.items():
                        if waited.get(key, 0) < v:
                            sem = dmasem[key[1]] if key[0] == "d" else engsem[key[1]]
                            eng.wait_ge(sem, v)
                            waited[key] = v
                    if op.fn is None:
                        continue
                    inst = op.fn(eng)
                    try:
                        self.names[inst.ins.name] = op.tag
                    except Exception:
                        pass
                    if op.dma is not None:
                        inst.then_inc(dmasem[op.dma], 16)
                    elif op.sig:
                        inst.then_inc(engsem[ename], 1)

            return body

        block.tensor(run("pe"))
        block.scalar(run("act"))
        block.vector(run("dve"))
        block.gpsimd(run("pool"))
        block.sync(run("sp"))


def _rng(a, n):
    return list(range(a, a + n))


def w_in_perm():
    p = []
    for pr in range(2):
        p += _rng(0 + pr * 128, 128) + _rng(256 + pr * 128, 128) + _rng(512 + pr * 128, 128)
    for pr in range(2):
        p += _rng(768 + pr * 128, 128) + _rng(1024 + pr * 128, 128) + _rng(1280 + pr * 128, 128)
    for g in range(2):
        p += _rng(1536 + g * 128, 128) + _rng(1792 + g * 64, 64) + _rng(1920 + g * 64, 64)
    p += _rng(2048, 384) + _rng(2432, 32)
    return np.asarray(p)


OFF_NA = (0, 384)
OFF_DF = (768, 1152)
OFF_GQ = (1536, 1792)
OFF_MLA = 2048
OFF_KR = 2432

NA_CLASSES = ((0, 0), (1, 0), (2, 0), (3, 0), (4, 0), (29, 24), (30, 24), (31, 24))


def na_class(r):
    if r <= 3:
        return r
    if r >= 29:
        return 5 + (r - 29)
    return 4


def na_index_tables():
    p = np.arange(128)[:, None, None, None]
    cls_r = np.asarray([c[0] for c in NA_CLASSES])[None, :, None, None]
    cls_rs = np.asarray([c[1] for c in NA_CLASSES])[None, :, None, None]
    t = np.arange(4)[None, None, :, None]
    c = np.arange(64)[None, None, None, :]
    kr = cls_rs + 2 * t + p // 64
    kc = p % 64
    dr = kr - cls_r
    dc = kc - c
    cs = np.clip(c - 8, 0, 48)
    valid = (kc >= cs) & (kc < cs + 16)
    valid = np.broadcast_to(valid, (128, 8, 4, 64))
    dri = np.broadcast_to(dr + 7, (128, 8, 4, 64))
    dci = np.clip(np.broadcast_to(dc + 15, (128, 8, 4, 64)), 0, 30)
    return dri, dci, valid


def host_constants():
    c = {}
    slopes = [2.0 ** (-8.0 * (i + 1) / 4) for i in range(4)]
    pp = np.arange(128, dtype=np.float64)[:, None]
    cc = np.arange(GW, dtype=np.float64)[None, :]
    g = np.stack([np.exp(-s * np.abs(cc - pp - 1920.0)) for s in slopes]).astype(np.float32)
    c["alibi_g"] = g
    pos = np.arange(S)
    row = (pos // 64).astype(np.float32)
    col = (pos % 64).astype(np.float32)
    inv32 = (10000.0 ** (-np.arange(0, 32, 2, dtype=np.float32) / 32)).astype(np.float32)

    def cs_tables(p):
        ang = p[:, None] * inv32[None, :]
        cosv = np.cos(ang).astype(np.float32)
        sinv = np.sin(ang).astype(np.float32)
        return np.concatenate([cosv, cosv], 1), np.concatenate([-sinv, sinv], 1)

    cr, sr = cs_tables(row)
    cc_, sc_ = cs_tables(col)
    cg = np.concatenate([cr, cc_], 1)
    sg = np.concatenate([sr, sc_], 1)
    cm, sm = cs_tables(pos.astype(np.float32))

    def tm(a):
        return np.ascontiguousarray(a.reshape(NT, 128, -1).transpose(1, 0, 2))

    c["rope_g"] = np.concatenate([tm(cg), tm(sg)], 2)
    c["rope_m"] = np.concatenate([tm(cm), tm(sm)], 2)
    c["ident"] = np.eye(128, dtype=np.float32)
    c["ident8"] = (8.0 * np.eye(128)).astype(np.float32)
    return c


def prep_shared(inp):
    o = {}
    w_in = np.asarray(inp["w_in"], np.float32)
    perm = w_in_perm()
    w1 = w_in[:, :, perm].reshape(NL, 8, 128, 2464).transpose(0, 2, 1, 3)
    o["w1"] = np.ascontiguousarray(w1)
    o["wo"] = np.ascontiguousarray(np.asarray(inp["w_o"], np.float32).reshape(NL, 8, 128, 1024).transpose(0, 2, 1, 3))
    gu = np.asarray(inp["ffn_w_gate_up"], np.float32).reshape(NL, 8, 128, 2, NJ, 128)
    o["gu"] = np.ascontiguousarray(gu.transpose(0, 4, 2, 1, 3, 5)).reshape(NL, NJ, 128, 8 * 256)
    o["wd"] = np.ascontiguousarray(np.asarray(inp["ffn_w_down"], np.float32).reshape(NL, NJ, 128, 1024))
    o["uq"] = np.ascontiguousarray(np.asarray(inp["mla_w_uq"], np.float32).reshape(NL, 2, 128, 384).transpose(0, 2, 1, 3))
    o["ukv"] = np.ascontiguousarray(np.asarray(inp["mla_w_ukv"], np.float32))

    def pc(a):
        return np.ascontiguousarray(np.asarray(a, np.float32).reshape(NL, 8, 128).transpose(2, 0, 1))

    o["g_pre"] = np.ascontiguousarray(np.concatenate([pc(inp["pre_mix_norm"]), pc(inp["pre_ffn_norm"])], 1))
    o["g_post"] = np.ascontiguousarray(np.stack([np.asarray(inp["post_mix_norm"], np.float32),
                                                 np.asarray(inp["post_ffn_norm"], np.float32)], 1))
    qn = np.asarray(inp["gqa_q_norm"], np.float32)
    kn = np.asarray(inp["gqa_k_norm"], np.float32)
    o["g_gqa"] = np.ascontiguousarray(np.concatenate([qn, qn, kn, kn], 1))
    o["g_mla"] = np.ascontiguousarray(np.concatenate([np.asarray(inp["mla_q_norm"], np.float32),
                                                      np.asarray(inp["mla_kv_norm"], np.float32)], 1))
    sub = np.asarray(inp["diff_subln"], np.float32)
    o["g_sub"] = np.ascontiguousarray(np.concatenate([sub, sub], 1).T)
    lamv = np.stack([np.asarray(inp[k], np.float32) for k in
                     ("diff_lambda_q1", "diff_lambda_k1", "diff_lambda_q2", "diff_lambda_k2")], 0)
    o["lam_in"] = np.ascontiguousarray(lamv.reshape(1, 4 * NL * 32))
    rb = np.asarray(inp["na_rel_bias"], np.float32)
    dri, dci, valid = na_index_tables()
    nab = np.empty((NL, 2, 128, 2, 8, 4, 64), np.float32)
    for l in range(NL):
        for h in range(4):
            tbl = rb[l, h][dri, dci]
            tbl = np.where(valid, tbl, np.float32(NEG))
            nab[l, h // 2, :, h % 2] = tbl
    o["nab"] = nab.reshape(NL, 2, 128, 4096)
    o.update(host_constants())
    return o


def lambda_init(l):
    return 0.8 - 0.6 * math.exp(-0.3 * l)


class Builder:
    def __init__(self, nseq, layers, mixers=(0, 1, 2, 3), do_ffn=True, dbg=None):
        self.nseq = nseq
        self.layers = list(layers)
        self.mixers = mixers
        self.do_ffn = do_ffn
        self.dbg = dbg

    def mm(self, out, lhsT, rhs, start, stop, reads, writes):
        self.P.add("pe", lambda e: e.matmul(out, lhsT=lhsT, rhs=rhs, start=start, stop=stop), reads, writes)

    def tr(self, out, in_, reads, writes):
        idn = self.ident[0:in_.shape[0], 0:in_.shape[0]]
        self.P.add("pe", lambda e: e.transpose(out, in_, idn), reads, writes)

    def act(self, out, in_, func, reads, writes, **kw):
        self.P.add("act", lambda e: e.activation(out=out, in_=in_, func=func, **kw), reads, writes)

    def ts(self, eng, out, in0, s1, s2, op0, op1, reads, writes):
        if op1 is None:
            self.P.add(eng, lambda e: e.tensor_scalar(out=out, in0=in0, scalar1=s1, scalar2=None, op0=op0), reads, writes)
        else:
            self.P.add(eng, lambda e: e.tensor_scalar(out=out, in0=in0, scalar1=s1, scalar2=s2, op0=op0, op1=op1), reads, writes)

    def tt(self, eng, out, in0, in1, op, reads, writes):
        self.P.add(eng, lambda e: e.tensor_tensor(out=out, in0=in0, in1=in1, op=op), reads, writes)

    def stt(self, out, in0, scalar, in1, op0, op1, reads, writes):
        self.P.add("dve", lambda e: e.scalar_tensor_tensor(out=out, in0=in0, scalar=scalar, in1=in1, op0=op0, op1=op1), reads, writes)

    def cp(self, eng, out, in_, reads, writes):
        if eng == "act":
            self.P.add("act", lambda e: e.activation(out=out, in_=in_, func=AF.Copy), reads, writes)
        else:
            self.P.add(eng, lambda e: e.tensor_copy(out=out, in_=in_), reads, writes)

    def dma(self, q, out, in_, reads, writes, sem):
        self.P.add(q, lambda e: e.dma_start(out=out, in_=in_), reads, writes, dma=sem)

    def recip(self, out, in_, reads, writes):
        self.P.add("dve", lambda e: e.reciprocal(out=out, in_=in_), reads, writes)

    def build(self):
        nc = bass.Bass("TRN2", target_bir_lowering=False)
        self.nc = nc
        ns = self.nseq
        dt = nc.dram_tensor
        I = {}

        def inp(name, shape, dtype=F32):
            I[name] = dt(name, list(shape), dtype, kind="ExternalInput").ap()

        inp("x", [ns, S, D])
        inp("w1", [NL, 128, 8 * 2464])
        inp("wo", [NL, 128, 8 * 1024])
        inp("gu", [NL, NJ, 128, 2048])
        inp("wd", [NL, NJ, 128, 1024])
        inp("uq", [NL, 128, 768])
        inp("ukv", [NL, 128, 512])
        inp("g_pre", [128, 2 * NL * 8])
        inp("g_post", [NL, 2, 1024])
        inp("g_gqa", [NL, 256])
        inp("g_mla", [NL, 384])
        inp("g_sub", [128, NL])
        inp("lam_in", [1, 4 * NL * 32])
        inp("nab", [NL, 2, 128, 4096])
        inp("alibi_g", [4, 128, GW])
        inp("rope_g", [128, 16 * 128])
        inp("rope_m", [128, 16 * 64])
        inp("ident", [128, 128])
        inp("ident8", [128, 128])
        self.I = I
        y = dt("y", [ns, S, D], F32, kind="ExternalOutput").ap()
        self.y = y
        if self.dbg:
            self.dbg_out = dt("dbg", list(self.dbg[1]), self.dbg[2], kind="ExternalOutput").ap()
        sc = {}
        for name, shape in (("w1", [NL, 128, 8 * 2464]), ("wo", [NL, 128, 8 * 1024]), ("gu", [NL, NJ, 128, 2048]),
                            ("wd", [NL, NJ, 128, 1024]), ("uq", [NL, 128, 768]), ("ukv", [NL, 128, 512]),
                            ("nab", [NL, 2, 128, 4096]), ("alibi_g", [4, 128, GW])):
            sc[name] = dt("sc_" + name, shape, BF16, kind="Internal").ap()
        self.sc = sc

        with contextlib.ExitStack() as es:
            P = Prog(nc)
            self.P = P
            sb = lambda name, shape, dtype: es.enter_context(nc.sbuf_tensor(name, list(shape), dtype))
            self.x_sb = sb("x_sb", [128, NT, D], F32)
            self.hT = sb("hT", [128, 8, S], BF16)
            self.mixT = sb("mixT", [128, 8, S], BF16)
            self.QT = sb("QT", [128, 2, S], BF16)
            self.KT = sb("KT", [128, 2, S], BF16)
            self.VA = sb("VA", [128, 2, NT, 192], BF16)
            self.TAB = sb("TAB", [128, 4096], BF16)
            self.NPT = 3
            self.PT = sb("PT", [128, self.NPT, 512], BF16)
            self.WS = sb("WS", [128, 2, 8 * 384], BF16)
            self.GB = sb("GB", [128, 1024], F32)
            self.SCR = sb("SCR", [128, 4, 512], F32)
            self.SCRB = sb("SCRB", [128, 2, 512], BF16)
            self.HN = sb("HN", [128, 1, 1024], BF16)
            self.RB = sb("RB", [128, 1, 512], F32)
            self.ROPG = sb("ROPG", [128, 16, 128], BF16)
            self.ROPM = sb("ROPM", [128, 16, 64], BF16)
            self.ident = sb("identb", [128, 128], BF16)
            self.ident8 = sb("ident8b", [128, 128], BF16)
            self.onesb = sb("onesb", [128, 128], BF16)
            self.onesf = sb("onesf", [1, 128], F32)
            self.GPRE = sb("GPRE", [128, 2 * NL * 8], F32)
            self.GMLA = sb("GMLA", [128, 384], F32)
            self.GGQA = self.GMLA
            self.GSUB = sb("GSUB", [128, NL], F32)
            self.NLAM = sb("NLAM", [128, NL], F32)
            self.STAT = sb("STAT", [128, 64], F32)
            self.LT = sb("LT", [128, 3, 128], BF16)
            self.PS = es.enter_context(nc.psum_tensor("PS", [128, 8, 512], F32))

            B = {}
            self.B = B
            for k in ("QT0", "QT1", "KT0", "KT1", "TAB", "GB", "HN0", "HN1", "RB0", "RB1", "ROP", "CONST", "GPRE",
                      "GGQA", "GMLA", "GSUB", "NLAM", "STAT", "LT", "WS0", "WS1", "SCRB0", "SCRB1", "OUT", "DBG"):
                B[k] = P.buf(k)
            B["X"] = P.bufs(NT, "X")
            B["HT"] = P.bufs(4, "HT")
            B["MIX"] = [P.bufs(4, f"MIX{c}_") for c in range(8)]
            B["VA"] = [P.bufs(NT, f"VA{a}_") for a in range(2)]
            B["PT"] = P.bufs(self.NPT, "PT")
            B["PTX"] = P.bufs(6, "PTX")
            vflat = self.VA[:, 1, :, :].rearrange("p t n -> p (t n)")
            self.pt_base = [(self.PT[:, i, :], B["PT"][i]) for i in range(self.NPT)]
            self.pt_ext = self.pt_base + [(vflat[:, k * 512:(k + 1) * 512], B["PTX"][k]) for k in range(6)]
            self.pt_cur = self.pt_base
            B["SCR"] = P.bufs(4, "SCR")
            B["PSB"] = P.bufs(8, "PS")
            for b_ in B["PSB"]:
                b_.excl = True
            B["TABS"] = P.bufs(4, "TABS")
            B["SC"] = {k: [P.buf(f"sc_{k}{i}") for i in range(NL)] for k in sc}
            self.stat_pos = {}
            B["STATK"] = {k: P.buf("STAT_" + k) for k in self.STAT_KEYS}
            B["SCRH"] = [[P.buf(f"SCRH{s_}_{p}") for p in range(2)] for s_ in range(4)]
            B["LTP"] = P.bufs(2, "LTP")
            P.add("pool", lambda e: e.memset(self.STAT[:], 0.0), [], [B["STAT"]] + list(B["STATK"].values()))

            P.tag = "pro"
            self.prologue()
            for s in range(ns):
                if not getattr(self, "x_loaded", False):
                    self.load_x(s)
                self.x_loaded = False
                for li, l in enumerate(self.layers):
                    self.layer(s, l, last=(li == len(self.layers) - 1))
            if self.dbg:
                self.dbg[0](self)
            P.add("sp", None, reads=[B["OUT"], B["DBG"]])
            P.emit(es)
        return nc

    def prologue(self):
        P, B, I, sc = self.P, self.B, self.I, self.sc
        nc = self.nc

        def cast(name, l, src, dst, rows, cols):
            nch = max(1, (rows + 8191) // 8192)
            step = (rows + nch - 1) // nch
            for r0 in range(0, rows, step):
                r1 = min(rows, r0 + step)
                self.dma("pool", dst[r0:r1, :], src[r0:r1, :], [], [B["SC"][name][l]], f"c_{name}{l}")

        def flat(ap, cols):
            names = " ".join(f"d{i}" for i in range(ap.ndim))
            a = ap.rearrange(f"{names} -> ({names})")
            return a.rearrange("(r c) -> r c", c=cols)

        P.add("pool", lambda e: e.memset(self.onesb[:], 1.0), [], [B["CONST"]])
        P.add("pool", lambda e: e.memset(self.onesf[:], 1.0), [], [B["CONST"]])
        P.add("pool", lambda e: e.memset(self.VA[:], 1.0), [], [b for a in B["VA"] for b in a])
        self.dma("pool", self.ident[:], I["ident"], [], [B["CONST"]], "const")
        self.dma("pool", self.ident8[:], I["ident8"], [], [B["CONST"]], "const")
        self.dma("pool", self.ROPG[:].rearrange("p t n -> p (t n)"), I["rope_g"], [], [B["ROP"]], "rop")
        self.dma("pool", self.ROPM[:].rearrange("p t n -> p (t n)"), I["rope_m"], [], [B["ROP"]], "rop")
        self.dma("sp", self.GPRE[:], I["g_pre"], [], [B["GPRE"]], "gpre")
        self.dma("sp", self.GSUB[:], I["g_sub"], [], [B["GSUB"]], "gsub")
        self.dma("sp", self.SCR[0:1, 0, :], I["lam_in"], [], [B["SCR"][0], B["SCR"][1]], "lamr")
        for l in self.layers:
            cast("w1", l, flat(I["w1"][l], 1232), flat(sc["w1"][l], 1232), 2048, 1232)
            cast("nab", l, flat(I["nab"][l], 2048), flat(sc["nab"][l], 2048), 512, 2048)
            if l == self.layers[0]:
                cast("alibi_g", 0, flat(I["alibi_g"], 1984), flat(sc["alibi_g"], 1984), 1024, 1984)
            cast("uq", l, flat(I["uq"][l], 2048), flat(sc["uq"][l], 2048), 48, 2048)
            cast("ukv", l, flat(I["ukv"][l], 2048), flat(sc["ukv"][l], 2048), 32, 2048)
            cast("wo", l, flat(I["wo"][l], 2048), flat(sc["wo"][l], 2048), 512, 2048)
            cast("gu", l, flat(I["gu"][l], 2048), flat(sc["gu"][l], 2048), NJ * 128, 2048)
            cast("wd", l, flat(I["wd"][l], 2048), flat(sc["wd"][l], 2048), 1408, 2048)
        L = self.SCR[0:1, 0:2, :].rearrange("p a n -> p (a n)")
        n = NL * 32
        self.tt("dve", L[:, 512:512 + n], L[:, 0:n], L[:, n:2 * n], ALU.mult, [B["SCR"][0], B["SCR"][1]], [B["SCR"][0], B["SCR"][1]])
        self.tt("dve", L[:, 0:n], L[:, 2 * n:3 * n], L[:, 3 * n:4 * n], ALU.mult, [B["SCR"][0], B["SCR"][1]], [B["SCR"][0], B["SCR"][1]])
        P.add("dve", lambda e: e.tensor_reduce(out=L[:, 256:256 + NL], in_=L[:, 512:512 + n].rearrange("p (l k) -> p l k", k=32),
                                               axis=AX.X, op=ALU.add), [B["SCR"][0], B["SCR"][1]], [B["SCR"][0], B["SCR"][1]])
        P.add("dve", lambda e: e.tensor_reduce(out=L[:, 264:264 + NL], in_=L[:, 0:n].rearrange("p (l k) -> p l k", k=32),
                                               axis=AX.X, op=ALU.add), [B["SCR"][0], B["SCR"][1]], [B["SCR"][0], B["SCR"][1]])
        self.act(L[:, 272:272 + NL], L[:, 256:256 + NL], AF.Exp, [B["SCR"][0], B["SCR"][1]], [B["SCR"][0], B["SCR"][1]])
        self.act(L[:, 280:280 + NL], L[:, 264:264 + NL], AF.Exp, [B["SCR"][0], B["SCR"][1]], [B["SCR"][0], B["SCR"][1]])
        self.tt("dve", L[:, 288:288 + NL], L[:, 280:280 + NL], L[:, 272:272 + NL], ALU.subtract, [B["SCR"][0], B["SCR"][1]], [B["SCR"][0], B["SCR"][1]])
        for l in range(NL):
            self.ts("dve", L[:, 296 + l:297 + l], L[:, 288 + l:289 + l], -lambda_init(l), None, ALU.add, None,
                    [B["SCR"][0], B["SCR"][1]], [B["SCR"][0], B["SCR"][1]])
        ps = self.PS[:, 7, 0:NL]
        self.mm(ps, self.onesf[0:1, :], L[0:1, 296:296 + NL], True, True, [B["SCR"][0], B["SCR"][1], B["CONST"]], [B["PSB"][7]])
        self.cp("dve", self.NLAM[:], ps, [B["PSB"][7]], [B["NLAM"]])
        for l in range(NL):
            self.ts("dve", self.GSUB[:, l:l + 1], self.GSUB[:, l:l + 1], 1.0 - lambda_init(l), None, ALU.mult, None,
                    [B["GSUB"]], [B["GSUB"]])

    def load_x(self, s):
        for t in range(NT):
            self.dma("sp", self.x_sb[:, t, :], self.I["x"][s, t * 128:(t + 1) * 128, :], [], [self.B["X"][t]], f"x{t}")

    def store_x(self, s, t):
        self.dma("sp", self.y[s, t * 128:(t + 1) * 128, :], self.x_sb[:, t, :], [self.B["X"][t]], [self.B["OUT"]], f"y{t}")

    STAT_KEYS = {"norm": (0, 4), "n0": (32, 4), "n1": (36, 4), "post": (4, 4), "p0": (8, 4), "p1": (12, 4), "g0": (16, 4), "g1": (20, 4), "m0": (24, 4), "m1": (28, 4)}

    def stat(self, n=1, key="norm"):
        base, size = self.STAT_KEYS[key]
        i = self.stat_pos.get(key, 0)
        if i + n > size:
            i = 0
        self.stat_pos[key] = i + n
        return self.STAT[:, base + i:base + i + n], self.B["STATK"][key]

    def pipeline(self, stages, n=NT):
        npair = n // 2
        ns = len(stages)
        for step in range(npair + ns - 1):
            for si in reversed(range(ns)):
                pi = step - si
                if not (0 <= pi < npair):
                    continue
                gens = [stages[si](2 * pi), stages[si](2 * pi + 1)]
                while gens:
                    for g_ in list(gens):
                        try:
                            next(g_)
                        except StopIteration:
                            gens.remove(g_)

    def fence_scr(self):
        B = self.B
        toks = list(B["SCR"]) + [B["SCRH"][s_][p] for s_ in range(4) for p in range(2)] + [B["HN0"], B["LTP"][0], B["LTP"][1]]
        self.P.add("dve", lambda e: e.tensor_copy(out=self.STAT[:, 62:63], in_=self.STAT[:, 63:64]), toks, toks)

    def norm_to_hT(self, gcol):
        for t in range(NT):
            self.norm_tile(t, gcol, 6 + (t % 2))

    def norm_tile(self, t, gcol, bank):
        for _ in self.norm_gen(t, gcol, bank, 0):
            pass

    def norm_gen(self, t, gcol, bank, par):
        P, B = self.P, self.B
        gain = self.GPRE[:, gcol * 8:(gcol + 1) * 8].unsqueeze(2).broadcast_to([128, 8, 128])
        psb = self.PS[:, bank, :].bitcast(BF16)
        if par == 0:
            hn, hb, key = self.HN[:, 0, :], B["HN0"], "norm"
        else:
            hn, hb, key = self.QT[:, 0, 0:1024], B["QT0"], "n1"
        ss, sk = self.stat(1, key)
        yield self.act(hn, self.x_sb[:, t, :], AF.Square, [B["X"][t]], [hb, sk], accum_out=ss)
        yield self.act(ss, ss, AF.Sqrt, [sk], [sk], scale=1.0 / D, bias=EPS)
        yield self.recip(ss, ss, [sk], [sk])
        yield self.ts("dve", hn, self.x_sb[:, t, :], ss, None, ALU.mult, None, [B["X"][t], sk], [hb])
        for c in range(8):
            yield self.tr(psb[:, c * 128:(c + 1) * 128], hn[:, c * 128:(c + 1) * 128], [hb, B["CONST"]], [B["PSB"][bank]])
        yield self.tt("dve", self.hT[:, :, t * 128:(t + 1) * 128], psb.rearrange("p (c n) -> p c n", c=8), gain, ALU.mult,
                      [B["PSB"][bank], B["GPRE"]], [B["HT"][t // 4]])

    @staticmethod
    def run_gens(gens):
        gens = list(gens)
        while gens:
            for g_ in list(gens):
                try:
                    next(g_)
                except StopIteration:
                    gens.remove(g_)

    def load_ws(self, slot, name, l, off, ncols, total):
        src = self.sc[name][l].rearrange("p (c n) -> p c n", c=8)[:, :, off:off + ncols]
        dst = self.WS[:, slot, 0:8 * ncols].rearrange("p (c n) -> p c n", c=8)
        self.dma("sp", dst, src, [self.B["SC"][name][l]], [self.B[f"WS{slot}"]], f"ws{slot}")
        return dst

    def proj_fm(self, w, c0, m, evac):
        B = self.B
        for tb in range(4):
            bank = 5 + (tb % 2)
            ps = self.PS[0:m, bank, :]
            for c in range(8):
                self.mm(ps, w[:, c, c0:c0 + m], self.hT[:, c, tb * 512:(tb + 1) * 512], c == 0, c == 7,
                        [B["HT"][tb], self.wsbuf], [B["PSB"][bank]])
            evac(tb, self.PS, bank)

    def proj_tm(self, w, c0, n, evac, tok0=0, ntiles=NT, banks=(5, 6)):
        B = self.B
        for t in range(ntiles):
            bank = banks[t % len(banks)]
            ps = self.PS[:, bank, 0:n]
            a = tok0 + t * 128
            tbs = sorted({a // 512, (a + 127) // 512})
            for c in range(8):
                self.mm(ps, self.hT[:, c, a:a + 128], w[:, c, c0:c0 + n], c == 0, c == 7,
                        [B["HT"][i] for i in tbs] + [self.wsbuf], [B["PSB"][bank]])
            evac(t, ps, bank)

    def evac_v(self, al):
        B = self.B

        def f(t, ps, bank):
            dst = self.VA[:, al, t, :].rearrange("p (a b) -> p a b", b=64)[:, 0:3:2, :]
            src = ps.rearrange("p (a b) -> p a b", b=64)
            self.cp("dve", dst, src, [B["PSB"][bank]], [B["VA"][al][t]])
        return f

    def normalize(self, bank, hh, dst, dst_bufs, dst_eng="dve"):
        B = self.B
        orows = slice(0, 64) if hh == 0 else slice(64, 128)
        srows = slice(64, 128) if hh == 0 else slice(0, 64)
        rb = self.RB[srows, 0, :]
        rbb = B["RB0"]
        self.act(rb, self.PS[srows, bank, :], AF.Ln, [B["PSB"][bank]], [rbb])
        self.act(rb, rb, AF.Exp, [rbb], [rbb], scale=-1.0)
        self.tt("dve", dst, self.PS[orows, bank, :], rb, ALU.mult, [B["PSB"][bank], rbb], dst_bufs)

    def attn_dense(self, qf, kf, vf, krows, scale, tabf, outf, reads):
        B = self.B
        sbanks = (0, 1, 2, 5, 6)
        obanks = (3, 4)
        look = 4
        pts = self.pt_cur
        cnt = getattr(self, "acnt", 0)
        for qb in range(4):
            ob = obanks[qb % 2]

            def qk(kt):
                sbk = sbanks[(cnt + kt) % len(sbanks)]
                self.mm(self.PS[:, sbk, :], kf(kt), qf(qb), True, True, reads, [B["PSB"][sbk]])

            def rest(kt):
                i = cnt + kt
                sbk = sbanks[i % len(sbanks)]
                pt, ptb = pts[i % len(pts)]
                self.act(pt, self.PS[:, sbk, :], AF.Exp, [B["PSB"][sbk]], [ptb], scale=scale)
                v, vb = vf(kt)
                self.mm(self.PS[:, ob, :], v, pt, kt == 0, kt == NT - 1, [ptb, vb], [B["PSB"][ob]])

            for kt in range(min(look, NT)):
                qk(kt)
            for kt in range(NT):
                if kt + look < NT:
                    qk(kt + look)
                rest(kt)
            cnt += NT
            outf(qb, ob)
        self.acnt = cnt

    def layer(self, s, l, last):
        P, B, sc = self.P, self.B, self.sc
        P.tag = "norm1"
        if not getattr(self, "norm_done", False):
            self.norm_to_hT(l)
        self.norm_done = False
        self.cur_last = last
        self.next_l = None if last else self.layers[self.layers.index(l) + 1]
        self.dma("sp", self.GB[:], self.I["g_post"][l, 0:1, :].broadcast_to([128, 1024]), [], [B["GB"]], "gb")
        slot = 0
        if 0 in self.mixers:
            for pr in range(2):
                P.tag = f"na{pr}"
                self.mixer_na(l, pr, slot)
                slot ^= 1
        else:
            self.zero_mix(0)
        if 1 in self.mixers:
            for pr in range(2):
                P.tag = f"diff{pr}"
                self.mixer_diff(l, pr, slot)
                slot ^= 1
        else:
            self.zero_mix(1)
        if 2 in self.mixers:
            self.dma("sp", self.GMLA[:, 0:256], self.I["g_gqa"][l:l + 1, :].broadcast_to([128, 256]), [], [B["GMLA"]], "gmla")
            for g in range(2):
                P.tag = f"gqa{g}"
                self.mixer_gqa(l, g, slot)
                slot ^= 1
        else:
            self.zero_mix(2)
        if 3 in self.mixers:
            self.dma("sp", self.GMLA[:], self.I["g_mla"][l:l + 1, :].broadcast_to([128, 384]), [], [B["GMLA"]], "gmla")
            for pr in range(2):
                P.tag = f"mla{pr}"
                self.mixer_mla(l, pr)
        else:
            self.zero_mix(3)
        P.tag = "wo"
        self.wo_residual(l)
        if self.do_ffn:
            P.tag = "ffn"
            self.ffn(s, l, last)
        elif last:
            for t in range(NT):
                self.store_x(s, t)

    def fence_ptx(self):
        self.P.add("pool", lambda e: e.memset(self.VA[0:1, 1, 15, 190:192], 1.0), [], self.B["VA"][1] + self.B["PTX"])
        self.pt_cur = self.pt_ext

    def zero_mix(self, m):
        for c in (2 * m, 2 * m + 1):
            self.P.add("pool", lambda e, c=c: e.memset(self.mixT[:, c, :], 0.0), [], self.B["MIX"][c])

    def mixer_na(self, l, pr, slot):
        P, B = self.P, self.B
        w = self.load_ws(slot, "w1", l, OFF_NA[pr], 384, 2464)
        self.wsbuf = B[f"WS{slot}"]
        self.dma("sp", self.TAB[:], self.sc["nab"][l, pr], [B["SC"]["nab"][l]], B["TABS"], "tab")
        chunk = pr

        def evq(dstT, dbuf):
            def f(tb, PS, bank):
                self.cp("dve", dstT[:, 0, tb * 512:(tb + 1) * 512], PS[:, bank, :], [B["PSB"][bank]], [dbuf])
            return f
        P.add("pool", lambda e: e.memset(self.KT[64:128, 0, :], 0.0), [], [B["KT0"]])
        P.add("pool", lambda e: e.memset(self.KT[0:64, 1, :], 0.0), [], [B["KT1"]])
        P.add("pool", lambda e: e.memset(self.VA[:, 1, :, 64:128], 1.0), [], B["VA"][1] + B["PTX"])
        self.pt_cur = self.pt_base

        def evk(tb, PS, bank):
            self.cp("dve", self.KT[0:64, 0, tb * 512:(tb + 1) * 512], PS[0:64, bank, :], [B["PSB"][bank]], [B["KT0"]])
            self.cp("dve", self.KT[64:128, 1, tb * 512:(tb + 1) * 512], PS[64:128, bank, :], [B["PSB"][bank]], [B["KT1"]])
        self.proj_fm(w, 0, 128, evq(self.QT, B["QT0"]))
        self.proj_fm(w, 128, 128, evk)
        self.proj_tm(w, 256, 128, self.evac_v(0))
        self.proj_tm(w, 256, 128, self.evac_v(1), tok0=64, ntiles=NT - 1)
        scale = 1.0 / 8.0
        items = [(hh, qb, rp) for hh in range(2) for qb in range(4) for rp in range(4)]
        sbanks = (0, 1, 2, 5, 6)
        look = 2

        def info(idx):
            hh, qb, rp = items[idx]
            return hh, qb, rp, sbanks[idx % len(sbanks)], idx % self.NPT, (qb * 8 + rp * 2, qb * 8 + rp * 2 + 1)

        def front(idx):
            hh, qb, rp, sbk, pi, rows = info(idx)
            for ri, r in enumerate(rows):
                cls = na_class(r)
                tcol = (hh * 8 + cls) * 256
                self.mm(self.PS[:, sbk, ri * 256:(ri + 1) * 256], self.ident8[:], self.TAB[:, tcol:tcol + 256],
                        ri == 0, False, B["TABS"] + [B["CONST"]], [B["PSB"][sbk]])
            for ri, r in enumerate(rows):
                rs = min(max(r - 4, 0), 24)
                for t in range(4):
                    ks = (rs + 2 * t) * 64
                    self.mm(self.PS[:, sbk, ri * 256 + t * 64:ri * 256 + (t + 1) * 64], self.KT[:, hh, ks:ks + 128],
                            self.QT[:, 0, r * 64:(r + 1) * 64], False, (ri == 1 and t == 3),
                            [B["QT0"], B[f"KT{hh}"]], [B["PSB"][sbk]])

        def back(idx):
            hh, qb, rp, sbk, pi, rows = info(idx)
            ob = 3 + (qb % 2)
            pt, ptb = self.pt_base[pi]
            self.act(pt, self.PS[:, sbk, :], AF.Exp, [B["PSB"][sbk]], [ptb], scale=scale)
            for ri, r in enumerate(rows):
                rs = min(max(r - 4, 0), 24)
                al = rs % 2
                for t in range(4):
                    j = (rs + 2 * t) // 2
                    qc = (r % 8) * 64
                    self.mm(self.PS[:, ob, qc:qc + 64], self.VA[:, al, j, hh * 64:hh * 64 + 128],
                            pt[:, ri * 256 + t * 64:ri * 256 + (t + 1) * 64], t == 0, t == 3,
                            [ptb, B["VA"][al][j]], [B["PSB"][ob]])
            if rp == 3:
                orows = slice(0, 64) if hh == 0 else slice(64, 128)
                self.normalize(ob, hh, self.mixT[orows, chunk, qb * 512:(qb + 1) * 512], [B["MIX"][chunk][qb]])

        for idx in range(min(look, len(items))):
            front(idx)
        for idx in range(len(items)):
            if idx + look < len(items):
                front(idx + look)
            back(idx)

    def mixer_diff(self, l, pr, slot):
        P, B = self.P, self.B
        w = self.load_ws(slot, "w1", l, OFF_DF[pr], 384, 2464)
        self.wsbuf = B[f"WS{slot}"]
        chunk = 2 + pr
        self.fence_ptx()
        for m in range(2):
            P.add("pool", lambda e, m=m: e.memset(self.KT[:, m, :], 0.0), [], [B[f"KT{m}"]])
        P.add("pool", lambda e: e.memset(self.QT[64:128, 0, :], 0.0), [], [B["QT0"]])
        P.add("pool", lambda e: e.memset(self.QT[0:64, 1, :], 0.0), [], [B["QT1"]])

        def evq(tb, PS, bank):
            self.cp("dve", self.QT[0:64, 0, tb * 512:(tb + 1) * 512], PS[0:64, bank, :], [B["PSB"][bank]], [B["QT0"]])
            self.cp("act", self.QT[64:128, 1, tb * 512:(tb + 1) * 512], PS[64:128, bank, :], [B["PSB"][bank]], [B["QT1"]])

        def evk(tb, PS, bank):
            for i in range(4):
                rs_ = slice(i * 32, (i + 1) * 32)
                m = i % 2
                self.cp("dve" if i < 2 else "act", self.KT[rs_, m, tb * 512:(tb + 1) * 512], PS[rs_, bank, :],
                        [B["PSB"][bank]], [B[f"KT{m}"]])
        self.proj_fm(w, 0, 128, evq)
        self.proj_fm(w, 128, 128, evk)
        self.proj_tm(w, 256, 128, self.evac_v(0))
        scale = 32.0 ** -0.5
        for hh in range(2):
            h = 2 * pr + hh
            self.dma("sp", self.TAB[:, 0:GW], self.sc["alibi_g"][h], [B["SC"]["alibi_g"][0]], B["TABS"], "tab")
            self.attn_diff_head(l, hh, scale, chunk)

    def attn_diff_head(self, l, hh, scale, chunk):
        B = self.B
        orows = slice(0, 64) if hh == 0 else slice(64, 128)
        sbanks = (0, 1, 2, 5, 6, 7)
        look = 5
        pts = self.pt_cur
        cnt = getattr(self, "acnt", 0)
        for qb in range(4):
            obs = (3, 4)
            seq = [(m, kt) for m in range(2) for kt in range(NT)]

            def qk(idx):
                m, kt = seq[idx]
                sbk = sbanks[(cnt + idx) % len(sbanks)]
                self.mm(self.PS[:, sbk, :], self.KT[:, m, kt * 128:(kt + 1) * 128], self.QT[:, hh, qb * 512:(qb + 1) * 512],
                        True, True, [B[f"QT{hh}"], B[f"KT{m}"]], [B["PSB"][sbk]])

            def rest(idx):
                m, kt = seq[idx]
                i = cnt + idx
                sbk = sbanks[i % len(sbanks)]
                pt, ptb = pts[i % len(pts)]
                self.act(pt, self.PS[:, sbk, :], AF.Exp, [B["PSB"][sbk]], [ptb], scale=scale)
                c0 = qb * 512 - kt * 128 + 1920
                eng = "pool" if (i % 8 == 0) else "dve"
                self.tt(eng, pt, pt, self.TAB[:, c0:c0 + 512], ALU.mult, [ptb] + B["TABS"], [ptb])
                ob = obs[m]
                self.mm(self.PS[:, ob, :], self.VA[:, 0, kt, hh * 64:hh * 64 + 128], pt, kt == 0, kt == NT - 1,
                        [ptb, B["VA"][0][kt]], [B["PSB"][ob]])

            for idx in range(look):
                qk(idx)
            for idx in range(len(seq)):
                if idx + look < len(seq):
                    qk(idx + look)
                rest(idx)
            cnt += len(seq)
            for m in range(2):
                self.normalize(obs[m], hh, self.SCR[orows, m, :], [B["SCR"][m]])
            self.diff_finish(l, hh, qb, chunk)
        self.acnt = cnt

    def diff_finish(self, l, hh, qb, chunk):
        B = self.B
        orows = slice(0, 64) if hh == 0 else slice(64, 128)
        d1 = self.SCR[orows, 0, :]
        d2 = self.SCR[orows, 1, :]
        dd = self.SCR[orows, 2, :]
        rs = self.SCR[orows, 3, :]
        sq = self.SCRB[orows, 0, :]
        self.stt(dd, d2, self.NLAM[orows, l:l + 1], d1, ALU.mult, ALU.add, [B["SCR"][0], B["SCR"][1], B["NLAM"]], [B["SCR"][2]])
        self.tt("dve", sq, dd, dd, ALU.mult, [B["SCR"][2]], [B["SCRB0"]])
        bank = 7
        self.mm(self.PS[:, bank, :], self.onesb[orows, :], sq, True, True, [B["SCRB0"], B["CONST"]], [B["PSB"][bank]])
        self.act(rs, self.PS[orows, bank, :], AF.Ln, [B["PSB"][bank]], [B["SCR"][3]], scale=1.0 / 64.0, bias=EPS)
        self.act(rs, rs, AF.Exp, [B["SCR"][3]], [B["SCR"][3]], scale=-0.5)
        self.stt(self.mixT[orows, chunk, qb * 512:(qb + 1) * 512], dd, self.GSUB[orows, l:l + 1], rs, ALU.mult, ALU.mult,
                 [B["SCR"][2], B["SCR"][3], B["GSUB"]], [B["MIX"][chunk][qb]])

    def mixer_gqa(self, l, g, slot):
        P, B = self.P, self.B
        w = self.load_ws(slot, "w1", l, OFF_GQ[g], 256, 2464)
        wsb = B[f"WS{slot}"]
        chunk = 4 + g
        P.add("pool", lambda e: e.memset(self.KT[64:128, 0, :], 0.0), [], [B["KT0"]])
        P.add("pool", lambda e: e.memset(self.KT[0:64, 1, :], 0.0), [], [B["KT1"]])
        self.fence_ptx()
        self.fence_scr()
        st_ = {}

        def bufs(t):
            p = t % 2
            H = B["SCRH"]
            return dict(p=p, bank=p, raw=self.SCR[:, 0, p * 256:p * 256 + 192], rawb=H[0][p],
                        y=self.SCR[:, 1, p * 256:(p + 1) * 256], yb=H[1][p],
                        a=self.SCR[:, 2, p * 256:(p + 1) * 256], ab=H[2][p],
                        bs=self.SCR[:, 3, p * 256:(p + 1) * 256], bsb=H[3][p],
                        qk=self.SCRB[:, p, 0:256], qkb=B[f"SCRB{p}"])

        def stage_a(t):
            u = bufs(t)
            pb = B["PSB"][u["bank"]]
            ps = self.PS[:, u["bank"], 0:256]
            for c in range(8):
                yield self.mm(ps, self.hT[:, c, t * 128:(t + 1) * 128], w[:, c, 0:256], c == 0, c == 7, [B["HT"][t // 4], wsb], [pb])
            dst = self.VA[:, 0, t, :].rearrange("p (a b) -> p a b", b=64)[:, 0:3:2, :]
            yield self.cp("dve", dst, ps[:, 192:256].unsqueeze(1).broadcast_to([128, 2, 64]), [pb], [B["VA"][0][t]])
            yield self.cp("act", u["raw"], ps[:, 0:192], [pb], [u["rawb"]])
            sq = u["y"][:, 0:192]
            yield self.tt("dve", sq, u["raw"], u["raw"], ALU.mult, [u["rawb"]], [u["yb"]])
            ss, sk = self.stat(3, f"g{u['p']}")
            st_[t] = (ss, sk)
            yield P.add("dve", lambda e: e.tensor_reduce(out=ss, in_=sq.rearrange("p (h d) -> p h d", d=64), axis=AX.X, op=ALU.add),
                  [u["yb"]], [sk])
            yield self.act(ss, ss, AF.Sqrt, [sk], [sk], scale=1.0 / 64.0, bias=EPS)

        def stage_b(t):
            u = bufs(t)
            ss, sk = st_.pop(t)
            raw, y = u["raw"], u["y"]
            yield self.recip(ss, ss, [sk], [sk])
            for h3 in range(3):
                yield self.stt(y[:, h3 * 64:(h3 + 1) * 64], raw[:, h3 * 64:(h3 + 1) * 64], ss[:, h3:h3 + 1],
                         self.GGQA[:, h3 * 64:(h3 + 1) * 64], ALU.mult, ALU.mult, [u["rawb"], sk, B["GMLA"]], [u["yb"]])
            yield self.stt(y[:, 192:256], raw[:, 128:192], ss[:, 2:3], self.GGQA[:, 192:256], ALU.mult, ALU.mult,
                     [u["rawb"], sk, B["GMLA"]], [u["yb"]])
            C = self.ROPG[:, t, 0:64].unsqueeze(1).broadcast_to([128, 4, 64])
            Sg = self.ROPG[:, t, 64:128]
            y3 = y.rearrange("p (h d) -> p h d", d=64)
            a3 = u["a"].rearrange("p (h d) -> p h d", d=64)
            b3 = u["bs"].rearrange("p (h d) -> p h d", d=64)
            yield self.tt("dve", a3, y3, C, ALU.mult, [u["yb"], B["ROP"]], [u["ab"]])
            y5 = y.rearrange("p (h a r d) -> p h a r d", a=2, r=2, d=16)
            b5 = u["bs"].rearrange("p (h a r d) -> p h a r d", a=2, r=2, d=16)
            s5 = Sg.rearrange("p (a r d) -> p a r d", a=2, r=2, d=16)
            for r in range(2):
                yield self.tt("dve", b5[:, :, :, r, :], y5[:, :, :, 1 - r, :], s5[:, :, r, :].unsqueeze(1).broadcast_to([128, 4, 2, 16]),
                        ALU.mult, [u["yb"], B["ROP"]], [u["bsb"]])
            qk, qkb = u["qk"], u["qkb"]
            yield self.tt("dve", qk.rearrange("p (h d) -> p h d", d=64), a3, b3, ALU.add, [u["ab"], u["bsb"]], [qkb])
            tbank = 6 + ((t // 4) % 2)
            psb = self.PS[:, tbank, :].bitcast(BF16)
            tq = t % 4
            yield self.tr(psb[:, tq * 128:(tq + 1) * 128], qk[:, 0:128], [qkb, B["CONST"]], [B["PSB"][tbank]])
            yield self.tr(psb[:, 512 + tq * 128:512 + (tq + 1) * 128], qk[:, 128:256], [qkb, B["CONST"]], [B["PSB"][tbank]])
            if tq == 3:
                tb = t // 4
                yield self.cp("act", self.QT[:, 0, tb * 512:(tb + 1) * 512], psb[:, 0:512], [B["PSB"][tbank]], [B["QT0"]])
                yield self.cp("act", self.KT[0:64, 0, tb * 512:(tb + 1) * 512], psb[0:64, 512:1024], [B["PSB"][tbank]], [B["KT0"]])
                yield self.cp("dve", self.KT[64:128, 1, tb * 512:(tb + 1) * 512], psb[64:128, 512:1024], [B["PSB"][tbank]], [B["KT1"]])

        self.pipeline([stage_a, stage_b])
        self.fence_scr()
        P.tag = P.tag[:4] + "a"
        for hh in range(2):
            orows = slice(hh * 64, hh * 64 + 64)

            def outf(qb, bank, hh=hh, orows=orows):
                self.normalize(bank, hh, self.mixT[orows, chunk, qb * 512:(qb + 1) * 512], [B["MIX"][chunk][qb]])

            self.attn_dense(lambda qb: self.QT[:, 0, qb * 512:(qb + 1) * 512],
                            lambda kt, hh=hh: self.KT[:, hh, kt * 128:(kt + 1) * 128],
                            lambda kt, hh=hh: (self.VA[:, 0, kt, hh * 64:hh * 64 + 128], B["VA"][0][kt]),
                            128, 1.0 / 8.0, None, outf, [B["QT0"], B[f"KT{hh}"]])

    def mixer_mla(self, l, pr):
        P, B = self.P, self.B
        wA = self.load_ws(0, "w1", l, OFF_MLA, 384, 2464)
        ws1 = self.WS[:, 1, :]
        self.dma("sp", ws1[:, 0:256].rearrange("p (c n) -> p c n", c=8),
                 self.sc["w1"][l].rearrange("p (c n) -> p c n", c=8)[:, :, OFF_KR:OFF_KR + 32], [B["SC"]["w1"][l]], [B["WS1"]], "ws1")
        self.dma("sp", ws1[:, 256:1024], self.sc["uq"][l], [B["SC"]["uq"][l]], [B["WS1"]], "ws1")
        self.dma("sp", ws1[:, 1024:1536], self.sc["ukv"][l], [B["SC"]["ukv"][l]], [B["WS1"]], "ws1")
        wkr = ws1[:, 0:256].rearrange("p (c n) -> p c n", c=8)
        wuq = ws1[:, 256:1024].rearrange("p (j n) -> p j n", j=2)
        wukv = ws1[:, 1024:1536]
        chunk = 6 + pr
        scale = 96.0 ** -0.5
        self.fence_ptx()
        self.fence_scr()
        st_ = {}
        H = B["SCRH"]

        def bufs(t):
            p = t % 2
            return dict(p=p, lbank=p, tbank=6 + p, qbank=(2, 5)[p],
                        raw=self.SCR[:, p, 0:416], rawb=B["SCR"][p],
                        lat=self.SCRB[:, p, 0:384], latb=B[f"SCRB{p}"],
                        lt=self.HN[:, 0, p * 384:(p + 1) * 384].rearrange("p (j n) -> p j n", j=3), ltb=B["LTP"][p],
                        qk=self.PT[:, p, 0:384], qkb=B["PT"][p],
                        rin=self.SCR[:, 2, p * 256:p * 256 + 96], rinb=H[2][p],
                        ra=self.SCR[:, 3, p * 256:p * 256 + 96], rb=self.SCR[:, 3, p * 256 + 96:p * 256 + 192], rab=H[3][p])

        def stage_a(t):
            u = bufs(t)
            pb = B["PSB"][u["lbank"]]
            ps = self.PS[:, u["lbank"], :]
            tb = t // 4
            for c in range(8):
                yield self.mm(ps[:, 0:384], self.hT[:, c, t * 128:(t + 1) * 128], wA[:, c, :], c == 0, c == 7, [B["HT"][tb], B["WS0"]], [pb])
            for c in range(8):
                yield self.mm(ps[:, 384:416], self.hT[:, c, t * 128:(t + 1) * 128], wkr[:, c, :], c == 0, c == 7, [B["HT"][tb], B["WS1"]], [pb])
            raw = u["raw"]
            yield self.cp("act", raw, ps[:, 0:416], [pb], [u["rawb"]])
            ss, sk = self.stat(2, f"m{u['p']}")
            st_[t] = (ss, sk)
            junk = u["lat"]
            yield self.act(junk[:, 0:256], raw[:, 0:256], AF.Square, [u["rawb"]], [u["latb"], sk], accum_out=ss[:, 0:1])
            yield self.act(junk[:, 256:384], raw[:, 256:384], AF.Square, [u["rawb"]], [u["latb"], sk], accum_out=ss[:, 1:2])
            yield self.act(ss[:, 0:1], ss[:, 0:1], AF.Sqrt, [sk], [sk], scale=1.0 / 256.0, bias=EPS)
            yield self.act(ss[:, 1:2], ss[:, 1:2], AF.Sqrt, [sk], [sk], scale=1.0 / 128.0, bias=EPS)
            yield self.recip(ss, ss, [sk], [sk])
            lat = u["lat"]
            yield self.stt(lat[:, 0:256], raw[:, 0:256], ss[:, 0:1], self.GMLA[:, 0:256], ALU.mult, ALU.mult,
                     [u["rawb"], sk, B["GMLA"]], [u["latb"]])
            yield self.stt(lat[:, 256:384], raw[:, 256:384], ss[:, 1:2], self.GMLA[:, 256:384], ALU.mult, ALU.mult,
                     [u["rawb"], sk, B["GMLA"]], [u["latb"]])

        def stage_b(t):
            u = bufs(t)
            st_.pop(t)
            lat, raw = u["lat"], u["raw"]
            tbk = B["PSB"][u["tbank"]]
            psb = self.PS[:, u["tbank"], :].bitcast(BF16)
            for j in range(3):
                yield self.tr(psb[:, j * 128:(j + 1) * 128], lat[:, j * 128:(j + 1) * 128], [u["latb"], B["CONST"]], [tbk])
            yield self.cp("dve", u["lt"].rearrange("p j n -> p (j n)"), psb[:, 0:384], [tbk], [u["ltb"]])
            pq = self.PS[:, u["qbank"], :]
            pqb = B["PSB"][u["qbank"]]
            for j in range(2):
                yield self.mm(pq[:, 0:192], u["lt"][:, j, :], wuq[:, j, pr * 192:(pr + 1) * 192], j == 0, j == 1, [u["ltb"], B["WS1"]], [pqb])
            yield self.mm(pq[:, 192:448], u["lt"][:, 2, :], wukv[:, pr * 256:(pr + 1) * 256], True, True, [u["ltb"], B["WS1"]], [pqb])
            dst = self.VA[:, 0, t, :].rearrange("p (a b) -> p a b", b=64)[:, 0:3:2, :]
            srcv = pq[:, 192:448].rearrange("p (h a b) -> p h a b", h=2, a=2)[:, :, 1, :]
            yield self.cp("dve", dst, srcv, [pqb], [B["VA"][0][t]])
            qk4 = u["qk"].rearrange("p (x d) -> p x d", d=96)
            yield self.cp("act", qk4[:, 0:2, 0:64], pq[:, 0:192].rearrange("p (h d) -> p h d", d=96)[:, :, 0:64], [pqb], [u["qkb"]])
            yield self.cp("act", qk4[:, 2:4, 0:64], pq[:, 192:448].rearrange("p (h a b) -> p h a b", h=2, a=2)[:, :, 0, :], [pqb], [u["qkb"]])
            rin = u["rin"].rearrange("p (x d) -> p x d", d=32)
            yield self.cp("dve", rin[:, 0:2, :], pq[:, 0:192].rearrange("p (h d) -> p h d", d=96)[:, :, 64:96], [pqb], [u["rinb"]])
            yield self.cp("dve", rin[:, 2, :], raw[:, 384:416], [u["rawb"]], [u["rinb"]])

        def stage_c(t):
            u = bufs(t)
            tbk = B["PSB"][u["tbank"]]
            psb = self.PS[:, u["tbank"], :].bitcast(BF16)
            qk4 = u["qk"].rearrange("p (x d) -> p x d", d=96)
            rin = u["rin"].rearrange("p (x d) -> p x d", d=32)
            rin4 = u["rin"].rearrange("p (x r d) -> p x r d", x=3, r=2)
            C = self.ROPM[:, t, 0:32].unsqueeze(1).broadcast_to([128, 3, 32])
            Sm = self.ROPM[:, t, 32:64].rearrange("p (r d) -> p r d", r=2)
            ra = u["ra"].rearrange("p (x d) -> p x d", d=32)
            rbm = u["rb"].rearrange("p (x r d) -> p x r d", x=3, r=2)
            rb3 = u["rb"].rearrange("p (x d) -> p x d", d=32)
            yield self.tt("dve", ra, rin, C, ALU.mult, [u["rinb"], B["ROP"]], [u["rab"]])
            for r in range(2):
                yield self.tt("dve", rbm[:, :, r, :], rin4[:, :, 1 - r, :], Sm[:, r, :].unsqueeze(1).broadcast_to([128, 3, 16]),
                        ALU.mult, [u["rinb"], B["ROP"]], [u["rab"]])
            yield self.tt("dve", qk4[:, 0:2, 64:96], ra[:, 0:2, :], rb3[:, 0:2, :], ALU.add, [u["rab"]], [u["qkb"]])
            for hk in range(2):
                yield self.tt("dve", qk4[:, 2 + hk, 64:96], ra[:, 2, :], rb3[:, 2, :], ALU.add, [u["rab"]], [u["qkb"]])
            for x4 in range(4):
                yield self.tr(psb[0:96, 512 + x4 * 128:512 + (x4 + 1) * 128], qk4[:, x4, :], [u["qkb"], B["CONST"]], [tbk])
            for hh in range(2):
                yield self.cp("dve", self.QT[0:96, hh, t * 128:(t + 1) * 128], psb[0:96, 512 + hh * 128:512 + (hh + 1) * 128],
                        [tbk], [B[f"QT{hh}"]])
                yield self.cp("act", self.KT[0:96, hh, t * 128:(t + 1) * 128], psb[0:96, 512 + (2 + hh) * 128:512 + (3 + hh) * 128],
                        [tbk], [B[f"KT{hh}"]])

        self.pipeline([stage_a, stage_b, stage_c])
        self.fence_scr()
        P.tag = P.tag[:4] + "a"
        for hh in range(2):
            orows = slice(hh * 64, hh * 64 + 64)

            def outf(qb, bank, hh=hh, orows=orows):
                self.normalize(bank, hh, self.mixT[orows, chunk, qb * 512:(qb + 1) * 512], [B["MIX"][chunk][qb]])

            self.attn_dense(lambda qb, hh=hh: self.QT[0:96, hh, qb * 512:(qb + 1) * 512],
                            lambda kt, hh=hh: self.KT[0:96, hh, kt * 128:(kt + 1) * 128],
                            lambda kt, hh=hh: (self.VA[:, 0, kt, hh * 64:hh * 64 + 128], B["VA"][0][kt]),
                            96, scale, None, outf, [B[f"QT{hh}"], B[f"KT{hh}"]])

    def post_residual(self, t, b0):
        for _ in self.post_gen(t, b0, 0):
            pass

    def post_gen(self, t, b0, par):
        B = self.B
        fps = self.PS[:, b0:b0 + 2, :].rearrange("p a n -> p (a n)")
        pbs = [B["PSB"][b0], B["PSB"][b0 + 1]]
        if par == 0:
            junk, jb = self.SCRB[:].rearrange("p a n -> p (a n)"), [B["SCRB0"], B["SCRB1"]]
            tmp, tb_ = self.SCR[:, 0:2, :].rearrange("p a n -> p (a n)"), [B["SCR"][0], B["SCR"][1]]
            key = "p0"
        else:
            junk, jb = self.PT[:, 0:2, :].rearrange("p a n -> p (a n)"), [B["PT"][0], B["PT"][1]]
            tmp, tb_ = self.SCR[:, 2:4, :].rearrange("p a n -> p (a n)"), [B["SCR"][2], B["SCR"][3]]
            key = "p1"
        ss, sk = self.stat(1, key)
        yield self.act(junk, fps, AF.Square, pbs, jb + [sk], accum_out=ss)
        yield self.tt("dve", tmp, fps, self.GB[:], ALU.mult, pbs + [B["GB"]], tb_)
        yield self.act(ss, ss, AF.Sqrt, [sk], [sk], scale=1.0 / D, bias=EPS)
        yield self.recip(ss, ss, [sk], [sk])
        yield self.stt(self.x_sb[:, t, :], tmp, ss, self.x_sb[:, t, :], ALU.mult, ALU.add, tb_ + [sk, B["X"][t]], [B["X"][t]])

    def wo_residual(self, l):
        P, B = self.P, self.B
        wsrc = self.sc["wo"][l].rearrange("p (c n) -> p c n", c=8)
        w0 = self.WS[:, 0, :].rearrange("p (c n) -> p c n", c=8)
        w1 = self.WS[:, 1, :].rearrange("p (c n) -> p c n", c=8)
        w2 = self.TAB[:, 0:2048].rearrange("p (c n) -> p c n", c=8)
        self.dma("sp", w0, wsrc[:, :, 0:384], [B["SC"]["wo"][l]], [B["WS0"]], "ws0")
        self.dma("sp", w1, wsrc[:, :, 384:768], [B["SC"]["wo"][l]], [B["WS1"]], "ws1")
        self.dma("sp", w2, wsrc[:, :, 768:1024], [B["SC"]["wo"][l]], B["TABS"][0:2], "tab")
        segs = ((0, 384, w0, 0, B["WS0"]), (384, 128, w1, 0, B["WS1"]), (512, 256, w1, 128, B["WS1"]), (768, 256, w2, 0, B["TABS"][0]))
        self.fence_scr()
        prev = None
        for pi in range(NT // 2):
            tiles = (2 * pi, 2 * pi + 1)
            for t in tiles:
                b0 = 0 if t % 2 == 0 else 2
                tb = t // 4
                for (o0, n, wt, wo_, wb) in segs:
                    bank = b0 + o0 // 512
                    col = o0 % 512
                    for c in range(8):
                        self.mm(self.PS[:, bank, col:col + n], self.mixT[:, c, t * 128:(t + 1) * 128], wt[:, c, wo_:wo_ + n],
                                c == 0, c == 7, [B["MIX"][c][tb], wb, B["TABS"][1]], [B["PSB"][bank]])
            gens = [self.post_gen(t, 0 if t % 2 == 0 else 2, t % 2) for t in tiles]
            if self.do_ffn and prev is not None:
                gens += [self.norm_gen(t, NL + l, 6 + (t % 2), t % 2) for t in prev]
            self.run_gens(gens)
            prev = tiles
        if self.do_ffn:
            self.run_gens([self.norm_gen(t, NL + l, 6 + (t % 2), t % 2) for t in prev])
        self.fence_scr()

    def ffn(self, s, l, last):
        P, B = self.P, self.B
        self.dma("sp", self.GB[:], self.I["g_post"][l, 1:2, :].broadcast_to([128, 1024]), [], [B["GB"]], "gb")
        actT = self.mixT[:].rearrange("p c n -> p (c n)")[:, 0:NJ * 512].rearrange("p (j n) -> p j n", j=NJ)
        abufs = [b for c in range(8) for b in B["MIX"][c]]
        gcnt = 0
        for tb in range(4):
            for j in range(NJ):
                slot = gcnt % 2
                gcnt += 1
                wv = self.WS[:, slot, 0:2048].rearrange("p (c n) -> p c n", c=8)
                self.dma("sp", self.WS[:, slot, 0:2048], self.sc["gu"][l, j], [B["SC"]["gu"][l]], [B[f"WS{slot}"]], f"ws{slot}")
                gb = 4 + 2 * (j % 2)
                ub = gb + 1
                for c in range(8):
                    self.mm(self.PS[:, gb, :], wv[:, c, 0:128], self.hT[:, c, tb * 512:(tb + 1) * 512], c == 0, c == 7,
                            [B["HT"][tb], B[f"WS{slot}"]], [B["PSB"][gb]])
                for c in range(8):
                    self.mm(self.PS[:, ub, :], wv[:, c, 128:256], self.hT[:, c, tb * 512:(tb + 1) * 512], c == 0, c == 7,
                            [B["HT"][tb], B[f"WS{slot}"]], [B["PSB"][ub]])
                si = 2 + (j % 2)
                sg = self.SCR[:, si, :]
                self.act(sg, self.PS[:, gb, :], AF.Silu, [B["PSB"][gb]], [B["SCR"][si]])
                self.tt("dve", actT[:, j, :], self.PS[:, ub, :], sg, ALU.mult, [B["PSB"][ub], B["SCR"][si]], [self.abuf(j)])
            for j in range(NJ):
                ts_ = j % 4
                wdt = self.TAB[:, ts_ * 1024:(ts_ + 1) * 1024]
                self.dma("sp", wdt, self.sc["wd"][l, j], [B["SC"]["wd"][l]], [B["TABS"][ts_]], f"tabs{ts_}")
                for tt_ in range(4):
                    for ch in range(2):
                        bank = tt_ * 2 + ch
                        self.mm(self.PS[:, bank, :], actT[:, j, tt_ * 128:(tt_ + 1) * 128], wdt[:, ch * 512:(ch + 1) * 512],
                                j == 0, j == NJ - 1, [self.abuf(j), B["TABS"][ts_]], [B["PSB"][bank]])
            if tb == 0:
                self.fence_scr()
            prev = None
            for half in range(2):
                tl = (tb * 4 + 2 * half, tb * 4 + 2 * half + 1)
                gens = [self.post_gen(t, (t % 4) * 2, t % 2) for t in tl]
                if prev is not None and not last and self.next_l is not None:
                    gens += [self.norm_gen(t, self.next_l, t % 2, t % 2) for t in prev]
                self.run_gens(gens)
                if last:
                    for t in tl:
                        self.store_x(s, t)
                        if s + 1 < self.nseq:
                            self.dma("pool", self.x_sb[:, t, :], self.I["x"][s + 1, t * 128:(t + 1) * 128, :], [], [B["X"][t]], f"x{t}")
                    if s + 1 < self.nseq:
                        self.run_gens([self.norm_gen(t, self.layers[0], t % 2, t % 2) for t in tl])
                prev = tl
            if not last and self.next_l is not None:
                self.run_gens([self.norm_gen(t, self.next_l, t % 2, t % 2) for t in prev])
            if tb == 3:
                self.fence_scr()
                if (not last and self.next_l is not None) or (last and s + 1 < self.nseq):
                    self.norm_done = True
                if last and s + 1 < self.nseq:
                    self.x_loaded = True

    def abuf(self, j):
        return self.B["MIX"][j // 4][j % 4]


_CACHE = {}


def kernel(**inputs):
    ncores = 8
    x = np.ascontiguousarray(np.asarray(inputs["x"], np.float32))
    nb = x.shape[0]
    per = nb // ncores
    shared = prep_shared(inputs)
    key = ("full", per)
    if key not in _CACHE:
        _CACHE[key] = Builder(per, range(NL)).build()
    nc = _CACHE[key]
    in_maps = []
    for c in range(ncores):
        m = dict(shared)
        m["x"] = x[c * per:(c + 1) * per]
        m["w1"] = shared["w1"].reshape(NL, 128, 8 * 2464)
        m["wo"] = shared["wo"].reshape(NL, 128, 8 * 1024)
        m["uq"] = shared["uq"].reshape(NL, 128, 768)
        m["g_pre"] = shared["g_pre"].reshape(128, 2 * NL * 8)
        m["rope_g"] = shared["rope_g"].reshape(128, 16 * 128)
        m["rope_m"] = shared["rope_m"].reshape(128, 16 * 64)
        in_maps.append(m)
    res = run_bass_kernel_spmd(nc, in_maps, core_ids=list(range(ncores)))
    out = np.concatenate([np.asarray(r["y"]) for r in res.results], axis=0)
    return out.astype(np.float32)
```
